# Optimizing a Trainium2 kernel written in Bass

```python
import math
import jax
import jax.numpy as jnp
from jax import lax
import numpy as np

D_MODEL = 1024
BATCH = 16
SEQ = 4096
DEPTH = 2
DEC_BATCH = 16
DEC_SEQ = 32
PAST_LEN = 1024

CHUNK = 64
NORM_EPS = 1e-6
RWKV_GN_EPS = 64e-5
RES_HALF = 0.5
N_MOD = 9
S5_WIDTH = D_MODEL // 2
S5_GROUP = 16
S5_GROUPS = S5_WIDTH // S5_GROUP
S5_STATE = 64
RWKV_WIDTH = D_MODEL - S5_WIDTH
RWKV_HEAD = 64
RWKV_HEADS = RWKV_WIDTH // RWKV_HEAD
RWKV_W_LORA = 64
RWKV_A_LORA = 64
RWKV_G_LORA = 128
RWKV_COLS = 3 * RWKV_WIDTH + RWKV_W_LORA + RWKV_A_LORA + RWKV_G_LORA
AB_IN_COLS = S5_WIDTH + RWKV_COLS
HGRN_WIDTH = D_MODEL
HGRN_HEAD = 128
HGRN_HEADS = HGRN_WIDTH // HGRN_HEAD
C_IN_COLS = 4 * HGRN_WIDTH
D_FF = 2816
N_EVEN = (DEPTH + 1) // 2
N_ODD = DEPTH // 2

kernel_name = 'hybrid_streaming_s5_rwkv7_hgrn2_step'


def rmsnorm(x, g):
    xf = x.astype(jnp.float32)
    y = xf * lax.rsqrt(jnp.mean(xf * xf, axis=-1, keepdims=True) + NORM_EPS)
    return (y * g.astype(jnp.float32)).astype(x.dtype)


def modulate(h, shift, scale):
    return h * (1 + scale[:, None, :]) + shift[:, None, :]


def swiglu(h, w1, w3, w2):
    return (jax.nn.silu(h @ w1) * (h @ w3)) @ w2


def _complex_affine_combine(e1, e2):
    a1r, a1i, b1r, b1i = e1
    a2r, a2i, b2r, b2i = e2
    return (a1r * a2r - a1i * a2i, a1r * a2i + a1i * a2r,
            a2r * b1r - a2i * b1i + b2r, a2r * b1i + a2i * b1r + b2i)


def s5_mixer(u, h0_re, h0_im, lam_re, lam_im, log_dt, b_re, b_im, c_re, c_im, d_skip, w_glu):
    f32 = jnp.float32
    bsz, L, _ = u.shape
    uf = u.astype(f32).reshape(bsz, L, S5_GROUPS, S5_GROUP)
    dt = jnp.exp(log_dt.astype(f32))[:, None]
    lr = lam_re.astype(f32)
    li = lam_im.astype(f32)
    mag = jnp.exp(lr * dt)
    ab_re = mag * jnp.cos(li * dt)
    ab_im = mag * jnp.sin(li * dt)
    den = lr * lr + li * li
    z_re = ((ab_re - 1.0) * lr + ab_im * li) / den
    z_im = (ab_im * lr - (ab_re - 1.0) * li) / den
    br = b_re.astype(f32)
    bi = b_im.astype(f32)
    bb_re = z_re[..., None] * br - z_im[..., None] * bi
    bb_im = z_re[..., None] * bi + z_im[..., None] * br
    bu_re = jnp.einsum('blgh,gph->lbgp', uf, bb_re)
    bu_im = jnp.einsum('blgh,gph->lbgp', uf, bb_im)
    h0r = h0_re.astype(f32)
    h0i = h0_im.astype(f32)
    bu_re = bu_re.at[0].add(ab_re * h0r - ab_im * h0i)
    bu_im = bu_im.at[0].add(ab_re * h0i + ab_im * h0r)
    a_re = jnp.broadcast_to(ab_re, (L, 1) + ab_re.shape)
    a_im = jnp.broadcast_to(ab_im, (L, 1) + ab_im.shape)
    _, _, xr, xi = lax.associative_scan(_complex_affine_combine, (a_re, a_im, bu_re, bu_im), axis=0)
    y = (jnp.einsum('lbgp,ghp->blgh', xr, c_re.astype(f32))
         - jnp.einsum('lbgp,ghp->blgh', xi, c_im.astype(f32))
         + uf * d_skip.astype(f32))
    y = jax.nn.gelu(y.reshape(bsz, L, S5_WIDTH))
    y = y * jax.nn.sigmoid(y @ w_glu.astype(f32))
    return y.astype(u.dtype), xr[-1], xi[-1]


def _rwkv_step(S, inp):
    r_t, k_t, v_t, w_t, a_t, b_t = inp
    sa = jnp.einsum('bhvk,bhk->bhv', S, a_t)
    S = S * w_t[:, :, None, :] + sa[..., None] * b_t[:, :, None, :] + v_t[..., None] * k_t[:, :, None, :]
    return S, jnp.einsum('bhvk,bhk->bhv', S, r_t)


def rwkv7_mixer(p, shift_prev, s0, mu, w0, w_w2, a0, w_a2, w_g2, k_k, k_a, r_k, lnx_g, lnx_b):
    f32 = jnp.float32
    bsz, L, _ = p.shape
    W = RWKV_WIDTH
    p_prev = jnp.concatenate([shift_prev[:, None, :].astype(p.dtype), p[:, :-1]], axis=1)
    ps = p + (p_prev - p) * mu
    r, k, v, w_low, a_low, g_low = jnp.split(
        ps, [W, 2 * W, 3 * W, 3 * W + RWKV_W_LORA, 3 * W + RWKV_W_LORA + RWKV_A_LORA], axis=-1)
    logw = -jax.nn.softplus(-(w0 + jnp.tanh(w_low) @ w_w2).astype(f32)) - 0.5
    decay = jnp.exp(-jnp.exp(logw))
    a = jax.nn.sigmoid((a0 + a_low @ w_a2).astype(f32))
    g = jax.nn.sigmoid(g_low) @ w_g2
    hd = (bsz, L, RWKV_HEADS, RWKV_HEAD)
    kk = (k * k_k).astype(f32).reshape(hd)
    kk = kk / jnp.maximum(jnp.linalg.norm(kk, axis=-1, keepdims=True), 1e-12)
    k = (k.astype(f32) * (1 + (a - 1) * k_a.astype(f32))).reshape(hd)
    r = r.astype(f32).reshape(hd)
    v = v.astype(f32).reshape(hd)
    a_h = a.reshape(hd)
    decay = decay.reshape(hd)
    tm = lambda t: jnp.moveaxis(t, 1, 0)
    S, ys = lax.scan(_rwkv_step, s0.astype(f32),
                     (tm(r), tm(k), tm(v), tm(decay), tm(-kk), tm(kk * a_h)))
    y = jnp.moveaxis(ys, 0, 1)
    mean = jnp.mean(y, axis=-1, keepdims=True)
    var = jnp.mean(jnp.square(y - mean), axis=-1, keepdims=True)
    y = ((y - mean) * lax.rsqrt(var + RWKV_GN_EPS)).reshape(bsz, L, W) * lnx_g + lnx_b
    bonus = jnp.sum(r * k * r_k.astype(f32), axis=-1, keepdims=True) * v
    y = (y + bonus.reshape(bsz, L, W)) * g
    return y.astype(p.dtype), p[:, -1], S


def gated_linear_chunked(q, k, v, log_f, s0):
    bsz, L, H, K = q.shape
    n = -(-L // CHUNK)
    pad = n * CHUNK - L
    padw = ((0, 0), (0, pad), (0, 0), (0, 0))
    q, k, v, log_f = [jnp.pad(t, padw) for t in (q, k, v, log_f)]
    blocks = lambda t: t.reshape(bsz, n, CHUNK, H, t.shape[-1]).transpose(1, 0, 3, 2, 4)
    tri = jnp.tril(jnp.ones((CHUNK, CHUNK), dtype=bool))[:, :, None]

    def step(S, inp):
        qc, kc, vc, gc = inp
        b = jnp.cumsum(gc, axis=2)
        diff = jnp.where(tri, b[:, :, :, None, :] - b[:, :, None, :, :], -jnp.inf)
        att = jnp.einsum('bhtk,bhtsk,bhsk->bhts', qc, jnp.exp(diff), kc)
        o = jnp.einsum('bhts,bhsv->bhtv', att, vc) + jnp.einsum('bhtk,bhkv->bhtv', qc * jnp.exp(b), S)
        b_last = b[:, :, -1:, :]
        S = S * jnp.exp(b_last[:, :, 0, :, None]) + jnp.einsum('bhsk,bhsv->bhkv', kc * jnp.exp(b_last - b), vc)
        return S, o

    S, o = lax.scan(step, s0, (blocks(q), blocks(k), blocks(v), blocks(log_f)))
    o = o.transpose(1, 0, 3, 2, 4).reshape(bsz, n * CHUNK, H, v.shape[-1])[:, :L]
    return o, S


def hgrn2_mixer(p, s0, lb, gnorm):
    f32 = jnp.float32
    bsz, L, _ = p.shape
    q, f, v, g = jnp.split(p, 4, axis=-1)
    q = jax.nn.silu(q).astype(f32)
    f = f.astype(f32)
    log_f = jnp.logaddexp(jnp.log(lb), jnp.log1p(-lb) + jax.nn.log_sigmoid(f))
    k = (1 - lb) * jax.nn.sigmoid(-f)
    hd = (bsz, L, HGRN_HEADS, HGRN_HEAD)
    o, S = gated_linear_chunked(q.reshape(hd), k.reshape(hd), v.astype(f32).reshape(hd),
                                log_f.reshape(hd), s0.astype(f32))
    o = rmsnorm(o, gnorm).reshape(bsz, L, HGRN_WIDTH) * jax.nn.silu(g.astype(f32))
    return o.astype(p.dtype), S


def trunk(x, c, s5_re, s5_im, rwkv_s, rwkv_shift, hgrn_s, P):
    lb_all = jax.nn.softmax(P['hgrn_lower_bounds'].astype(jnp.float32), axis=0)
    lb_all = jnp.cumsum(lb_all, axis=0) - lb_all[0]
    cs = jax.nn.silu(c)
    n_re, n_im, n_rw, n_sh, n_hg = [], [], [], [], []
    for l in range(DEPTH):
        mods = cs @ P['w_ada'][l] + P['b_ada'][l]
        sh1, sc1, g1, sh2, sc2, g2, sh3, sc3, g3 = jnp.split(mods, N_MOD, axis=-1)
        h = modulate(rmsnorm(x, P['norm_ffn1'][l]), sh1, sc1)
        x = x + RES_HALF * (1 + g1[:, None]) * swiglu(h, P['ffn1_w1'][l], P['ffn1_w3'][l], P['ffn1_w2'][l])
        h = modulate(rmsnorm(x, P['norm_mix'][l]), sh2, sc2)
        i = l // 2
        if l % 2 == 0:
            p = h @ P['ab_w_in'][i]
            ya, hr, hi = s5_mixer(p[..., :S5_WIDTH], s5_re[i], s5_im[i], P['s5_lam_re'][i], P['s5_lam_im'][i],
                                  P['s5_log_dt'][i], P['s5_b_re'][i], P['s5_b_im'][i], P['s5_c_re'][i],
                                  P['s5_c_im'][i], P['s5_d'][i], P['s5_w_glu'][i])
            yb, sh_new, rw_new = rwkv7_mixer(p[..., S5_WIDTH:], rwkv_shift[i], rwkv_s[i], P['rwkv_mu'][i],
                                             P['rwkv_w0'][i], P['rwkv_w_w2'][i], P['rwkv_a0'][i],
                                             P['rwkv_w_a2'][i], P['rwkv_w_g2'][i], P['rwkv_k_k'][i],
                                             P['rwkv_k_a'][i], P['rwkv_r_k'][i], P['rwkv_lnx_g'][i],
                                             P['rwkv_lnx_b'][i])
            y = jnp.concatenate([ya, yb], axis=-1) @ P['ab_w_out'][i]
            n_re.append(hr)
            n_im.append(hi)
            n_rw.append(rw_new)
            n_sh.append(sh_new)
        else:
            p = h @ P['c_w_in'][i]
            yc, hg_new = hgrn2_mixer(p, hgrn_s[i], lb_all[l], P['hgrn_gnorm'][i])
            y = yc @ P['c_w_out'][i]
            n_hg.append(hg_new)
        x = x + (1 + g2[:, None]) * y
        h = modulate(rmsnorm(x, P['norm_ffn2'][l]), sh3, sc3)
        x = x + RES_HALF * (1 + g3[:, None]) * swiglu(h, P['ffn2_w1'][l], P['ffn2_w3'][l], P['ffn2_w2'][l])
    y = rmsnorm(x, P['final_norm'])
    return (y, jnp.stack(n_re).astype(s5_re.dtype), jnp.stack(n_im).astype(s5_im.dtype),
            jnp.stack(n_rw).astype(rwkv_s.dtype), jnp.stack(n_sh).astype(rwkv_shift.dtype),
            jnp.stack(n_hg).astype(hgrn_s.dtype))


def setup_inputs(seed: int = 0) -> dict:
    key = jax.random.key(seed)
    ks = iter(jax.random.split(key, 64))
    f32 = jnp.float32

    def nrm(shape, scale):
        return jax.random.normal(next(ks), shape, f32) * scale

    def unif(shape, lo, hi):
        return jax.random.uniform(next(ks), shape, f32, lo, hi)

    D = D_MODEL
    G, Pst, H = S5_GROUPS, S5_STATE, S5_GROUP
    W = RWKV_WIDTH
    w0_base = -6.0 + 5.0 * (jnp.arange(W, dtype=f32) / (W - 1)) ** 0.85
    lam_im_base = jnp.pi * jnp.arange(Pst, dtype=f32)
    return {
        'x_prompt': nrm((BATCH, SEQ, D), 1.0),
        'x_sample': nrm((DEC_BATCH, DEC_SEQ, D), 1.0),
        'state_s5_re': nrm((N_EVEN, DEC_BATCH, G, Pst), 0.1),
        'state_s5_im': nrm((N_EVEN, DEC_BATCH, G, Pst), 0.1),
        'state_rwkv': nrm((N_EVEN, DEC_BATCH, RWKV_HEADS, RWKV_HEAD, RWKV_HEAD), 0.1),
        'state_rwkv_shift': nrm((N_EVEN, DEC_BATCH, RWKV_COLS), 1.0),
        'state_hgrn': nrm((N_ODD, DEC_BATCH, HGRN_HEADS, HGRN_HEAD, HGRN_HEAD), 0.5),
        'c_prompt': nrm((BATCH, D), 1.0),
        'c_sample': nrm((DEC_BATCH, D), 1.0),
        'w_ada': nrm((DEPTH, D, N_MOD * D), 0.5 * D ** -0.5),
        'b_ada': nrm((DEPTH, N_MOD * D), 0.01),
        'norm_ffn1': 1.0 + nrm((DEPTH, D), 0.02),
        'norm_mix': 1.0 + nrm((DEPTH, D), 0.02),
        'norm_ffn2': 1.0 + nrm((DEPTH, D), 0.02),
        'ffn1_w1': nrm((DEPTH, D, D_FF), D ** -0.5),
        'ffn1_w3': nrm((DEPTH, D, D_FF), D ** -0.5),
        'ffn1_w2': nrm((DEPTH, D_FF, D), D_FF ** -0.5),
        'ffn2_w1': nrm((DEPTH, D, D_FF), D ** -0.5),
        'ffn2_w3': nrm((DEPTH, D, D_FF), D ** -0.5),
        'ffn2_w2': nrm((DEPTH, D_FF, D), D_FF ** -0.5),
        'ab_w_in': nrm((N_EVEN, D, AB_IN_COLS), D ** -0.5),
        'ab_w_out': nrm((N_EVEN, S5_WIDTH + RWKV_WIDTH, D), (S5_WIDTH + RWKV_WIDTH) ** -0.5),
        's5_lam_re': -0.5 + nrm((N_EVEN, G, Pst), 0.01),
        's5_lam_im': lam_im_base + nrm((N_EVEN, G, Pst), 0.01),
        's5_log_dt': unif((N_EVEN, G), math.log(0.001), math.log(0.1)),
        's5_b_re': nrm((N_EVEN, G, Pst, H), H ** -0.5),
        's5_b_im': nrm((N_EVEN, G, Pst, H), H ** -0.5),
        's5_c_re': nrm((N_EVEN, G, H, Pst), Pst ** -0.5),
        's5_c_im': nrm((N_EVEN, G, H, Pst), Pst ** -0.5),
        's5_d': nrm((N_EVEN, G, H), 1.0),
        's5_w_glu': nrm((N_EVEN, S5_WIDTH, S5_WIDTH), S5_WIDTH ** -0.5),
        'rwkv_mu': unif((N_EVEN, RWKV_COLS), 0.0, 1.0),
        'rwkv_w0': w0_base + nrm((N_EVEN, W), 0.05),
        'rwkv_w_w2': nrm((N_EVEN, RWKV_W_LORA, W), 0.1),
        'rwkv_a0': nrm((N_EVEN, W), 0.1),
        'rwkv_w_a2': nrm((N_EVEN, RWKV_A_LORA, W), 0.1),
        'rwkv_w_g2': nrm((N_EVEN, RWKV_G_LORA, W), RWKV_G_LORA ** -0.5),
        'rwkv_k_k': 0.85 + nrm((N_EVEN, W), 0.02),
        'rwkv_k_a': 1.0 + nrm((N_EVEN, W), 0.02),
        'rwkv_r_k': nrm((N_EVEN, RWKV_HEADS, RWKV_HEAD), 0.1),
        'rwkv_lnx_g': 1.0 + nrm((N_EVEN, W), 0.02),
        'rwkv_lnx_b': nrm((N_EVEN, W), 0.01),
        'c_w_in': nrm((N_ODD, D, C_IN_COLS), D ** -0.5),
        'c_w_out': nrm((N_ODD, HGRN_WIDTH, D), HGRN_WIDTH ** -0.5),
        'hgrn_lower_bounds': nrm((DEPTH, HGRN_WIDTH), 0.1),
        'hgrn_gnorm': 1.0 + nrm((N_ODD, HGRN_HEAD), 0.02),
        'final_norm': 1.0 + nrm((D,), 0.02),
    }


def reference(x_prompt, x_sample, state_s5_re, state_s5_im, state_rwkv, state_rwkv_shift, state_hgrn,
              c_prompt, c_sample, w_ada, b_ada, norm_ffn1, norm_mix, norm_ffn2,
              ffn1_w1, ffn1_w3, ffn1_w2, ffn2_w1, ffn2_w3, ffn2_w2, ab_w_in, ab_w_out,
              s5_lam_re, s5_lam_im, s5_log_dt, s5_b_re, s5_b_im, s5_c_re, s5_c_im, s5_d, s5_w_glu,
              rwkv_mu, rwkv_w0, rwkv_w_w2, rwkv_a0, rwkv_w_a2, rwkv_w_g2, rwkv_k_k, rwkv_k_a, rwkv_r_k,
              rwkv_lnx_g, rwkv_lnx_b, c_w_in, c_w_out, hgrn_lower_bounds, hgrn_gnorm, final_norm):
    P = dict(w_ada=w_ada, b_ada=b_ada, norm_ffn1=norm_ffn1, norm_mix=norm_mix, norm_ffn2=norm_ffn2,
             ffn1_w1=ffn1_w1, ffn1_w3=ffn1_w3, ffn1_w2=ffn1_w2, ffn2_w1=ffn2_w1, ffn2_w3=ffn2_w3,
             ffn2_w2=ffn2_w2, ab_w_in=ab_w_in, ab_w_out=ab_w_out, s5_lam_re=s5_lam_re, s5_lam_im=s5_lam_im,
             s5_log_dt=s5_log_dt, s5_b_re=s5_b_re, s5_b_im=s5_b_im, s5_c_re=s5_c_re, s5_c_im=s5_c_im,
             s5_d=s5_d, s5_w_glu=s5_w_glu, rwkv_mu=rwkv_mu, rwkv_w0=rwkv_w0, rwkv_w_w2=rwkv_w_w2,
             rwkv_a0=rwkv_a0, rwkv_w_a2=rwkv_w_a2, rwkv_w_g2=rwkv_w_g2, rwkv_k_k=rwkv_k_k,
             rwkv_k_a=rwkv_k_a, rwkv_r_k=rwkv_r_k, rwkv_lnx_g=rwkv_lnx_g, rwkv_lnx_b=rwkv_lnx_b,
             c_w_in=c_w_in, c_w_out=c_w_out, hgrn_lower_bounds=hgrn_lower_bounds, hgrn_gnorm=hgrn_gnorm,
             final_norm=final_norm)
    dt = x_prompt.dtype
    bp = x_prompt.shape[0]
    z_re = jnp.zeros((N_EVEN, bp, S5_GROUPS, S5_STATE), dt)
    z_im = jnp.zeros((N_EVEN, bp, S5_GROUPS, S5_STATE), dt)
    z_rw = jnp.zeros((N_EVEN, bp, RWKV_HEADS, RWKV_HEAD, RWKV_HEAD), dt)
    z_sh = jnp.zeros((N_EVEN, bp, RWKV_COLS), dt)
    z_hg = jnp.zeros((N_ODD, bp, HGRN_HEADS, HGRN_HEAD, HGRN_HEAD), dt)
    y_prompt, s5_re_p, s5_im_p, rwkv_p, rwkv_shift_p, hgrn_p = trunk(
        x_prompt, c_prompt, z_re, z_im, z_rw, z_sh, z_hg, P)
    y_sample, s5_re_s, s5_im_s, rwkv_s, rwkv_shift_s, hgrn_s = trunk(
        x_sample, c_sample, state_s5_re, state_s5_im, state_rwkv, state_rwkv_shift, state_hgrn, P)
    return (y_prompt, y_sample, s5_re_p, s5_im_p, rwkv_p, rwkv_shift_p, hgrn_p,
            s5_re_s, s5_im_s, rwkv_s, rwkv_shift_s, hgrn_s)
```

```python
from contextlib import ExitStack
import numpy as np
import concourse.bass as bass
import concourse.mybir as mybir
from concourse.bass_utils import run_bass_kernel_spmd

F32 = mybir.dt.float32
BF16 = mybir.dt.bfloat16
AF = mybir.ActivationFunctionType
ALU = mybir.AluOpType

D = 1024
DFF = 2816
NFT = 22
KT = 8
NCORES = 8
NORM_EPS = 1e-6
GN_EPS = 64e-5


class Tok:
    __slots__ = ("w", "rs", "name")

    def __init__(self, name=""):
        self.w = None
        self.rs = {}
        self.name = name


class Ins:
    __slots__ = ("eng", "fn", "deps", "needs_inc", "evnum", "slot", "dmaval")

    def __init__(self, eng, fn, slot):
        self.eng = eng
        self.fn = fn
        self.slot = slot
        self.deps = ()
        self.needs_inc = False
        self.evnum = 0
        self.dmaval = 0


class Slot:
    __slots__ = ("count", "sem", "name")

    def __init__(self, name):
        self.count = 0
        self.sem = None
        self.name = name


class Prog:
    ENGS = ("pe", "act", "dve", "pool", "sp")

    def __init__(self):
        self.lists = {e: [] for e in self.ENGS}
        self.slots = []
        self.nins = 0

    def slot(self, name):
        s = Slot(name)
        self.slots.append(s)
        return s

    def op(self, eng, fn, reads=(), writes=(), slot=None):
        ins = Ins(eng, fn, slot)
        deps = {}
        for t in reads:
            if t.w is not None:
                deps[id(t.w)] = t.w
        for t in writes:
            if t.w is not None:
                deps[id(t.w)] = t.w
            for r in t.rs.values():
                deps[id(r)] = r
        dl = []
        for d in deps.values():
            if d is ins:
                continue
            if d.slot is None and slot is None and d.eng == "pe" and eng == "pe":
                continue
            d.needs_inc = True
            dl.append(d)
        ins.deps = dl
        if slot is not None:
            slot.count += 16
            ins.dmaval = slot.count
        for t in reads:
            key = eng if slot is None else ("d", id(ins))
            t.rs[key] = ins
        for t in writes:
            t.w = ins
            t.rs = {}
        self.lists[eng].append(ins)
        self.nins += 1
        return ins

    def emit(self, block, sems, final_wait_slots=()):
        for e in self.ENGS:
            n = 0
            for ins in self.lists[e]:
                if ins.slot is None and ins.needs_inc:
                    n += 1
                    ins.evnum = n

        def run(e, eng):
            seen = {}
            for ins in self.lists[e]:
                for d in ins.deps:
                    if d.slot is not None:
                        sem, val, key = d.slot.sem, d.dmaval, id(d.slot)
                    else:
                        sem, val, key = sems[d.eng], d.evnum, d.eng
                    if seen.get(key, 0) < val:
                        eng.wait_ge(sem, val)
                        seen[key] = val
                bi = ins.fn(eng)
                if ins.slot is not None:
                    bi.then_inc(ins.slot.sem, 16)
                elif ins.needs_inc:
                    bi.then_inc(sems[e], 1)
            for s in final_wait_slots.get(e, ()):
                if s.count > 0:
                    eng.wait_ge(s.sem, s.count)

        @block.sync
        def _(eng):
            run("sp", eng)

        @block.tensor
        def _(eng):
            run("pe", eng)

        @block.scalar
        def _(eng):
            run("act", eng)

        @block.vector
        def _(eng):
            run("dve", eng)

        @block.gpsimd
        def _(eng):
            run("pool", eng)


def alias(old, new):
    acc = {}
    for t in old:
        if t.w is not None:
            acc[("w", id(t.w))] = t.w
        for k, r in t.rs.items():
            acc[("r", id(r))] = r
    for t in new:
        t.w = None
        t.rs = dict(acc)


def vec_layout():
    off = {}
    n = 0

    def add(name, cols):
        nonlocal n
        off[name] = n
        n += cols

    for l in range(2):
        for nm in ("norm_ffn1", "norm_mix", "norm_ffn2"):
            add(f"{nm}{l}", 8)
        add(f"b_ada{l}", 72)
    add("final_norm", 8)
    add("hgrn_gnorm", 1)
    add("s5_d", 4)
    add("rwkv_mu", 14)
    for nm in ("rwkv_a0", "rwkv_k_k", "rwkv_k_a", "rwkv_r_k", "rwkv_lnx_g", "rwkv_lnx_b"):
        add(nm, 4)
    return off, n


class View:
    def __init__(self, arena, name):
        self.arena = arena
        self.off = 0
        self.toks = []
        self.name = name

    def alloc(self, shape, dt=F32):
        n = 1
        for d in shape[1:]:
            n *= d
        w = n if dt == F32 else (n + 1) // 2
        a = self.arena[0:shape[0], self.off:self.off + w]
        self.off += w
        assert self.off <= self.arena.shape[1], (self.name, self.off, self.arena.shape)
        if dt != F32:
            a = a.bitcast(dt)[:, 0:n]
        if len(shape) == 3:
            a = a.rearrange("p (a b) -> p a b", a=shape[1])
        elif len(shape) == 4:
            a = a.rearrange("p (a b c) -> p a b c", a=shape[1], b=shape[2])
        return a

    def tok(self, name=""):
        t = Tok(name)
        self.toks.append(t)
        return t


def build(cfg):
    SEQ = cfg["SEQ"]
    DSEQ = cfg.get("DSEQ", 32)
    TOK = min(512, SEQ)
    mix = cfg.get("mix", ("s5", "rwkv", "hgrn"))
    nc = bass.Bass("TRN2", target_bir_lowering=False)
    P = Prog()
    voff, NV = vec_layout()

    def din(name, shape, dt=F32):
        return nc.dram_tensor(name, list(shape), dt, kind="ExternalInput").ap()

    def dout(name, shape):
        return nc.dram_tensor(name, list(shape), F32, kind="ExternalOutput").ap()

    def dscr(name, shape, dt=BF16):
        return nc.dram_tensor(name, list(shape), dt, kind="Internal").ap()

    xT_d = din("xT", [2, D, SEQ])
    xsT_d = din("xsT", [2, D, DSEQ])
    cT_d = din("cT", [128, 8, 4])
    wada_d = din("w_ada", [2, D, 9 * D])
    vecs_d = din("vecs", [128, NV])
    ident_d = din("ident", [128, 128])
    cmask_d = din("cmask", [128, 6, 128])
    yT_d = dout("yT", [2, D, SEQ])
    ysT_d = dout("ysT", [2, D, DSEQ])
    wsrc = {}
    wscr = {}

    def wreg(key, name, shape):
        wsrc[key] = din("w_" + name, shape)
        wscr[key] = dscr("s_" + name, shape)

    for l in range(2):
        for fi in range(2):
            for nm in ("w1", "w3"):
                wreg((l, fi, nm), f"f{l}{fi}{nm}", [11, 128, 8, 256])
            wreg((l, fi, "w2"), f"f{l}{fi}w2", [2, 11, 128, 2, 512])
    if "hgrn" in mix:
        wreg("c_in", "c_in", [16, 128, 8, 256])
        wreg("c_out", "c_out", [2, 128, 8, 512])
        lbrow_d = din("lbrow", [128, 2, 1024])
        hg_in_d = din("hg_in", [2, 128, 8, 128])
        hg_out_d = dout("hg_out", [4, 128, 8, 128])

    if "s5" in mix or "rwkv" in mix:
        wreg("ab_in", "ab_in", [18, 128, 8, 128])
        wreg("ab_out", "ab_out", [2, 128, 8, 512])
    if "rwkv" in mix:
        rw_lo_d = din("rw_lo", [128, 512])
        rw_g2_d = din("rw_g2", [128, 512])
        rw_w0_d = din("rw_w0row", [128, 512])
        blk64_d = din("blk64", [128, 128])
        m5_d = din("m5", [64, 5, 64])
        rw_st_d = din("rw_st", [2, 128, 8, 64])
        rw_sh_d = din("rw_sh", [2, 128, 14])
        rw_sto_d = dout("rw_sto", [4, 128, 8, 64])
        rw_sho_d = dout("rw_sho", [4, 128, 14])
    if "s5" in mix:
        wreg("glu", "glu", [1, 128, 4, 512])
        s5p_d = din("s5p", [128, 3, 16])
        s5b_d = din("s5b", [128, 2, 16, 16])
        s5c_d = din("s5c", [128, 2, 16, 16])
        s5x_d = din("s5x", [2, 128, 2, 16])
        s5o_d = dout("s5o", [4, 128, 2, 16])
        rowmask_d = din("rowmask", [128, 4])

    es = ExitStack()
    with es:
        def sb(name, shape, dt=F32):
            return es.enter_context(nc.sbuf_tensor(name, list(shape), dt))

        xT = sb("xT_sb", [128, KT, TOK])
        hT = sb("hT_sb", [128, KT, TOK], BF16)
        rs1 = sb("rs1", [128, TOK])
        rstd = sb("rstd", [128, TOK])
        vecs = sb("vecs_sb", [128, NV])
        ident = sb("ident_sb", [128, 128])
        cmask = sb("cmask_sb", [128, 6, 128])
        onesb = sb("onesb", [128, 128], BF16)
        onesf = sb("onesf", [128, 128])
        epsc = sb("epsc", [128, 4])
        cs = sb("cs_sb", [128, 8, 4])
        modT = sb("modT", [128, 2, 72, 4])
        modA = sb("modA", [128, 2, 3, 8, 4])
        modG = sb("modG", [128, 2, 3, 8, 4])
        NW13 = 2
        w1b = [sb(f"w1b{i}", [128, 8, 256], BF16) for i in range(NW13)]
        w3b = [sb(f"w3b{i}", [128, 8, 256], BF16) for i in range(NW13)]
        NW2 = 3
        w2b = [sb(f"w2b{i}", [128, 2, 512], BF16) for i in range(NW2)]
        NWM = 2
        wmb = [sb(f"wmb{i}", [128, 8, 512], BF16) for i in range(NWM)]
        ps = [es.enter_context(nc.psum_tensor(f"ps{i}", [128, 512], F32)) for i in range(8)]
        NA = 15872
        arena = sb("arena", [128, NA])

        Tx = [Tok(f"x{k}") for k in range(KT)]
        Th = [Tok(f"h{k}") for k in range(KT)]
        Trs1, Trstd = Tok(), Tok()
        Tvecs, Tident, Tones, Teps, Tcs, Tcm = Tok(), Tok(), Tok(), Tok(), Tok(), Tok()
        TmodT, TmodA, TmodG = Tok(), Tok(), Tok()
        Tw13 = [Tok() for _ in range(NW13)]
        Tw2 = [Tok() for _ in range(NW2)]
        Twm = [Tok() for _ in range(NWM)]
        Tps = [Tok(f"ps{i}") for i in range(8)]
        Tscr = {k: Tok() for k in wscr}

        VI = View(arena, "init")
        NWA = 6
        wabuf = [VI.alloc([128, 512]) for _ in range(NWA)]
        Twab = [VI.tok() for _ in range(NWA)]
        modtm = VI.alloc([4, 512])
        Tmodtm = VI.tok()
        lbtmp = VI.alloc([128, 2, 1024])
        Tlbtmp = VI.tok()

        VF = View(arena, "ffn")
        sq = VF.alloc([128, KT, TOK], BF16)
        hid = VF.alloc([128, NFT, TOK], BF16)
        stmp = [VF.alloc([128, TOK]) for _ in range(2)]
        ntmp = [VF.alloc([128, TOK]) for _ in range(2)]
        yT = VF.alloc([128, KT, TOK])
        Tsq = VF.tok("sq")
        Thid = [VF.tok(f"hid{f}") for f in range(NFT)]
        Tstmp = [VF.tok(), VF.tok()]
        Tntmp = [VF.tok(), VF.tok()]
        Ty = [VF.tok() for _ in range(KT)]

        cur_view = [VI]

        def switch(view, keep=()):
            if cur_view[0] is not view:
                kk_ = set(id(t) for t in keep)
                alias(cur_view[0].toks, [t for t in view.toks if id(t) not in kk_])
                cur_view[0] = view

        S_const = [P.slot(f"const{i}") for i in range(8)]
        S_wab = [P.slot(f"wab{i}") for i in range(NWA)]
        S_w1 = [P.slot(f"w1_{i}") for i in range(NW13)]
        S_w3 = [P.slot(f"w3_{i}") for i in range(NW13)]
        S_w2 = [P.slot(f"w2_{i}") for i in range(NW2)]
        S_wm = [P.slot(f"wm_{i}") for i in range(NWM)]
        S_x = [P.slot("xin0"), P.slot("xin1")]
        S_y = [P.slot("yout0"), P.slot("yout1")]
        S_pre = P.slot("pre")
        S_st = [P.slot(f"st{i}") for i in range(4)]
        S_so = [P.slot(f"so{i}") for i in range(4)]
        S_rm = P.slot("rm")
        S_rc = [P.slot(f"rc{i}") for i in range(5)]
        S_rst = [P.slot(f"rst{i}") for i in range(4)]
        S_rso = [P.slot(f"rso{i}") for i in range(8)]
        S_sx = [P.slot(f"sx{i}") for i in range(2)]
        S_s5o = [P.slot(f"s5o{i}") for i in range(4)]

        for key in wscr:
            src, dst = wsrc[key], wscr[key]
            for i in range(src.shape[0] if len(src.shape) == 5 else 1):
                if len(src.shape) == 5:
                    P.op("pool", lambda e, s=src, d=dst, i=i: e.dma_start(out=d[i], in_=s[i]),
                         writes=[Tscr[key]], slot=S_pre)
                else:
                    P.op("pool", lambda e, s=src, d=dst: e.dma_start(out=d[:], in_=s[:]),
                         writes=[Tscr[key]], slot=S_pre)
        last_pre = P.lists["pool"][-1]
        for key in Tscr:
            Tscr[key].w = last_pre

        P.op("sp", lambda e: e.dma_start(out=vecs[:], in_=vecs_d[:, :]), writes=[Tvecs], slot=S_const[0])
        P.op("sp", lambda e: e.dma_start(out=ident[:], in_=ident_d[:, :]), writes=[Tident], slot=S_const[1])
        P.op("sp", lambda e: e.dma_start(out=cs[:], in_=cT_d[:, :, :]), writes=[Tcs], slot=S_const[2])
        P.op("sp", lambda e: e.dma_start(out=cmask[:], in_=cmask_d[:, :, :]), writes=[Tcm], slot=S_const[3])
        P.op("dve", lambda e: e.memset(onesb[:], 1.0 / D), writes=[Tones])
        P.op("dve", lambda e: e.memset(onesf[:], 1.0 / 128), writes=[Tones])
        P.op("dve", lambda e: e.memset(epsc[:, 0:1], NORM_EPS), writes=[Teps])
        P.op("dve", lambda e: e.memset(epsc[:, 1:2], GN_EPS), writes=[Teps])
        P.op("dve", lambda e: e.memset(epsc[:, 2:3], 1.0), writes=[Teps])
        P.op("dve", lambda e: e.memset(epsc[:, 3:4], 0.0), writes=[Teps])

        P.op("act", lambda e: e.activation(out=cs[:], in_=cs[:], func=AF.Silu), reads=[Tcs], writes=[Tcs])
        wi = 0
        for l in range(2):
            for ch in range(18):
                for k in range(KT):
                    b = wi % NWA
                    wi += 1
                    P.op("sp", lambda e, l=l, ch=ch, k=k, b=b: e.dma_start(
                        out=wabuf[b][:], in_=wada_d[l, k * 128:(k + 1) * 128, ch * 512:(ch + 1) * 512]),
                        writes=[Twab[b]], slot=S_wab[b])
                    P.op("pe", lambda e, k=k, b=b: e.matmul(ps[0][0:4, :], lhsT=cs[:, k, :], rhs=wabuf[b][:],
                                                             start=(k == 0), stop=(k == KT - 1)),
                         reads=[Tcs, Twab[b]], writes=[Tps[0]])
                P.op("dve", lambda e: e.tensor_copy(out=modtm[:], in_=ps[0][0:4, :]), reads=[Tps[0]], writes=[Tmodtm])
                for j in range(4):
                    P.op("pe", lambda e, j=j: e.transpose(ps[1][:, j * 4:(j + 1) * 4], modtm[:, j * 128:(j + 1) * 128],
                                                           ident[0:4, 0:4]),
                         reads=[Tmodtm, Tident], writes=[Tps[1]])
                P.op("act", lambda e, l=l, ch=ch: e.copy(
                    out=modT[:, l, ch * 4:(ch + 1) * 4, :], in_=ps[1][:, 0:16].rearrange("p (j s) -> p j s", j=4)),
                    reads=[Tps[1]], writes=[TmodT])
        for l in range(2):
            o = voff[f"b_ada{l}"]
            P.op("dve", lambda e, l=l, o=o: e.tensor_tensor(
                out=modT[:, l, :, :], in0=modT[:, l, :, :],
                in1=vecs[:, o:o + 72].unsqueeze(2).to_broadcast([128, 72, 4]), op=ALU.add),
                reads=[TmodT, Tvecs], writes=[TmodT])
        for l in range(2):
            for n, nm in enumerate(("norm_ffn1", "norm_mix", "norm_ffn2")):
                o = voff[f"{nm}{l}"]
                P.op("dve", lambda e, l=l, n=n: e.tensor_scalar(
                    out=modA[:, l, n, :, :], in0=modT[:, l, (3 * n + 1) * 8:(3 * n + 2) * 8, :],
                    scalar1=1.0, scalar2=None, op0=ALU.add), reads=[TmodT], writes=[TmodA])
                P.op("dve", lambda e, l=l, n=n, o=o: e.tensor_tensor(
                    out=modA[:, l, n, :, :], in0=modA[:, l, n, :, :],
                    in1=vecs[:, o:o + 8].unsqueeze(2).to_broadcast([128, 8, 4]), op=ALU.mult),
                    reads=[TmodA, Tvecs], writes=[TmodA])
                cg = 1.0 if n == 1 else 0.5
                P.op("dve", lambda e, l=l, n=n, cg=cg: e.tensor_scalar(
                    out=modG[:, l, n, :, :], in0=modT[:, l, (3 * n + 2) * 8:(3 * n + 3) * 8, :],
                    scalar1=1.0, scalar2=cg, op0=ALU.add, op1=ALU.mult), reads=[TmodT], writes=[TmodG])

        def norm_mod(l, n, segs, T, final=False):
            switch(VF)
            P.op("pool", lambda e: e.tensor_tensor(out=sq[:, :, :T], in0=xT[:, :, :T], in1=xT[:, :, :T], op=ALU.mult),
                 reads=Tx, writes=[Tsq])
            for k in range(KT):
                P.op("pe", lambda e, k=k: e.matmul(ps[4][:, :T], lhsT=onesb[:], rhs=sq[:, k, :T],
                                                   start=(k == 0), stop=(k == KT - 1)),
                     reads=[Tsq, Tones], writes=[Tps[4]])
            P.op("act", lambda e: e.activation(out=rs1[:, :T], in_=ps[4][:, :T], func=AF.Ln, bias=epsc[:, 0:1], scale=1.0),
                 reads=[Tps[4], Teps], writes=[Trs1])
            P.op("act", lambda e: e.activation(out=rstd[:, :T], in_=rs1[:, :T], func=AF.Exp, scale=-0.5),
                 reads=[Trs1], writes=[Trstd])
            for k in range(KT):
                for (sq_, c0, n_) in segs:
                    if final:
                        o = voff["final_norm"]
                        P.op("dve", lambda e, k=k, c0=c0, n_=n_, o=o: e.scalar_tensor_tensor(
                            out=yT[:, k, c0:c0 + n_], in0=xT[:, k, c0:c0 + n_], scalar=vecs[:, o + k:o + k + 1],
                            in1=rstd[:, c0:c0 + n_], op0=ALU.mult, op1=ALU.mult),
                            reads=[Tx[k], Trstd, Tvecs], writes=[Ty[k]])
                    else:
                        nb = k % 2
                        P.op("dve", lambda e, k=k, c0=c0, n_=n_, s=sq_, nb=nb: e.scalar_tensor_tensor(
                            out=ntmp[nb][:, c0:c0 + n_], in0=xT[:, k, c0:c0 + n_], scalar=modA[:, l, n, k, s:s + 1],
                            in1=rstd[:, c0:c0 + n_], op0=ALU.mult, op1=ALU.mult),
                            reads=[Tx[k], Trstd, TmodA], writes=[Tntmp[nb]])
                        P.op("act", lambda e, k=k, c0=c0, n_=n_, s=sq_, nb=nb: e.activation(
                            out=hT[:, k, c0:c0 + n_], in_=ntmp[nb][:, c0:c0 + n_], func=AF.Identity,
                            bias=modT[:, l, 3 * n * 8 + k, s:s + 1], scale=1.0),
                            reads=[Tntmp[nb], TmodT], writes=[Th[k]])

        cnt = {"w13": 0, "w2": 0, "wm": 0}

        def ffn(l, fi, segs, T):
            n = 0 if fi == 0 else 2
            norm_mod(l, n, segs, T)
            w1s, w3s, w2s = wscr[(l, fi, "w1")], wscr[(l, fi, "w3")], wscr[(l, fi, "w2")]
            t1, t3, t2 = Tscr[(l, fi, "w1")], Tscr[(l, fi, "w3")], Tscr[(l, fi, "w2")]
            for g in range(11):
                b = cnt["w13"] % NW13
                cnt["w13"] += 1
                P.op("sp", lambda e, g=g, b=b: e.dma_start(out=w1b[b][:], in_=w1s[g]), reads=[t1], writes=[Tw13[b]], slot=S_w1[b])
                P.op("sp", lambda e, g=g, b=b: e.dma_start(out=w3b[b][:], in_=w3s[g]), reads=[t3], writes=[Tw13[b]], slot=S_w3[b])
                for j in range(2):
                    ft = 2 * g + j
                    ia, ib = (2 * ft) % 4, (2 * ft + 1) % 4
                    for k in range(KT):
                        P.op("pe", lambda e, k=k, b=b, j=j, ia=ia: e.matmul(
                            ps[ia][:, :T], lhsT=w1b[b][:, k, j * 128:(j + 1) * 128], rhs=hT[:, k, :T],
                            start=(k == 0), stop=(k == KT - 1)), reads=[Tw13[b], Th[k]], writes=[Tps[ia]])
                    for k in range(KT):
                        P.op("pe", lambda e, k=k, b=b, j=j, ib=ib: e.matmul(
                            ps[ib][:, :T], lhsT=w3b[b][:, k, j * 128:(j + 1) * 128], rhs=hT[:, k, :T],
                            start=(k == 0), stop=(k == KT - 1)), reads=[Tw13[b], Th[k]], writes=[Tps[ib]])
                    sbi = ft % 2
                    P.op("act", lambda e, ia=ia, sbi=sbi: e.activation(out=stmp[sbi][:, :T], in_=ps[ia][:, :T], func=AF.Silu),
                         reads=[Tps[ia]], writes=[Tstmp[sbi]])
                    P.op("dve", lambda e, ib=ib, sbi=sbi, ft=ft: e.tensor_tensor(
                        out=hid[:, ft, :T], in0=stmp[sbi][:, :T], in1=ps[ib][:, :T], op=ALU.mult),
                        reads=[Tstmp[sbi], Tps[ib]], writes=[Thid[ft]])
            for hf in range(2):
                for c in range(11):
                    b = cnt["w2"] % NW2
                    cnt["w2"] += 1
                    P.op("sp", lambda e, hf=hf, c=c, b=b: e.dma_start(out=w2b[b][:], in_=w2s[hf, c]),
                         reads=[t2], writes=[Tw2[b]], slot=S_w2[b])
                    for j in range(2):
                        ft = 2 * c + j
                        for d4 in range(4):
                            P.op("pe", lambda e, b=b, j=j, d4=d4, ft=ft: e.matmul(
                                ps[4 + d4][:, :T], lhsT=w2b[b][:, j, d4 * 128:(d4 + 1) * 128], rhs=hid[:, ft, :T],
                                start=(ft == 0), stop=(ft == NFT - 1)), reads=[Tw2[b], Thid[ft]], writes=[Tps[4 + d4]])
                for d4 in range(4):
                    d = hf * 4 + d4
                    for (s, c0, n_) in segs:
                        P.op("dve", lambda e, d=d, d4=d4, s=s, c0=c0, n_=n_: e.scalar_tensor_tensor(
                            out=xT[:, d, c0:c0 + n_], in0=ps[4 + d4][:, c0:c0 + n_], scalar=modG[:, l, n, d, s:s + 1],
                            in1=xT[:, d, c0:c0 + n_], op0=ALU.mult, op1=ALU.add),
                            reads=[Tps[4 + d4], TmodG, Tx[d]], writes=[Tx[d]])

        def load_wm(key, idx, width=512):
            b = cnt["wm"] % NWM
            cnt["wm"] += 1
            src = wscr[key]
            P.op("sp", lambda e, b=b, idx=idx: e.dma_start(out=wmb[b][:, :, 0:width], in_=src[idx]),
                 reads=[Tscr[key]], writes=[Twm[b]], slot=S_wm[b])
            return b

        def out_proj(key, l, ycT, Tyc, segs, T):
            for hf in range(2):
                b = load_wm(key, hf)
                for d4 in range(4):
                    for k in range(KT):
                        P.op("pe", lambda e, b=b, d4=d4, k=k: e.matmul(
                            ps[4 + d4][:, :T], lhsT=wmb[b][:, k, d4 * 128:(d4 + 1) * 128], rhs=ycT[:, k, :T],
                            start=(k == 0), stop=(k == KT - 1)), reads=[Twm[b], Tyc[k]], writes=[Tps[4 + d4]])
                for d4 in range(4):
                    d = hf * 4 + d4
                    for (s, c0, n_) in segs:
                        P.op("dve", lambda e, d=d, d4=d4, s=s, c0=c0, n_=n_: e.scalar_tensor_tensor(
                            out=xT[:, d, c0:c0 + n_], in0=ps[4 + d4][:, c0:c0 + n_], scalar=modG[:, l, 1, d, s:s + 1],
                            in1=xT[:, d, c0:c0 + n_], op0=ALU.mult, op1=ALU.add),
                            reads=[Tps[4 + d4], TmodG, Tx[d]], writes=[Tx[d]])

        if "hgrn" in mix:
            omlrow = sb("omlrow", [128, 1024])
            Toml = Tok()
            Sg = [sb(f"Sg{i}", [128, 8, 128]) for i in range(2)]
            TS = [Tok(), Tok()]
            P.op("sp", lambda e: e.dma_start(out=lbtmp[:], in_=lbrow_d[:, :, :]), writes=[Tlbtmp], slot=S_const[4])
            P.op("dve", lambda e: e.tensor_tensor(out=lbtmp[:, 0, :], in0=lbtmp[:, 1, :], in1=lbtmp[:, 0, :], op=ALU.subtract),
                 reads=[Tlbtmp], writes=[Tlbtmp])
            P.op("act", lambda e: e.activation(out=omlrow[:], in_=lbtmp[:, 0, :], func=AF.Sigmoid, scale=-1.0),
                 reads=[Tlbtmp], writes=[Toml])
            VH = View(arena, "hgrn")
            NH = 2
            NHG = 8 // NH
            CW = NH * 128
            h_qT = VH.alloc([128, NH, TOK])
            h_sg = VH.alloc([128, NH, TOK])
            h_eb = VH.alloc([128, NH, TOK])
            h_ktT = VH.alloc([128, NH, TOK])
            NSM = max(1, TOK // 128)
            h_lf = VH.alloc([128, NSM, CW])
            h_k = VH.alloc([128, NSM, CW])
            h_v = VH.alloc([128, NSM, CW])
            h_att = [VH.alloc([128, NH, 128]) for _ in range(2)]
            h_e1 = VH.alloc([128, CW])
            h_e2 = VH.alloc([128, CW])
            h_osq = [VH.alloc([128, TOK]) for _ in range(2)]
            h_t1 = [VH.alloc([128, TOK]) for _ in range(2)]
            h_yc = VH.alloc([128, 8, TOK], BF16)
            TqT = [VH.tok() for _ in range(NH)]
            Tsg = [VH.tok() for _ in range(NH)]
            Teb = [VH.tok() for _ in range(NH)]
            TktT = [VH.tok() for _ in range(NH)]
            Tlf = [VH.tok() for _ in range(NSM)]
            Tk = [VH.tok() for _ in range(NSM)]
            Tv = [VH.tok() for _ in range(NSM)]
            Tatt = [VH.tok(), VH.tok()]
            Te1, Te2 = VH.tok(), VH.tok()
            Tosq = [VH.tok(), VH.tok()]
            Tt1 = [VH.tok(), VH.tok()]
            Tyc = [VH.tok() for _ in range(8)]

            def hgrn(segs, T, blocks, R, CB, mi):
                switch(VH)
                NS = T // R
                tri = cmask[:R, mi, :R]
                trirev = cmask[:R, mi + 1, :R]
                for HG in range(NHG):
                    for (grp, dst, Tdst) in ((HG, h_qT, TqT), (3 * NHG + HG, h_sg, Tsg)):
                        b = load_wm("c_in", grp, CW)
                        for hh in range(NH):
                            pb = hh % 2
                            for k in range(KT):
                                P.op("pe", lambda e, b=b, hh=hh, k=k, pb=pb: e.matmul(
                                    ps[pb][:, :T], lhsT=wmb[b][:, k, hh * 128:(hh + 1) * 128], rhs=hT[:, k, :T],
                                    start=(k == 0), stop=(k == KT - 1)), reads=[Twm[b], Th[k]], writes=[Tps[pb]])
                            P.op("act", lambda e, dst=dst, hh=hh, pb=pb: e.activation(
                                out=dst[:, hh, :T], in_=ps[pb][:, :T], func=AF.Silu),
                                reads=[Tps[pb]], writes=[Tdst[hh]])
                    b = load_wm("c_in", NHG + HG, CW)
                    for s in range(NS):
                        pb = 2 + s % 2
                        for k in range(KT):
                            P.op("pe", lambda e, b=b, s=s, k=k, pb=pb: e.matmul(
                                ps[pb][:R, :CW], lhsT=hT[:, k, s * R:(s + 1) * R], rhs=wmb[b][:, k, :CW],
                                start=(k == 0), stop=(k == KT - 1)), reads=[Twm[b], Th[k]], writes=[Tps[pb]])
                        P.op("act", lambda e, s=s, pb=pb: e.activation(
                            out=h_k[:R, s, :], in_=ps[pb][:R, :CW], func=AF.Sigmoid, scale=-1.0),
                            reads=[Tps[pb]], writes=[Tk[s]])
                        P.op("dve", lambda e, s=s, HG=HG: e.tensor_tensor(
                            out=h_k[:R, s, :], in0=h_k[:R, s, :], in1=omlrow[:R, HG * CW:(HG + 1) * CW], op=ALU.mult),
                            reads=[Tk[s], Toml], writes=[Tk[s]])
                        P.op("act", lambda e, s=s: e.activation(
                            out=h_lf[:R, s, :], in_=h_k[:R, s, :], func=AF.Ln, scale=-1.0, bias=epsc[:R, 2:3]),
                            reads=[Tk[s], Teps], writes=[Tlf[s]])
                    b = load_wm("c_in", 2 * NHG + HG, CW)
                    for s in range(NS):
                        pb = 2 + s % 2
                        for k in range(KT):
                            P.op("pe", lambda e, b=b, s=s, k=k, pb=pb: e.matmul(
                                ps[pb][:R, :CW], lhsT=hT[:, k, s * R:(s + 1) * R], rhs=wmb[b][:, k, :CW],
                                start=(k == 0), stop=(k == KT - 1)), reads=[Twm[b], Th[k]], writes=[Tps[pb]])
                        P.op("act", lambda e, s=s, pb=pb: e.copy(out=h_v[:R, s, :], in_=ps[pb][:R, :CW]),
                             reads=[Tps[pb]], writes=[Tv[s]])
                    for hh in range(NH):
                        pb = 4 + hh % 2
                        for s in range(NS):
                            P.op("pe", lambda e, hh=hh, s=s, pb=pb: e.matmul(
                                ps[pb][:, s * R:(s + 1) * R], lhsT=h_lf[:R, s, hh * 128:(hh + 1) * 128], rhs=tri,
                                start=True, stop=True), reads=[Tlf[s], Tcm], writes=[Tps[pb]])
                        P.op("act", lambda e, hh=hh, pb=pb: e.activation(out=h_eb[:, hh, :T], in_=ps[pb][:, :T], func=AF.Exp),
                             reads=[Tps[pb]], writes=[Teb[hh]])
                        P.op("dve", lambda e, hh=hh: e.tensor_tensor(
                            out=h_qT[:, hh, :T], in0=h_qT[:, hh, :T], in1=h_eb[:, hh, :T], op=ALU.mult),
                            reads=[TqT[hh], Teb[hh]], writes=[TqT[hh]])
                    for s in range(NS):
                        P.op("pe", lambda e, s=s: e.matmul(ps[6][:R, :CW], lhsT=tri, rhs=h_lf[:R, s, :], start=True, stop=True),
                             reads=[Tlf[s], Tcm], writes=[Tps[6]])
                        P.op("pe", lambda e, s=s: e.matmul(ps[7][:R, :CW], lhsT=trirev, rhs=h_lf[:R, s, :], start=True, stop=True),
                             reads=[Tlf[s], Tcm], writes=[Tps[7]])
                        P.op("act", lambda e: e.activation(out=h_e1[:R, :], in_=ps[6][:R, :CW], func=AF.Exp, scale=-1.0),
                             reads=[Tps[6]], writes=[Te1])
                        P.op("act", lambda e: e.activation(out=h_e2[:R, :], in_=ps[7][:R, :CW], func=AF.Exp),
                             reads=[Tps[7]], writes=[Te2])
                        P.op("dve", lambda e, s=s: e.tensor_tensor(out=h_lf[:R, s, :], in0=h_k[:R, s, :], in1=h_e1[:R, :], op=ALU.mult),
                             reads=[Tk[s], Te1], writes=[Tlf[s]])
                        P.op("dve", lambda e, s=s: e.tensor_tensor(out=h_k[:R, s, :], in0=h_k[:R, s, :], in1=h_e2[:R, :], op=ALU.mult),
                             reads=[Tk[s], Te2], writes=[Tk[s]])
                    for hh in range(NH):
                        pb = 4 + hh % 2
                        for s in range(NS):
                            P.op("pe", lambda e, hh=hh, s=s, pb=pb: e.transpose(
                                ps[pb][:, s * R:(s + 1) * R], h_lf[:R, s, hh * 128:(hh + 1) * 128], ident[:R, :R]),
                                reads=[Tlf[s], Tident], writes=[Tps[pb]])
                        P.op("act", lambda e, hh=hh, pb=pb: e.copy(out=h_ktT[:, hh, :T], in_=ps[pb][:, :T]),
                             reads=[Tps[pb]], writes=[TktT[hh]])
                    for s in range(NS):
                        pa = s % 2
                        ab = s % 2
                        for hh in range(NH):
                            P.op("pe", lambda e, hh=hh, s=s, pa=pa: e.matmul(
                                ps[pa][:R, hh * R:(hh + 1) * R], lhsT=h_ktT[:, hh, s * R:(s + 1) * R],
                                rhs=h_qT[:, hh, s * R:(s + 1) * R], start=True, stop=True),
                                reads=[TktT[hh], TqT[hh]], writes=[Tps[pa]])
                        P.op("dve", lambda e, pa=pa, ab=ab: e.tensor_tensor(
                            out=h_att[ab][:R, :, :R], in0=ps[pa][:R, 0:NH * R].rearrange("p (h t) -> p h t", h=NH),
                            in1=tri.unsqueeze(1).to_broadcast([R, NH, R]), op=ALU.mult),
                            reads=[Tps[pa], Tcm], writes=[Tatt[ab]])
                        po = 2 + s % 2
                        for hb in range(2):
                            sbuf_i, c0 = blocks[2 * s + hb]
                            Sb = Sg[sbuf_i]
                            for hh in range(NH):
                                h = HG * NH + hh
                                P.op("pe", lambda e, hh=hh, s=s, po=po, ab=ab, hb=hb: e.matmul(
                                    ps[po][:, hh * R + hb * CB:hh * R + (hb + 1) * CB], lhsT=h_v[:R, s, hh * 128:(hh + 1) * 128],
                                    rhs=h_att[ab][:R, hh, hb * CB:(hb + 1) * CB], start=True, stop=False),
                                    reads=[Tv[s], Tatt[ab]], writes=[Tps[po]])
                                P.op("pe", lambda e, hh=hh, h=h, po=po, hb=hb, c0=c0, Sb=Sb: e.matmul(
                                    ps[po][:, hh * R + hb * CB:hh * R + (hb + 1) * CB], lhsT=Sb[:, h, :],
                                    rhs=h_qT[:, hh, c0:c0 + CB], start=False, stop=True),
                                    reads=[TS[sbuf_i], TqT[hh]], writes=[Tps[po]])
                            pu = 6 + hb
                            for hh in range(NH):
                                P.op("pe", lambda e, hh=hh, s=s, hb=hb, pu=pu: e.matmul(
                                    ps[pu][:, hh * 128:(hh + 1) * 128],
                                    lhsT=h_k[hb * CB:(hb + 1) * CB, s, hh * 128:(hh + 1) * 128],
                                    rhs=h_v[hb * CB:(hb + 1) * CB, s, hh * 128:(hh + 1) * 128], start=True, stop=True),
                                    reads=[Tk[s], Tv[s]], writes=[Tps[pu]])
                            cl = c0 + CB - 1
                            P.op("dve", lambda e, HG=HG, cl=cl, Sb=Sb: e.tensor_tensor(
                                out=Sb[:, HG * NH:(HG + 1) * NH, :], in0=Sb[:, HG * NH:(HG + 1) * NH, :],
                                in1=h_eb[:, :, cl:cl + 1].to_broadcast([128, NH, 128]), op=ALU.mult),
                                reads=[TS[sbuf_i]] + Teb, writes=[TS[sbuf_i]])
                            P.op("dve", lambda e, HG=HG, pu=pu, Sb=Sb: e.tensor_tensor(
                                out=Sb[:, HG * NH:(HG + 1) * NH, :], in0=Sb[:, HG * NH:(HG + 1) * NH, :],
                                in1=ps[pu][:, :CW].rearrange("p (h v) -> p h v", h=NH), op=ALU.add),
                                reads=[TS[sbuf_i], Tps[pu]], writes=[TS[sbuf_i]])
                        P.op("act", lambda e, s=s, po=po: e.copy(
                            out=h_qT[:, :, s * R:(s + 1) * R], in_=ps[po][:, 0:NH * R].rearrange("p (h t) -> p h t", h=NH)),
                            reads=[Tps[po]], writes=TqT)
                    og = voff["hgrn_gnorm"]
                    for hh in range(NH):
                        h = HG * NH + hh
                        i2 = hh % 2
                        P.op("act", lambda e, hh=hh, i2=i2: e.activation(out=h_osq[i2][:, :T], in_=h_qT[:, hh, :T], func=AF.Square),
                             reads=[TqT[hh]], writes=[Tosq[i2]])
                        pb = 4 + hh % 2
                        P.op("pe", lambda e, i2=i2, pb=pb: e.matmul(ps[pb][:, :T], lhsT=onesf[:], rhs=h_osq[i2][:, :T], start=True, stop=True),
                             reads=[Tosq[i2], Tones], writes=[Tps[pb]])
                        P.op("act", lambda e, i2=i2, pb=pb: e.activation(out=h_osq[i2][:, :T], in_=ps[pb][:, :T], func=AF.Ln, bias=epsc[:, 0:1], scale=1.0),
                             reads=[Tps[pb], Teps], writes=[Tosq[i2]])
                        P.op("act", lambda e, i2=i2: e.activation(out=h_osq[i2][:, :T], in_=h_osq[i2][:, :T], func=AF.Exp, scale=-0.5),
                             reads=[Tosq[i2]], writes=[Tosq[i2]])
                        P.op("dve", lambda e, hh=hh, i2=i2: e.scalar_tensor_tensor(
                            out=h_t1[i2][:, :T], in0=h_qT[:, hh, :T], scalar=vecs[:, og:og + 1], in1=h_osq[i2][:, :T],
                            op0=ALU.mult, op1=ALU.mult), reads=[TqT[hh], Tosq[i2], Tvecs], writes=[Tt1[i2]])
                        P.op("dve", lambda e, hh=hh, h=h, i2=i2: e.tensor_tensor(
                            out=h_yc[:, h, :T], in0=h_t1[i2][:, :T], in1=h_sg[:, hh, :T], op=ALU.mult),
                            reads=[Tt1[i2], Tsg[hh]], writes=[Tyc[h]])
                out_proj("c_out", 1, h_yc, Tyc, segs, T)


        def TT(eng, out, a, b, op, r, w):
            return P.op(eng, lambda e: e.tensor_tensor(out=out, in0=a, in1=b, op=op), reads=r, writes=w)

        def TSC(eng, out, a, s1, s2, op0, op1, r, w):
            if s2 is None:
                return P.op(eng, lambda e: e.tensor_scalar(out=out, in0=a, scalar1=s1, scalar2=None, op0=op0), reads=r, writes=w)
            return P.op(eng, lambda e: e.tensor_scalar(out=out, in0=a, scalar1=s1, scalar2=s2, op0=op0, op1=op1), reads=r, writes=w)

        def ACTF(out, a, func, r, w, bias=None, scale=1.0):
            if bias is None:
                return P.op("act", lambda e: e.activation(out=out, in_=a, func=func, scale=scale), reads=r, writes=w)
            return P.op("act", lambda e: e.activation(out=out, in_=a, func=func, bias=bias, scale=scale), reads=r, writes=w)

        if "s5" in mix or "rwkv" in mix:
            VA = View(arena, "mix0")
            a_yc = VA.alloc([128, 8, TOK], BF16)
            Tayc = [VA.tok() for _ in range(8)]
            VB = View(arena, "rwkv")
            VB.alloc([128, 8, TOK], BF16)
            VB.toks.extend(Tayc)
        if "s5" in mix:
            FR = min(64, TOK)
            s5s = sb("s5s", [128, 24, 16])
            Ftab = sb("Ftab", [128, 2, 16, FR])
            Etab = sb("Etab", [128, 2, 16, FR])
            BbTz = sb("BbTz", [128, 16, 2, 128], BF16)
            Cblk = sb("Cblk", [128, 16, 2, 128], BF16)
            rmaskA = sb("rmaskA", [128, 16, FR])
            rmaskB = sb("rmaskB", [128, 16, DSEQ])
            rowmask = sb("rowmask_sb", [128, 4])
            Xs = [sb(f"Xs{i}", [128, 2, 16]) for i in range(2)]
            TXs = [Tok(), Tok()]
            Tc5 = Tok("s5const")
            LR, LI, LDT, DT, TH, MAG, CC, SS, T1, T2, ABR, ABI, ZR, ZI, DEN, PR, PI, T3, T4 = [s5s[:, i, :] for i in range(19)]
            Bsrc = VI.alloc([128, 2, 16, 16])
            Csrc = VI.alloc([128, 2, 16, 16])
            bbar = VI.alloc([128, 2, 16, 16])
            Bblk = VI.alloc([128, 2, 16, 2, 16]) if False else VI.alloc([128, 2, 512])
            ftmp = [VI.alloc([128, 16, FR]) for _ in range(4)]
            P.op("sp", lambda e: e.dma_start(out=s5s[:, 0:3, :], in_=s5p_d[:, :, :]), writes=[Tc5], slot=S_const[5])
            P.op("sp", lambda e: e.dma_start(out=Bsrc[:], in_=s5b_d[:, :, :, :]), writes=[Tc5], slot=S_const[6])
            P.op("sp", lambda e: e.dma_start(out=Csrc[:], in_=s5c_d[:, :, :, :]), writes=[Tc5], slot=S_const[7])
            P.op("sp", lambda e: e.dma_start(out=rowmask[:], in_=rowmask_d[:, :]), writes=[Tc5], slot=S_rm)
            C5 = [Tc5]
            ACTF(DT, LDT, AF.Exp, C5, C5)
            TT("dve", MAG, LR, DT, ALU.mult, C5, C5)
            TT("dve", TH, LI, DT, ALU.mult, C5, C5)
            ACTF(MAG, MAG, AF.Exp, C5, C5)
            P.op("dve", lambda e: e.memset(s5s[:, 23, :], float(np.pi / 2)), writes=C5)
            ACTF(SS, TH, AF.Sin, C5, C5, scale=1.0 / 16)
            TSC("dve", T1, TH, 1.0 / 16, float(np.pi / 2), ALU.mult, ALU.add, C5, C5)
            ACTF(CC, T1, AF.Sin, C5, C5)
            for _ in range(4):
                TT("dve", T1, CC, CC, ALU.mult, C5, C5)
                TT("dve", T2, SS, SS, ALU.mult, C5, C5)
                TT("dve", T3, CC, SS, ALU.mult, C5, C5)
                TSC("dve", SS, T3, 2.0, None, ALU.mult, None, C5, C5)
                TT("dve", CC, T1, T2, ALU.subtract, C5, C5)
            TT("dve", ABR, MAG, CC, ALU.mult, C5, C5)
            TT("dve", ABI, MAG, SS, ALU.mult, C5, C5)
            TT("dve", T1, LR, LR, ALU.mult, C5, C5)
            TT("dve", T2, LI, LI, ALU.mult, C5, C5)
            TT("dve", DEN, T1, T2, ALU.add, C5, C5)
            P.op("dve", lambda e: e.reciprocal(out=DEN, in_=DEN), reads=C5, writes=C5)
            TSC("dve", T3, ABR, -1.0, None, ALU.add, None, C5, C5)
            TT("dve", T1, T3, LR, ALU.mult, C5, C5)
            TT("dve", T2, ABI, LI, ALU.mult, C5, C5)
            TT("dve", T1, T1, T2, ALU.add, C5, C5)
            TT("dve", ZR, T1, DEN, ALU.mult, C5, C5)
            TT("dve", T1, ABI, LR, ALU.mult, C5, C5)
            TT("dve", T2, T3, LI, ALU.mult, C5, C5)
            TT("dve", T1, T1, T2, ALU.subtract, C5, C5)
            TT("dve", ZI, T1, DEN, ALU.mult, C5, C5)
            zrb = ZR.unsqueeze(2).to_broadcast([128, 16, 16])
            zib = ZI.unsqueeze(2).to_broadcast([128, 16, 16])
            b1, b2 = ftmp[0][:, :, 0:16], ftmp[1][:, :, 0:16]
            TT("dve", b1, Bsrc[:, 0], zrb, ALU.mult, C5, C5)
            TT("dve", b2, Bsrc[:, 1], zib, ALU.mult, C5, C5)
            TT("dve", bbar[:, 0], b1, b2, ALU.subtract, C5, C5)
            TT("dve", b1, Bsrc[:, 1], zrb, ALU.mult, C5, C5)
            TT("dve", b2, Bsrc[:, 0], zib, ALU.mult, C5, C5)
            TT("dve", bbar[:, 1], b1, b2, ALU.add, C5, C5)
            P.op("dve", lambda e: e.memset(Bblk[:], 0.0), writes=C5)
            for ri in range(2):
                bv = Bblk[:, ri, :].rearrange("p (j g h) -> p j g h", j=16, g=2)
                for gl in range(2):
                    P.op("dve", lambda e, ri=ri, gl=gl, bv=bv: e.tensor_copy(
                        out=bv[gl * 64:(gl + 1) * 64, :, gl, :], in_=bbar[gl * 64:(gl + 1) * 64, ri, :, :]), reads=C5, writes=C5)
            for ri in range(2):
                for J in range(4):
                    P.op("pe", lambda e, ri=ri, J=J: e.transpose(ps[0][:, 0:128], Bblk[:, ri, J * 128:(J + 1) * 128], ident[:, :]),
                         reads=[Tc5, Tident], writes=[Tps[0]])
                    for jj in range(4):
                        P.op("dve", lambda e, ri=ri, J=J, jj=jj: e.tensor_scalar(
                            out=BbTz[:, 4 * J + jj, ri, :], in0=ps[0][:, 0:128], scalar1=rowmask[:, jj:jj + 1], scalar2=None,
                            op0=ALU.mult), reads=[Tps[0], Tc5], writes=[Tc5])
            P.op("dve", lambda e: e.memset(Cblk[:], 0.0), writes=C5)
            Cb5 = Cblk[:].rearrange("p (J q) r c -> p J q r c", q=4)
            for ri in range(2):
                Cs4 = Csrc[:, ri].rearrange("p (J q) h -> p J q h", q=4)
                for jj in range(4):
                    for gl in range(2):
                        c0_ = 32 * jj + 16 * gl
                        P.op("dve", lambda e, ri=ri, jj=jj, gl=gl, c0_=c0_, Cs4=Cs4: e.tensor_scalar(
                            out=Cb5[gl * 64:(gl + 1) * 64, :, jj, ri, c0_:c0_ + 16], in0=Cs4[gl * 64:(gl + 1) * 64, :, jj, :],
                            scalar1=(1.0 if ri == 0 else -1.0), scalar2=None, op0=ALU.mult), reads=C5, writes=C5)
            P.op("dve", lambda e: e.memset(Ftab[:, 0, :, 0:1], 1.0), writes=C5)
            P.op("dve", lambda e: e.memset(Ftab[:, 1, :, 0:1], 0.0), writes=C5)
            P.op("dve", lambda e: e.tensor_copy(out=PR, in_=ABR), reads=C5, writes=C5)
            P.op("dve", lambda e: e.tensor_copy(out=PI, in_=ABI), reads=C5, writes=C5)
            m = 1
            while m < FR:
                prb = PR.unsqueeze(2).to_broadcast([128, 16, m])
                pib = PI.unsqueeze(2).to_broadcast([128, 16, m])
                f1, f2 = ftmp[0][:, :, 0:m], ftmp[1][:, :, 0:m]
                TT("dve", f1, Ftab[:, 0, :, 0:m], prb, ALU.mult, C5, C5)
                TT("dve", f2, Ftab[:, 1, :, 0:m], pib, ALU.mult, C5, C5)
                TT("dve", Ftab[:, 0, :, m:2 * m], f1, f2, ALU.subtract, C5, C5)
                TT("dve", f1, Ftab[:, 0, :, 0:m], pib, ALU.mult, C5, C5)
                TT("dve", f2, Ftab[:, 1, :, 0:m], prb, ALU.mult, C5, C5)
                TT("dve", Ftab[:, 1, :, m:2 * m], f1, f2, ALU.add, C5, C5)
                TT("dve", T1, PR, PR, ALU.mult, C5, C5)
                TT("dve", T2, PI, PI, ALU.mult, C5, C5)
                TT("dve", T3, PR, PI, ALU.mult, C5, C5)
                TT("dve", PR, T1, T2, ALU.subtract, C5, C5)
                TSC("dve", PI, T3, 2.0, None, ALU.mult, None, C5, C5)
                m *= 2
            TT("dve", ftmp[0][:], Ftab[:, 0], Ftab[:, 0], ALU.mult, C5, C5)
            TT("dve", ftmp[1][:], Ftab[:, 1], Ftab[:, 1], ALU.mult, C5, C5)
            TT("dve", ftmp[0][:], ftmp[0][:], ftmp[1][:], ALU.add, C5, C5)
            P.op("dve", lambda e: e.reciprocal(out=ftmp[0][:], in_=ftmp[0][:]), reads=C5, writes=C5)
            TT("dve", Etab[:, 0], Ftab[:, 0], ftmp[0][:], ALU.mult, C5, C5)
            TT("dve", ftmp[1][:], Ftab[:, 1], ftmp[0][:], ALU.mult, C5, C5)
            TSC("dve", Etab[:, 1], ftmp[1][:], -1.0, None, ALU.mult, None, C5, C5)
            P.op("dve", lambda e: e.memset(rmaskA[:], 1.0), writes=C5)
            P.op("dve", lambda e: e.memset(rmaskA[:, :, 0:1], 0.0), writes=C5)
            P.op("dve", lambda e: e.memset(rmaskB[:], 1.0), writes=C5)
            P.op("dve", lambda e: e.memset(rmaskB[:, :, 0:1], 0.0), writes=C5)

            a_uT = VA.alloc([128, 4, TOK])
            a_ub = VA.alloc([128, 4, TOK], BF16)
            a_t = [VA.alloc([128, 8 * FR]) for _ in range(4)]
            _w_off = VA.off
            a_w = [VA.alloc([128, 16 * FR]) for _ in range(2)]
            a_cs = [VA.alloc([128, 16 * FR]) for _ in range(2)]
            _cs_end = VA.off
            a_xb = [VA.alloc([128, 16 * FR], BF16) for _ in range(2)]
            a_lx = VA.alloc([128, 4, 16])
            a_ya = VA.alloc([128, 4, TOK])
            _off = VA.off
            VA.off = _w_off
            a_g1 = VA.alloc([128, 4, TOK])
            a_yb = VA.alloc([128, 4, TOK], BF16)
            assert VA.off <= _cs_end
            VA.off = _off
            Tu = [VA.tok() for _ in range(4)]
            Tt = [VA.tok() for _ in range(4)]
            Tw = [VA.tok(), VA.tok()]
            Tcs_ = [VA.tok(), VA.tok()]
            Txb = [VA.tok(), VA.tok()]
            Tlx = VA.tok()
            Tya = [VA.tok() for _ in range(4)]
            Tg1L = [Tw[0], Tw[1]]
            Tyb = [Tcs_[0]]
            GC = float(2.0 * np.sqrt(2.0 / np.pi))

            def s5_proj(T):
                for ft in range(4):
                    b = load_wm("ab_in", ft, 128)
                    pb = ft % 2
                    for k in range(KT):
                        P.op("pe", lambda e, b=b, k=k, pb=pb: e.matmul(
                            ps[pb][:, :T], lhsT=wmb[b][:, k, 0:128], rhs=hT[:, k, :T],
                            start=(k == 0), stop=(k == KT - 1)), reads=[Twm[b], Th[k]], writes=[Tps[pb]])
                    P.op("act", lambda e, ft=ft, pb=pb: e.copy(out=a_uT[:, ft, :T], in_=ps[pb][:, :T]), reads=[Tps[pb]], writes=[Tu[ft]])
                    P.op("dve", lambda e, ft=ft, pb=pb: e.tensor_copy(out=a_ub[:, ft, :T], in_=ps[pb][:, :T]), reads=[Tps[pb]], writes=[Tu[ft]])

            def s5_frame(xi, c0, n, fi):
                X = Xs[xi]
                TX = TXs[xi]
                rm = (rmaskA if n == FR else rmaskB)[:].rearrange("p j t -> p (j t)")
                v3 = lambda buf, nj: buf[:, 0:nj * n].rearrange("p (j t) -> p j t", j=nj)
                for hf in range(2):
                    pz = (ps[0], ps[1]) if hf == 0 else (ps[2], ps[3])
                    tz = (Tps[0], Tps[1]) if hf == 0 else (Tps[2], Tps[3])
                    for ri in range(2):
                        for jj in range(8):
                            j = hf * 8 + jj
                            P.op("pe", lambda e, ri=ri, jj=jj, j=j, pz=pz: e.matmul(
                                pz[ri][:, jj * n:(jj + 1) * n], lhsT=BbTz[:, j, ri, :], rhs=a_ub[:, j // 4, c0:c0 + n],
                                start=True, stop=True), reads=[Tc5, Tu[j // 4]], writes=[tz[ri]])
                    zr = pz[0][:, 0:8 * n].rearrange("p (j t) -> p j t", j=8)
                    zi = pz[1][:, 0:8 * n].rearrange("p (j t) -> p j t", j=8)
                    Er = Etab[:, 0, hf * 8:(hf + 1) * 8, 0:n]
                    Ei = Etab[:, 1, hf * 8:(hf + 1) * 8, 0:n]
                    t = [v3(a_t[i], 8) for i in range(4)]
                    TT("dve", t[0], zr, Er, ALU.mult, [tz[0], Tc5], [Tt[0]])
                    TT("dve", t[1], zi, Ei, ALU.mult, [tz[1], Tc5], [Tt[1]])
                    TT("dve", t[2], zr, Ei, ALU.mult, [tz[0], Tc5], [Tt[2]])
                    TT("dve", t[3], zi, Er, ALU.mult, [tz[1], Tc5], [Tt[3]])
                    TT("pool", v3(a_w[0], 16)[:, hf * 8:(hf + 1) * 8, :], t[0], t[1], ALU.subtract, [Tt[0], Tt[1]], [Tw[0]])
                    TT("pool", v3(a_w[1], 16)[:, hf * 8:(hf + 1) * 8, :], t[2], t[3], ALU.add, [Tt[2], Tt[3]], [Tw[1]])
                lx = a_lx
                TT("pool", lx[:, 0], ABR, X[:, 0], ALU.mult, [Tc5, TX], [Tlx])
                TT("pool", lx[:, 1], ABI, X[:, 1], ALU.mult, [Tc5, TX], [Tlx])
                TT("pool", lx[:, 2], ABR, X[:, 1], ALU.mult, [Tc5, TX], [Tlx])
                TT("pool", lx[:, 3], ABI, X[:, 0], ALU.mult, [Tc5, TX], [Tlx])
                TT("pool", lx[:, 0], lx[:, 0], lx[:, 1], ALU.subtract, [Tlx], [Tlx])
                TT("pool", lx[:, 2], lx[:, 2], lx[:, 3], ALU.add, [Tlx], [Tlx])
                w0 = v3(a_w[0], 16)
                w1 = v3(a_w[1], 16)
                TT("pool", w0[:, :, 0:1], w0[:, :, 0:1], lx[:, 0].unsqueeze(2), ALU.add, [Tw[0], Tlx], [Tw[0]])
                TT("pool", w1[:, :, 0:1], w1[:, :, 0:1], lx[:, 2].unsqueeze(2), ALU.add, [Tw[1], Tlx], [Tw[1]])
                for ri, eng in ((0, "dve"), (1, "dve")):
                    P.op(eng, lambda e, ri=ri: e.tensor_tensor_scan(
                        out=a_cs[ri][:, 0:16 * n], data0=rm[:, 0:16 * n] if n == FR else rm, data1=a_w[ri][:, 0:16 * n], initial=0.0,
                        op0=ALU.mult, op1=ALU.add), reads=[Tw[ri], Tc5], writes=[Tcs_[ri]])
                for hf in range(2):
                    t = [v3(a_t[i], 8) for i in range(4)]
                    cr = v3(a_cs[0], 16)[:, hf * 8:(hf + 1) * 8, :]
                    ci = v3(a_cs[1], 16)[:, hf * 8:(hf + 1) * 8, :]
                    Fr = Ftab[:, 0, hf * 8:(hf + 1) * 8, 0:n]
                    Fi = Ftab[:, 1, hf * 8:(hf + 1) * 8, 0:n]
                    TT("dve", t[0], cr, Fr, ALU.mult, [Tcs_[0], Tc5], [Tt[0]])
                    TT("dve", t[1], ci, Fi, ALU.mult, [Tcs_[1], Tc5], [Tt[1]])
                    TT("dve", t[2], cr, Fi, ALU.mult, [Tcs_[0], Tc5], [Tt[2]])
                    TT("dve", t[3], ci, Fr, ALU.mult, [Tcs_[1], Tc5], [Tt[3]])
                    TT("pool", v3(a_xb[0], 16)[:, hf * 8:(hf + 1) * 8, :], t[0], t[1], ALU.subtract, [Tt[0], Tt[1]], [Txb[0]])
                    TT("pool", v3(a_xb[1], 16)[:, hf * 8:(hf + 1) * 8, :], t[2], t[3], ALU.add, [Tt[2], Tt[3]], [Txb[1]])
                    TT("pool", X[:, 0, hf * 8:(hf + 1) * 8].unsqueeze(2), t[0][:, :, n - 1:n], t[1][:, :, n - 1:n], ALU.subtract, [Tt[0], Tt[1]], [TX])
                    TT("pool", X[:, 1, hf * 8:(hf + 1) * 8].unsqueeze(2), t[2][:, :, n - 1:n], t[3][:, :, n - 1:n], ALU.add, [Tt[2], Tt[3]], [TX])
                py = ps[4 + fi % 2]
                tpy = Tps[4 + fi % 2]
                xr = v3(a_xb[0], 16)
                xim = v3(a_xb[1], 16)
                for c in range(4):
                    for jj in range(4):
                        for ri in range(2):
                            xx = xr if ri == 0 else xim
                            P.op("pe", lambda e, c=c, jj=jj, ri=ri, xx=xx, py=py: e.matmul(
                                py[:, c * n:(c + 1) * n], lhsT=Cblk[:, 4 * c + jj, ri, :], rhs=xx[:, 4 * c + jj, :],
                                start=(jj == 0 and ri == 0), stop=(jj == 3 and ri == 1)), reads=[Tc5, Txb[ri]], writes=[tpy])
                og = voff["s5_d"]
                for c in range(4):
                    P.op("dve", lambda e, c=c, py=py, og=og: e.scalar_tensor_tensor(
                        out=a_ya[:, c, c0:c0 + n], in0=a_uT[:, c, c0:c0 + n], scalar=vecs[:, og + c:og + c + 1],
                        in1=py[:, c * n:(c + 1) * n], op0=ALU.mult, op1=ALU.add), reads=[Tu[c], tpy, Tvecs], writes=[Tya[c]])

            def s5_post(T):
                TT("dve", a_g1[:, :, :T], a_ya[:, :, :T], a_ya[:, :, :T], ALU.mult, Tya, [Tw[0], Tw[1], Tcs_[0], Tcs_[1]])
                TSC("dve", a_g1[:, :, :T], a_g1[:, :, :T], 0.044715, 1.0, ALU.mult, ALU.add, Tg1L, Tg1L)
                TT("dve", a_g1[:, :, :T], a_g1[:, :, :T], a_ya[:, :, :T], ALU.mult, Tya + Tg1L, Tg1L)
                ACTF(a_g1[:, :, :T], a_g1[:, :, :T], AF.Sigmoid, Tg1L, Tg1L, scale=GC)
                TT("dve", a_ya[:, :, :T], a_ya[:, :, :T], a_g1[:, :, :T], ALU.mult, Tya + Tg1L, Tya)
                P.op("pool", lambda e: e.tensor_copy(out=a_yb[:, :, :T], in_=a_ya[:, :, :T]), reads=Tya, writes=Tyb)
                b = cnt["wm"] % NWM
                cnt["wm"] += 1
                gsrc = wscr["glu"]
                P.op("sp", lambda e, b=b: e.dma_start(out=wmb[b][:, 0:4, :], in_=gsrc[0]), reads=[Tscr["glu"]], writes=[Twm[b]], slot=S_wm[b])
                for c2 in range(4):
                    pb = c2 % 2
                    for c in range(4):
                        P.op("pe", lambda e, b=b, c=c, c2=c2, pb=pb: e.matmul(
                            ps[pb][:, :T], lhsT=wmb[b][:, c, c2 * 128:(c2 + 1) * 128], rhs=a_yb[:, c, :T],
                            start=(c == 0), stop=(c == 3)), reads=[Twm[b]] + Tyb, writes=[Tps[pb]])
                    ACTF(a_g1[:, c2, :T], ps[pb][:, :T], AF.Sigmoid, [Tps[pb]], Tg1L)
                    TT("dve", a_yc[:, c2, :T], a_ya[:, c2, :T], a_g1[:, c2, :T], ALU.mult, [Tya[c2]] + Tg1L, [Tayc[c2]])


        if "rwkv" in mix:
            LAM = float(np.exp(-0.5))
            wlo = sb("wlo", [128, 512])
            wg2 = sb("wg2", [128, 512])
            w0row = sb("w0row", [128, 512])
            blk64 = sb("blk64_sb", [128, 128])
            m5 = sb("m5_sb", [64, 5, 64])
            STz = [sb(f"STz{i}", [128, 8, 64]) for i in range(2)]
            shs = [sb(f"shs{i}", [128, 14]) for i in range(2)]
            omm = sb("omm", [128, 14])
            omka = sb("omka", [128, 4])
            TST = [Tok(), Tok()]
            Tsh = [Tok(), Tok()]
            Trc = Tok("rwconst")
            P.op("sp", lambda e: e.dma_start(out=wlo[:], in_=rw_lo_d[:, :]), writes=[Trc], slot=S_rc[0])
            P.op("sp", lambda e: e.dma_start(out=wg2[:], in_=rw_g2_d[:, :]), writes=[Trc], slot=S_rc[1])
            P.op("sp", lambda e: e.dma_start(out=w0row[:], in_=rw_w0_d[:, :]), writes=[Trc], slot=S_rc[2])
            P.op("sp", lambda e: e.dma_start(out=blk64[:], in_=blk64_d[:, :]), writes=[Trc], slot=S_rc[3])
            P.op("sp", lambda e: e.dma_start(out=m5[:], in_=m5_d[:, :, :]), writes=[Trc], slot=S_rc[4])
            omu = voff["rwkv_mu"]
            TSC("dve", omm[:], vecs[:, omu:omu + 14], -1.0, 1.0, ALU.mult, ALU.add, [Tvecs], [Trc])
            oka = voff["rwkv_k_a"]
            TSC("dve", omka[:], vecs[:, oka:oka + 4], -1.0, 1.0, ALU.mult, ALU.add, [Tvecs], [Trc])

            NSM2 = max(1, TOK // 128)
            b_wa = VB.alloc([128, TOK])
            b_gl = VB.alloc([128, TOK])
            b_r = VB.alloc([128, TOK])
            b_k = VB.alloc([128, TOK])
            b_v = VB.alloc([128, TOK])
            b_a = VB.alloc([128, TOK])
            b_sig = VB.alloc([128, NSM2, 128])
            b_eW = VB.alloc([128, TOK])
            b_e = [VB.alloc([128, TOK]) for _ in range(2)]
            b_kk = VB.alloc([128, TOK])
            b_kp = VB.alloc([128, TOK])
            b_kb = VB.alloc([128, TOK])
            b_tmp = VB.alloc([128, TOK])
            b_AR = VB.alloc([128, 2, TOK])
            b_Bh = VB.alloc([128, TOK])
            b_Kh = VB.alloc([128, TOK])
            b_Bc = VB.alloc([128, TOK])
            b_Kc = VB.alloc([128, TOK])
            b_bon = VB.alloc([128, TOK])
            b_y = VB.alloc([128, TOK])
            c_tm = VB.alloc([64, 384])
            c_A5 = [VB.alloc([64, 5, 64]) for _ in range(2)]
            c_NP = VB.alloc([64, 5, 4, 64])
            c_X = [VB.alloc([64, 2, 64]) for _ in range(2)]
            Twa, Tgl, Tr_, Tk_, Tv_, Ta_, Tsig, TeW = [VB.tok() for _ in range(8)]
            Te_ = [VB.tok(), VB.tok()]
            Tkk, Tkp, Tkb, Ttmp, TAR, TBh, TKh, TBc, TKc, Tbon, Tyy = [VB.tok() for _ in range(11)]
            Ttm = VB.tok()
            TA5 = [VB.tok(), VB.tok()]
            TNP = [VB.tok() for _ in range(5)]
            TXc = [VB.tok(), VB.tok()]

            def shift_evac(pb, dst, Tdst, ti, segs):
                for (sq_, c0, n_) in segs:
                    sl = sq_ % 2
                    P.op("act", lambda e, c0=c0, n_=n_: e.activation(
                        out=dst[:, c0:c0 + n_], in_=ps[pb][:, c0:c0 + n_], func=AF.Identity, bias=epsc[:, 3:4], scale=omm[:, ti:ti + 1]),
                        reads=[Tps[pb], Trc, Teps], writes=[Tdst])
                    P.op("dve", lambda e, c0=c0, n_=n_: e.scalar_tensor_tensor(
                        out=dst[:, c0 + 1:c0 + n_], in0=ps[pb][:, c0:c0 + n_ - 1], scalar=vecs[:, omu + ti:omu + ti + 1],
                        in1=dst[:, c0 + 1:c0 + n_], op0=ALU.mult, op1=ALU.add), reads=[Tps[pb], Tvecs, Tdst], writes=[Tdst])
                    P.op("dve", lambda e, c0=c0, sl=sl: e.scalar_tensor_tensor(
                        out=dst[:, c0:c0 + 1], in0=shs[sl][:, ti:ti + 1], scalar=vecs[:, omu + ti:omu + ti + 1],
                        in1=dst[:, c0:c0 + 1], op0=ALU.mult, op1=ALU.add), reads=[Tsh[sl], Tvecs, Tdst], writes=[Tdst])
                    P.op("act", lambda e, c0=c0, n_=n_, sl=sl: e.copy(out=shs[sl][:, ti:ti + 1], in_=ps[pb][:, c0 + n_ - 1:c0 + n_]),
                         reads=[Tps[pb]], writes=[Tsh[sl]])

            def proj128(grp, pb, T):
                b = load_wm("ab_in", grp, 128)
                for k in range(KT):
                    P.op("pe", lambda e, b=b, k=k: e.matmul(ps[pb][:, :T], lhsT=wmb[b][:, k, 0:128], rhs=hT[:, k, :T],
                                                            start=(k == 0), stop=(k == KT - 1)),
                         reads=[Twm[b], Th[k]], writes=[Tps[pb]])

            def rwkv(segs, T, chunks, R, mi):
                switch(VB, keep=Tayc)
                NS = T // R
                tri = cmask[:R, mi, :R]
                trirev = cmask[:R, mi + 1, :R]
                tristr = cmask[:R, 4 + mi // 2, :R]
                proj128(16, 0, T)
                shift_evac(0, b_wa, Twa, 12, segs)
                proj128(17, 1, T)
                shift_evac(1, b_gl, Tgl, 13, segs)
                ACTF(b_wa[0:64, :T], b_wa[0:64, :T], AF.Tanh, [Twa], [Twa])
                ACTF(b_gl[:, :T], b_gl[:, :T], AF.Sigmoid, [Tgl], [Tgl])
                for hp in range(4):
                    hc = slice(hp * 128, (hp + 1) * 128)
                    for (grp, dst, Td, ti, pb) in ((4 + hp, b_r, Tr_, hp, 0), (8 + hp, b_k, Tk_, 4 + hp, 1), (12 + hp, b_v, Tv_, 8 + hp, 0)):
                        proj128(grp, pb, T)
                        shift_evac(pb, dst, Td, ti, segs)
                    P.op("pe", lambda e, hc=hc: e.matmul(ps[5][:, :T], lhsT=wlo[64:128, hc], rhs=b_wa[64:128, :T], start=True, stop=True),
                         reads=[Trc, Twa], writes=[Tps[5]])
                    oa0 = voff["rwkv_a0"]
                    ACTF(b_a[:, :T], ps[5][:, :T], AF.Sigmoid, [Tps[5], Tvecs], [Ta_], bias=vecs[:, oa0 + hp:oa0 + hp + 1])
                    for s_ in range(NS):
                        P.op("pe", lambda e, s_=s_, hc=hc: e.matmul(ps[0][:R, 0:128], lhsT=b_wa[0:64, s_ * R:(s_ + 1) * R], rhs=wlo[0:64, hc],
                                                                     start=True, stop=True), reads=[Trc, Twa], writes=[Tps[0]])
                        TT("dve", b_sig[:R, s_, :], ps[0][:R, 0:128], w0row[:R, hc], ALU.add, [Tps[0], Trc], [Tsig])
                        ACTF(b_sig[:R, s_, :], b_sig[:R, s_, :], AF.Sigmoid, [Tsig], [Tsig])
                    for (pb, msk) in ((2, tri), (3, tristr), (4, trirev)):
                        for s_ in range(NS):
                            P.op("pe", lambda e, s_=s_, pb=pb, msk=msk: e.matmul(
                                ps[pb][:, s_ * R:(s_ + 1) * R], lhsT=b_sig[:R, s_, :], rhs=msk, start=True, stop=True),
                                reads=[Tsig, Tcm], writes=[Tps[pb]])
                    ACTF(b_eW[:, :T], ps[2][:, :T], AF.Exp, [Tps[2]], [TeW], scale=-LAM)
                    ACTF(b_e[0][:, :T], ps[3][:, :T], AF.Exp, [Tps[3]], [Te_[0]], scale=-LAM)
                    okk = voff["rwkv_k_k"]
                    TSC("dve", b_kk[:, :T], b_k[:, :T], vecs[:, okk + hp:okk + hp + 1], None, ALU.mult, None, [Tk_, Tvecs], [Tkk])
                    TT("pool", b_tmp[:, :T], b_kk[:, :T], b_kk[:, :T], ALU.mult, [Tkk], [Ttmp])
                    P.op("pe", lambda e: e.matmul(ps[5][:, :T], lhsT=blk64[:], rhs=b_tmp[:, :T], start=True, stop=True),
                         reads=[Trc, Ttmp], writes=[Tps[5]])
                    ACTF(b_tmp[:, :T], ps[5][:, :T], AF.Ln, [Tps[5]], [Ttmp])
                    ACTF(b_tmp[:, :T], b_tmp[:, :T], AF.Exp, [Ttmp], [Ttmp], scale=-0.5)
                    TT("dve", b_kk[:, :T], b_kk[:, :T], b_tmp[:, :T], ALU.mult, [Tkk, Ttmp], [Tkk])
                    P.op("dve", lambda e: e.scalar_tensor_tensor(out=b_AR[:, 0, :T], in0=b_kk[:, :T], scalar=-1.0, in1=b_e[0][:, :T],
                                                                 op0=ALU.mult, op1=ALU.mult), reads=[Tkk, Te_[0]], writes=[TAR])
                    TT("dve", b_AR[:, 1, :T], b_r[:, :T], b_eW[:, :T], ALU.mult, [Tr_, TeW], [TAR])
                    TSC("dve", b_kp[:, :T], b_a[:, :T], vecs[:, oka + hp:oka + hp + 1], omka[:, hp:hp + 1], ALU.mult, ALU.add, [Ta_, Tvecs, Trc], [Tkp])
                    TT("dve", b_kp[:, :T], b_kp[:, :T], b_k[:, :T], ALU.mult, [Tkp, Tk_], [Tkp])
                    TT("pool", b_kb[:, :T], b_kk[:, :T], b_a[:, :T], ALU.mult, [Tkk, Ta_], [Tkb])
                    ACTF(b_e[1][:, :T], ps[2][:, :T], AF.Exp, [Tps[2]], [Te_[1]], scale=LAM)
                    TT("dve", b_Bh[:, :T], b_kb[:, :T], b_e[1][:, :T], ALU.mult, [Tkb, Te_[1]], [TBh])
                    TT("pool", b_Kh[:, :T], b_kp[:, :T], b_e[1][:, :T], ALU.mult, [Tkp, Te_[1]], [TKh])
                    ACTF(b_e[0][:, :T], ps[4][:, :T], AF.Exp, [Tps[4]], [Te_[0]], scale=-LAM)
                    TT("dve", b_Bc[:, :T], b_kb[:, :T], b_e[0][:, :T], ALU.mult, [Tkb, Te_[0]], [TBc])
                    TT("pool", b_Kc[:, :T], b_kp[:, :T], b_e[0][:, :T], ALU.mult, [Tkp, Te_[0]], [TKc])
                    ork = voff["rwkv_r_k"]
                    P.op("dve", lambda e, hp=hp: e.scalar_tensor_tensor(out=b_tmp[:, :T], in0=b_r[:, :T], scalar=vecs[:, ork + hp:ork + hp + 1],
                                                                 in1=b_kp[:, :T], op0=ALU.mult, op1=ALU.mult), reads=[Tr_, Tvecs, Tkp, Ttmp], writes=[Ttmp])
                    P.op("pe", lambda e: e.matmul(ps[6][:, :T], lhsT=blk64[:], rhs=b_tmp[:, :T], start=True, stop=True),
                         reads=[Trc, Ttmp], writes=[Tps[6]])
                    TT("dve", b_bon[:, :T], b_v[:, :T], ps[6][:, :T], ALU.mult, [Tv_, Tps[6]], [Tbon])
                    P.op("pe", lambda e, hc=hc: e.matmul(ps[1][:, :T], lhsT=wg2[:, hc], rhs=b_gl[:, :T], start=True, stop=True),
                         reads=[Trc, Tgl], writes=[Tps[1]])
                    for (sl, c0, n) in chunks:
                        cc = slice(c0, c0 + n)
                        ST = STz[sl]
                        for q, (src, Ts_) in enumerate(((b_v, Tv_), (b_Bc, TBc), (b_Kc, TKc))):
                            P.op("pe", lambda e, q=q, src=src, cc=cc: e.transpose(ps[0][:n, q * 128:(q + 1) * 128], src[:, cc], ident[:, :]),
                                 reads=[Ts_, Tident], writes=[Tps[0]])
                        P.op("act", lambda e: e.copy(out=c_tm[:n, :], in_=ps[0][:n, 0:384]), reads=[Tps[0]], writes=[Ttm])
                        for hl in range(2):
                            pr = slice(64 * hl, 64 * hl + 64)
                            pa = ps[2 + hl]
                            P.op("pe", lambda e, pr=pr, pa=pa, cc=cc: e.matmul(
                                pa[:n, 0:2 * n], lhsT=b_Bh[pr, cc], rhs=b_AR[pr, :, cc], start=True, stop=True),
                                reads=[TBh, TAR], writes=[Tps[2 + hl]])
                            P.op("pe", lambda e, pr=pr, pa=pa, cc=cc: e.matmul(
                                pa[:n, 2 * n:4 * n], lhsT=b_Kh[pr, cc], rhs=b_AR[pr, :, cc], start=True, stop=True),
                                reads=[TKh, TAR], writes=[Tps[2 + hl]])
                            P.op("pe", lambda e, pr=pr, pa=pa, cc=cc: e.matmul(
                                pa[:n, 4 * n:5 * n], lhsT=b_AR[pr, 0, cc], rhs=b_Bh[pr, cc], start=True, stop=True),
                                reads=[TBh, TAR], writes=[Tps[2 + hl]])
                            P.op("dve", lambda e, hl=hl, pa=pa: e.tensor_tensor(
                                out=c_A5[hl][:n, :, :n], in0=pa[:n, 0:5 * n].rearrange("p (q t) -> p q t", q=5),
                                in1=m5[:n, :, :n], op=ALU.mult), reads=[Tps[2 + hl], Trc], writes=[TA5[hl]])
                        for lev in range(5):
                            for hl in range(2):
                                if lev == 0:
                                    Nn, NTn = c_A5[hl][:n, 4, :n], c_A5[hl][:n, 0, :n]
                                    rd = [TA5[hl]]
                                else:
                                    Nn, NTn = c_NP[:n, lev - 1, 2 * hl, :n], c_NP[:n, lev - 1, 2 * hl + 1, :n]
                                    rd = [TNP[lev - 1]]
                                P.op("pe", lambda e, hl=hl, Nn=Nn, NTn=NTn: e.matmul(
                                    ps[4][:n, (2 * hl) * n:(2 * hl + 1) * n], lhsT=NTn, rhs=Nn, start=True, stop=True), reads=rd, writes=[Tps[4]])
                                P.op("pe", lambda e, hl=hl, Nn=Nn, NTn=NTn: e.matmul(
                                    ps[4][:n, (2 * hl + 1) * n:(2 * hl + 2) * n], lhsT=Nn, rhs=NTn, start=True, stop=True), reads=rd, writes=[Tps[4]])
                            P.op("act", lambda e, lev=lev: e.copy(out=c_NP[:n, lev, :, :n], in_=ps[4][:n, 0:4 * n].rearrange("p (q t) -> p q t", q=4)),
                                 reads=[Tps[4]], writes=[TNP[lev]])
                        for hl in range(2):
                            h = 2 * hp + hl
                            P.op("pe", lambda e, hl=hl, h=h, cc=cc, ST=ST: e.matmul(
                                ps[5][:n, hl * 64:(hl + 1) * 64], lhsT=b_AR[:, 0, cc], rhs=ST[:, h, :], start=True, stop=False),
                                reads=[TAR, TST[sl]], writes=[Tps[5]])
                            P.op("pe", lambda e, hl=hl: e.matmul(
                                ps[5][:n, hl * 64:(hl + 1) * 64], lhsT=c_A5[hl][:n, 2, :n], rhs=c_tm[:n, hl * 64:(hl + 1) * 64], start=False, stop=True),
                                reads=[TA5[hl], Ttm], writes=[Tps[5]])
                        P.op("act", lambda e: e.copy(out=c_X[0][:n, :, :], in_=ps[5][:n, 0:128].rearrange("p (h v) -> p h v", h=2)),
                             reads=[Tps[5]], writes=[TXc[0]])
                        xi = 0
                        for lev in range(6):
                            for hl in range(2):
                                if lev == 0:
                                    NTn, rd = c_A5[hl][:n, 0, :n], [TA5[hl]]
                                else:
                                    NTn, rd = c_NP[:n, lev - 1, 2 * hl + 1, :n], [TNP[lev - 1]]
                                P.op("pe", lambda e, hl=hl, NTn=NTn, xi=xi: e.matmul(
                                    ps[5][:n, hl * 64:(hl + 1) * 64], lhsT=NTn, rhs=c_X[xi][:n, hl, :], start=True, stop=True),
                                    reads=rd + [TXc[xi]], writes=[Tps[5]])
                            P.op("dve", lambda e, xi=xi: e.tensor_tensor(
                                out=c_X[1 - xi][:n, :, :], in0=c_X[xi][:n, :, :], in1=ps[5][:n, 0:128].rearrange("p (h v) -> p h v", h=2),
                                op=ALU.add), reads=[TXc[xi], Tps[5]], writes=[TXc[1 - xi]])
                            xi = 1 - xi
                        U = c_X[xi]
                        TU = TXc[xi]
                        for hl in range(2):
                            h = 2 * hp + hl
                            pr = slice(64 * hl, 64 * hl + 64)
                            P.op("pe", lambda e, pr=pr, h=h, cc=cc, ST=ST: e.matmul(
                                ps[6][pr, 0:n], lhsT=ST[:, h, :], rhs=b_AR[:, 1, cc], start=True, stop=False),
                                reads=[TAR, TST[sl]], writes=[Tps[6]])
                            P.op("pe", lambda e, pr=pr, hl=hl, U=U: e.matmul(
                                ps[6][pr, 0:n], lhsT=U[:n, hl, :], rhs=c_A5[hl][:n, 1, :n], start=False, stop=False),
                                reads=[TA5[hl], TU], writes=[Tps[6]])
                            P.op("pe", lambda e, pr=pr, hl=hl: e.matmul(
                                ps[6][pr, 0:n], lhsT=c_tm[:n, hl * 64:(hl + 1) * 64], rhs=c_A5[hl][:n, 3, :n], start=False, stop=True),
                                reads=[TA5[hl], Ttm], writes=[Tps[6]])
                        P.op("act", lambda e, cc=cc: e.copy(out=b_y[:, cc], in_=ps[6][:, 0:n]), reads=[Tps[6]], writes=[Tyy])
                        for hl in range(2):
                            h = 2 * hp + hl
                            pr = slice(64 * hl, 64 * hl + 64)
                            P.op("pe", lambda e, pr=pr, hl=hl, U=U: e.matmul(
                                ps[7][pr, 0:64], lhsT=c_tm[:n, 128 + hl * 64:128 + (hl + 1) * 64], rhs=U[:n, hl, :], start=True, stop=False),
                                reads=[Ttm, TU], writes=[Tps[7]])
                            P.op("pe", lambda e, pr=pr, hl=hl: e.matmul(
                                ps[7][pr, 0:64], lhsT=c_tm[:n, 256 + hl * 64:256 + (hl + 1) * 64], rhs=c_tm[:n, hl * 64:(hl + 1) * 64], start=False, stop=True),
                                reads=[Ttm], writes=[Tps[7]])
                            cl = c0 + n - 1
                            P.op("dve", lambda e, pr=pr, h=h, cl=cl, ST=ST: e.scalar_tensor_tensor(
                                out=ST[pr, h, :], in0=ST[pr, h, :], scalar=b_eW[pr, cl:cl + 1], in1=ps[7][pr, 0:64],
                                op0=ALU.mult, op1=ALU.add), reads=[TST[sl], TeW, Tps[7]], writes=[TST[sl]])
                    P.op("pe", lambda e: e.matmul(ps[5][:, :T], lhsT=blk64[:], rhs=b_y[:, :T], start=True, stop=True),
                         reads=[Trc, Tyy], writes=[Tps[5]])
                    P.op("dve", lambda e: e.scalar_tensor_tensor(out=b_y[:, :T], in0=ps[5][:, :T], scalar=-1.0 / 64, in1=b_y[:, :T],
                                                                 op0=ALU.mult, op1=ALU.add), reads=[Tps[5], Tyy], writes=[Tyy])
                    TT("pool", b_tmp[:, :T], b_y[:, :T], b_y[:, :T], ALU.mult, [Tyy, Ttmp], [Ttmp])
                    P.op("pe", lambda e: e.matmul(ps[6][:, :T], lhsT=blk64[:], rhs=b_tmp[:, :T], start=True, stop=True),
                         reads=[Trc, Ttmp], writes=[Tps[6]])
                    ACTF(b_tmp[:, :T], ps[6][:, :T], AF.Ln, [Tps[6], Teps], [Ttmp], bias=epsc[:, 1:2], scale=1.0 / 64)
                    ACTF(b_tmp[:, :T], b_tmp[:, :T], AF.Exp, [Ttmp], [Ttmp], scale=-0.5)
                    olg, olb = voff["rwkv_lnx_g"], voff["rwkv_lnx_b"]
                    P.op("dve", lambda e, hp=hp: e.scalar_tensor_tensor(out=b_y[:, :T], in0=b_y[:, :T], scalar=vecs[:, olg + hp:olg + hp + 1],
                                                                 in1=b_tmp[:, :T], op0=ALU.mult, op1=ALU.mult), reads=[Tyy, Tvecs, Ttmp], writes=[Tyy])
                    P.op("dve", lambda e, hp=hp: e.scalar_tensor_tensor(out=b_y[:, :T], in0=b_y[:, :T], scalar=vecs[:, olb + hp:olb + hp + 1],
                                                                 in1=b_bon[:, :T], op0=ALU.add, op1=ALU.add), reads=[Tyy, Tvecs, Tbon], writes=[Tyy])
                    TT("dve", a_yc[:, 4 + hp, :T], b_y[:, :T], ps[1][:, :T], ALU.mult, [Tyy, Tps[1]], [Tayc[4 + hp]])

        dbg_done = []
        S_dbg = P.slot("dbg")
        if cfg.get("dbg"):
            dbg_d = nc.dram_tensor("dbg", [128, 8, TOK], BF16, kind="ExternalOutput").ap()

        def mixer0(segs, T, frames, kind):
            switch(VA)
            if "s5" in mix:
                s5_proj(T)
                for fi, (xi, c0, n) in enumerate(frames):
                    s5_frame(xi, c0, n, fi)
                s5_post(T)
            else:
                P.op("dve", lambda e: e.memset(a_yc[:, 0:4, :T], 0.0), writes=Tayc[0:4])
            if "rwkv" in mix:
                rwkv(segs, T, frames, min(128, T), 0 if kind == "p" else 2)
                if cfg.get("dbg") and not dbg_done:
                    dbg_done.append(1)
                    P.op("pool", lambda e: e.dma_start(out=dbg_d[:, :, :], in_=a_yc[:, :, :]), reads=Tayc, slot=S_dbg)
            else:
                P.op("dve", lambda e: e.memset(a_yc[:, 4:8, :T], 0.0), writes=Tayc[4:8])
            out_proj("ab_out", 0, a_yc, Tayc, segs, T)

        tiles = []
        for s in range(2):
            for t0 in range(0, SEQ, TOK):
                tiles.append(("p", [(s, 0, TOK)], TOK, t0))
        tiles.append(("s", [(2, 0, DSEQ), (3, DSEQ, DSEQ)], 2 * DSEQ, 0))

        for (kind, segs, T, t0) in tiles:
            if kind == "p":
                s = segs[0][0]
                P.op("sp", lambda e, s=s, t0=t0: e.dma_start(
                    out=xT[:, :, :TOK], in_=xT_d[s, :, t0:t0 + TOK].rearrange("(k p) t -> p k t", p=128)),
                    writes=Tx, slot=S_x[0])
                R, CB, mi = min(128, TOK), min(128, TOK) // 2, 0
                blocks = [(s, c) for c in range(0, TOK, CB)]
                if t0 == 0 and "hgrn" in mix:
                    P.op("dve", lambda e, s=s: e.memset(Sg[s][:], 0.0), writes=[TS[s]])
                if t0 == 0 and "s5" in mix:
                    P.op("dve", lambda e, s=s: e.memset(Xs[s][:], 0.0), writes=[TXs[s]])
                if t0 == 0 and "rwkv" in mix:
                    P.op("dve", lambda e, s=s: e.memset(STz[s][:], 0.0), writes=[TST[s]])
                    P.op("dve", lambda e, s=s: e.memset(shs[s][:], 0.0), writes=[Tsh[s]])
                frames = [(s, c, min(64, TOK)) for c in range(0, TOK, min(64, TOK))]
            else:
                for si, (s, c0, n_) in enumerate(segs):
                    P.op("sp", lambda e, s=s, c0=c0, n_=n_: e.dma_start(
                        out=xT[:, :, c0:c0 + n_], in_=xsT_d[s - 2, :, :].rearrange("(k p) t -> p k t", p=128)),
                        writes=Tx, slot=S_x[si])
                R, CB, mi = 2 * DSEQ, DSEQ, 2
                blocks = [(0, 0), (1, DSEQ)]
                if "hgrn" in mix:
                    for i in range(2):
                        P.op("sp", lambda e, i=i: e.dma_start(out=Sg[i][:], in_=hg_in_d[i]), writes=[TS[i]], slot=S_st[i])
                if "s5" in mix:
                    for i in range(2):
                        P.op("sp", lambda e, i=i: e.dma_start(out=Xs[i][:], in_=s5x_d[i]), writes=[TXs[i]], slot=S_sx[i])
                if "rwkv" in mix:
                    for i in range(2):
                        P.op("sp", lambda e, i=i: e.dma_start(out=STz[i][:], in_=rw_st_d[i]), writes=[TST[i]], slot=S_rst[i])
                        P.op("sp", lambda e, i=i: e.dma_start(out=shs[i][:], in_=rw_sh_d[i]), writes=[Tsh[i]], slot=S_rst[2 + i])
                frames = [(0, 0, DSEQ), (1, DSEQ, DSEQ)]
            for l in range(2):
                ffn(l, 0, segs, T)
                if l == 0 and ("s5" in mix or "rwkv" in mix):
                    norm_mod(l, 1, segs, T)
                    mixer0(segs, T, frames, kind)
                if l == 1 and "hgrn" in mix:
                    norm_mod(l, 1, segs, T)
                    hgrn(segs, T, blocks, R, CB, mi)
                ffn(l, 1, segs, T)
            norm_mod(0, 0, segs, T, final=True)
            if kind == "p":
                s = segs[0][0]
                P.op("pool", lambda e, s=s, t0=t0: e.dma_start(
                    out=yT_d[s, :, t0:t0 + TOK].rearrange("(k p) t -> p k t", p=128), in_=yT[:, :, :TOK]),
                    reads=Ty, slot=S_y[0])
                if t0 + TOK >= SEQ and "hgrn" in mix:
                    P.op("pool", lambda e, s=s: e.dma_start(out=hg_out_d[s], in_=Sg[s][:]), reads=[TS[s]], slot=S_so[s])
                if t0 + TOK >= SEQ and "s5" in mix:
                    P.op("pool", lambda e, s=s: e.dma_start(out=s5o_d[s], in_=Xs[s][:]), reads=[TXs[s]], slot=S_s5o[s])
                if t0 + TOK >= SEQ and "rwkv" in mix:
                    P.op("pool", lambda e, s=s: e.dma_start(out=rw_sto_d[s], in_=STz[s][:]), reads=[TST[s]], slot=S_rso[s])
                    P.op("pool", lambda e, s=s: e.dma_start(out=rw_sho_d[s], in_=shs[s][:]), reads=[Tsh[s]], slot=S_rso[4 + s])
            else:
                for si, (s, c0, n_) in enumerate(segs):
                    P.op("pool", lambda e, s=s, c0=c0, n_=n_: e.dma_start(
                        out=ysT_d[s - 2, :, :].rearrange("(k p) t -> p k t", p=128), in_=yT[:, :, c0:c0 + n_]),
                        reads=Ty, slot=S_y[si])
                if "hgrn" in mix:
                    for i in range(2):
                        P.op("pool", lambda e, i=i: e.dma_start(out=hg_out_d[2 + i], in_=Sg[i][:]), reads=[TS[i]], slot=S_so[2 + i])
                if "s5" in mix:
                    for i in range(2):
                        P.op("pool", lambda e, i=i: e.dma_start(out=s5o_d[2 + i], in_=Xs[i][:]), reads=[TXs[i]], slot=S_s5o[2 + i])
                if "rwkv" in mix:
                    for i in range(2):
                        P.op("pool", lambda e, i=i: e.dma_start(out=rw_sto_d[2 + i], in_=STz[i][:]), reads=[TST[i]], slot=S_rso[2 + i])
                        P.op("pool", lambda e, i=i: e.dma_start(out=rw_sho_d[2 + i], in_=shs[i][:]), reads=[Tsh[i]], slot=S_rso[6 + i])

        sems = {e: es.enter_context(nc.semaphore("sem_" + e)) for e in ("pe", "act", "dve", "pool")}
        for s in P.slots:
            s.sem = es.enter_context(nc.semaphore("dsem_" + s.name))
        with nc.Block() as block:
            P.emit(block, sems, final_wait_slots={"pool": S_y + [S_pre] + S_so + S_s5o + S_rso + [S_dbg], "sp": []})
    return nc


def fm(v):
    v = np.asarray(v, np.float32)
    return np.ascontiguousarray(v.reshape(-1, 128).T)


def tile_kc(w, ncol):
    w = np.asarray(w, np.float32)
    K, N = w.shape
    return np.ascontiguousarray(w.reshape(K // 128, 128, N // ncol, ncol).transpose(2, 1, 0, 3))


def const_masks():
    cm = np.zeros((128, 6, 128), np.float32)
    for (mi, n, cb) in ((0, 128, 64), (2, 64, 32)):
        i = np.arange(n)
        same = (i[:, None] // cb) == (i[None, :] // cb)
        cm[:n, mi, :n] = (same & (i[:, None] <= i[None, :])).astype(np.float32)
        cm[:n, mi + 1, :n] = (same & (i[:, None] > i[None, :])).astype(np.float32)
        cm[:n, 4 + mi // 2, :n] = (same & (i[:, None] < i[None, :])).astype(np.float32)
    return cm


def prep_shared(inp, mix):
    sh = {}
    sh["w_ada"] = np.ascontiguousarray(inp["w_ada"], np.float32)
    voff, NV = vec_layout()
    vecs = np.zeros((128, NV), np.float32)
    for l in range(2):
        for nm in ("norm_ffn1", "norm_mix", "norm_ffn2"):
            vecs[:, voff[f"{nm}{l}"]:voff[f"{nm}{l}"] + 8] = fm(inp[nm][l])
        vecs[:, voff[f"b_ada{l}"]:voff[f"b_ada{l}"] + 72] = fm(inp["b_ada"][l])
    vecs[:, voff["final_norm"]:voff["final_norm"] + 8] = fm(inp["final_norm"])
    vecs[:, voff["hgrn_gnorm"]:voff["hgrn_gnorm"] + 1] = fm(inp["hgrn_gnorm"][0])
    sh["vecs"] = vecs
    sh["ident"] = np.eye(128, dtype=np.float32)
    sh["cmask"] = const_masks()
    ffw = ((inp["ffn1_w1"], inp["ffn1_w3"], inp["ffn1_w2"]), (inp["ffn2_w1"], inp["ffn2_w3"], inp["ffn2_w2"]))
    for l in range(2):
        for fi in range(2):
            sh[f"w_f{l}{fi}w1"] = tile_kc(ffw[fi][0][l], 256)
            sh[f"w_f{l}{fi}w3"] = tile_kc(ffw[fi][1][l], 256)
            w2 = np.asarray(ffw[fi][2][l], np.float32)
            sh[f"w_f{l}{fi}w2"] = np.ascontiguousarray(w2.reshape(11, 2, 128, 2, 512).transpose(3, 0, 2, 1, 4))
    vecs[:, voff["s5_d"]:voff["s5_d"] + 4] = fm(np.asarray(inp["s5_d"][0]).reshape(-1))
    vecs[:, voff["rwkv_mu"]:voff["rwkv_mu"] + 14] = fm(inp["rwkv_mu"][0])
    for nm in ("rwkv_a0", "rwkv_k_k", "rwkv_k_a", "rwkv_r_k", "rwkv_lnx_g", "rwkv_lnx_b"):
        vecs[:, voff[nm]:voff[nm] + 4] = fm(np.asarray(inp[nm][0]).reshape(-1))
    if "rwkv" in mix:
        sh["rw_lo"] = np.ascontiguousarray(np.concatenate([inp["rwkv_w_w2"][0], inp["rwkv_w_a2"][0]], 0), np.float32)
        sh["rw_g2"] = np.ascontiguousarray(inp["rwkv_w_g2"][0], np.float32)
        sh["rw_w0row"] = np.ascontiguousarray(np.broadcast_to(np.asarray(inp["rwkv_w0"][0], np.float32)[None], (128, 512)))
        bl = np.zeros((128, 128), np.float32)
        bl[0:64, 0:64] = 1.0
        bl[64:128, 64:128] = 1.0
        sh["blk64"] = bl
        i = np.arange(64)
        lt = (i[:, None] < i[None, :]).astype(np.float32)
        le = (i[:, None] <= i[None, :]).astype(np.float32)
        gt = (i[:, None] > i[None, :]).astype(np.float32)
        sh["m5"] = np.ascontiguousarray(np.stack([lt, le, lt, le, gt], 1))
    if "s5" in mix or "rwkv" in mix:
        sh["w_ab_in"] = tile_kc(inp["ab_w_in"][0], 128)
        sh["w_ab_out"] = tile_kc(inp["ab_w_out"][0], 512)
    if "s5" in mix:
        sh["w_glu"] = tile_kc(inp["s5_w_glu"][0], 512)
        gp = lambda a: np.asarray(a, np.float32).reshape(16, 2, 64, -1).transpose(1, 2, 0, 3).reshape(128, 16, -1)
        ldt = np.broadcast_to(np.asarray(inp["s5_log_dt"][0], np.float32)[:, None], (32, 64))
        sh["s5p"] = np.ascontiguousarray(np.stack([gp(inp["s5_lam_re"][0])[:, :, 0], gp(inp["s5_lam_im"][0])[:, :, 0], gp(ldt)[:, :, 0]], 1))
        sh["s5b"] = np.ascontiguousarray(np.stack([gp(inp["s5_b_re"][0]), gp(inp["s5_b_im"][0])], 1))
        cT = lambda a: np.asarray(a, np.float32).transpose(0, 2, 1)
        sh["s5c"] = np.ascontiguousarray(np.stack([gp(cT(inp["s5_c_re"][0])), gp(cT(inp["s5_c_im"][0]))], 1))
        rmk = np.zeros((128, 4), np.float32)
        for jj in range(4):
            rmk[32 * jj:32 * jj + 32, jj] = 1.0
        sh["rowmask"] = rmk
    if "hgrn" in mix:
        sh["w_c_in"] = tile_kc(inp["c_w_in"][0], 256)
        sh["w_c_out"] = tile_kc(inp["c_w_out"][0], 512)
        sh["lbrow"] = np.ascontiguousarray(np.broadcast_to(
            np.asarray(inp["hgrn_lower_bounds"], np.float32)[None], (128, 2, 1024)))
    return sh


def prep_core(inp, c, mix):
    m = {}
    sl = slice(2 * c, 2 * c + 2)
    m["xT"] = np.ascontiguousarray(np.asarray(inp["x_prompt"][sl], np.float32).transpose(0, 2, 1))
    m["xsT"] = np.ascontiguousarray(np.asarray(inp["x_sample"][sl], np.float32).transpose(0, 2, 1))
    cc = np.concatenate([inp["c_prompt"][sl], inp["c_sample"][sl]], 0).astype(np.float32)
    m["cT"] = np.ascontiguousarray(cc.reshape(4, 8, 128).transpose(2, 1, 0))
    if "rwkv" in mix:
        st = np.asarray(inp["state_rwkv"][0, sl], np.float32)
        stz = np.zeros((2, 128, 8, 64), np.float32)
        for h in range(8):
            hl = h % 2
            stz[:, 64 * hl:64 * hl + 64, h, :] = st[:, h].transpose(0, 2, 1)
        m["rw_st"] = stz
        m["rw_sh"] = np.ascontiguousarray(np.stack([fm(inp["state_rwkv_shift"][0, b]) for b in range(2 * c, 2 * c + 2)], 0))
    if "s5" in mix:
        gp = lambda a: np.asarray(a, np.float32).reshape(16, 2, 64).transpose(1, 2, 0).reshape(128, 16)
        m["s5x"] = np.ascontiguousarray(np.stack([np.stack([gp(inp["state_s5_re"][0, b]), gp(inp["state_s5_im"][0, b])], 1)
                                                  for b in range(2 * c, 2 * c + 2)], 0))
    if "hgrn" in mix:
        m["hg_in"] = np.ascontiguousarray(np.asarray(inp["state_hgrn"][0, sl], np.float32).transpose(0, 2, 1, 3))
    return m


ALL_MIX = ("s5", "rwkv", "hgrn")


def run(inp, cfg, runner=None):
    mix = cfg.get("mix", ALL_MIX)
    nc = build(cfg)
    sh = prep_shared(inp, mix)
    in_maps = []
    for c in range(NCORES):
        m = dict(sh)
        m.update(prep_core(inp, c, mix))
        in_maps.append(m)
    if runner is not None:
        return runner(nc, in_maps)
    res = run_bass_kernel_spmd(nc, in_maps, core_ids=list(range(NCORES)))
    return res.results


def assemble(results, cfg):
    mix = cfg.get("mix", ALL_MIX)
    f32 = lambda a: np.ascontiguousarray(a, np.float32)
    y_p = f32(np.concatenate([np.asarray(r["yT"]).transpose(0, 2, 1) for r in results], 0))
    y_s = f32(np.concatenate([np.asarray(r["ysT"]).transpose(0, 2, 1) for r in results], 0))
    nb = 2 * len(results)
    z = lambda *s: np.zeros(s, np.float32)
    s5_re_p, s5_im_p, s5_re_s, s5_im_s = z(1, nb, 32, 64), z(1, nb, 32, 64), z(1, nb, 32, 64), z(1, nb, 32, 64)
    rw_p, rw_s = z(1, nb, 8, 64, 64), z(1, nb, 8, 64, 64)
    sh_p, sh_s = z(1, nb, 1792), z(1, nb, 1792)
    hg_p, hg_s = z(1, nb, 8, 128, 128), z(1, nb, 8, 128, 128)
    if "s5" in mix:
        so = np.stack([np.asarray(r["s5o"]) for r in results], 0)
        so = so.reshape(len(results), 4, 2, 64, 2, 16).transpose(0, 1, 4, 5, 2, 3).reshape(len(results), 4, 2, 32, 64)
        s5_re_p, s5_im_p = f32(so[:, 0:2, 0].reshape(1, nb, 32, 64)), f32(so[:, 0:2, 1].reshape(1, nb, 32, 64))
        s5_re_s, s5_im_s = f32(so[:, 2:4, 0].reshape(1, nb, 32, 64)), f32(so[:, 2:4, 1].reshape(1, nb, 32, 64))
    if "rwkv" in mix:
        sto = np.stack([np.asarray(r["rw_sto"]) for r in results], 0)
        rw = np.zeros((len(results), 4, 8, 64, 64), np.float32)
        for h in range(8):
            hl = h % 2
            rw[:, :, h] = sto[:, :, 64 * hl:64 * hl + 64, h, :].transpose(0, 1, 3, 2)
        rw_p, rw_s = f32(rw[:, 0:2].reshape(1, nb, 8, 64, 64)), f32(rw[:, 2:4].reshape(1, nb, 8, 64, 64))
        sho = np.stack([np.asarray(r["rw_sho"]) for r in results], 0)
        sho = sho.transpose(0, 1, 3, 2).reshape(len(results), 4, 1792)
        sh_p, sh_s = f32(sho[:, 0:2].reshape(1, nb, 1792)), f32(sho[:, 2:4].reshape(1, nb, 1792))
    if "hgrn" in mix:
        hg = np.stack([np.asarray(r["hg_out"]) for r in results], 0)
        hg = hg.transpose(0, 1, 3, 2, 4)
        hg_p = f32(hg[:, 0:2].reshape(1, nb, 8, 128, 128))
        hg_s = f32(hg[:, 2:4].reshape(1, nb, 8, 128, 128))
    return (y_p, y_s, s5_re_p, s5_im_p, rw_p, sh_p, hg_p, s5_re_s, s5_im_s, rw_s, sh_s, hg_s)


def kernel(**inputs):
    cfg = {"SEQ": inputs["x_prompt"].shape[1], "DSEQ": inputs["x_sample"].shape[1]}
    results = run(inputs, cfg)
    return assemble(results, cfg)
```

```python
from contextlib import ExitStack
import numpy as np
import concourse.bass as bass
import concourse.mybir as mybir
from concourse.bass_utils import run_bass_kernel_spmd

F32 = mybir.dt.float32
BF16 = mybir.dt.bfloat16
AF = mybir.ActivationFunctionType
ALU = mybir.AluOpType

D = 1024
DFF = 2816
NFT = 22
KT = 8
NCORES = 8
NORM_EPS = 1e-6
GN_EPS = 64e-5


class Tok:
    __slots__ = ("w", "rs", "name")

    def __init__(self, name=""):
        self.w = None
        self.rs = {}
        self.name = name


class Ins:
    __slots__ = ("eng", "fn", "deps", "needs_inc", "evnum", "slot", "dmaval")

    def __init__(self, eng, fn, slot):
        self.eng = eng
        self.fn = fn
        self.slot = slot
        self.deps = ()
        self.needs_inc = False
        self.evnum = 0
        self.dmaval = 0


class Slot:
    __slots__ = ("count", "sem", "name")

    def __init__(self, name):
        self.count = 0
        self.sem = None
        self.name = name


class Prog:
    ENGS = ("pe", "act", "dve", "pool", "sp")

    def __init__(self):
        self.lists = {e: [] for e in self.ENGS}
        self.slots = []
        self.nins = 0

    def slot(self, name):
        s = Slot(name)
        self.slots.append(s)
        return s

    def op(self, eng, fn, reads=(), writes=(), slot=None):
        ins = Ins(eng, fn, slot)
        deps = {}
        for t in reads:
            if t.w is not None:
                deps[id(t.w)] = t.w
        for t in writes:
            if t.w is not None:
                deps[id(t.w)] = t.w
            for r in t.rs.values():
                deps[id(r)] = r
        dl = []
        for d in deps.values():
            if d is ins:
                continue
            if d.slot is None and slot is None and d.eng == "pe" and eng == "pe":
                continue
            d.needs_inc = True
            dl.append(d)
        ins.deps = dl
        if slot is not None:
            slot.count += 16
            ins.dmaval = slot.count
        for t in reads:
            key = eng if slot is None else ("d", id(ins))
            t.rs[key] = ins
        for t in writes:
            t.w = ins
            t.rs = {}
        self.lists[eng].append(ins)
        self.nins += 1
        return ins

    def emit(self, block, sems, final_wait_slots=()):
        for e in self.ENGS:
            n = 0
            for ins in self.lists[e]:
                if ins.slot is None and ins.needs_inc:
                    n += 1
                    ins.evnum = n

        def run(e, eng):
            seen = {}
            for ins in self.lists[e]:
                for d in ins.deps:
                    if d.slot is not None:
                        sem, val, key = d.slot.sem, d.dmaval, id(d.slot)
                    else:
                        sem, val, key = sems[d.eng], d.evnum, d.eng
                    if seen.get(key, 0) < val:
                        eng.wait_ge(sem, val)
                        seen[key] = val
                bi = ins.fn(eng)
                if ins.slot is not None:
                    bi.then_inc(ins.slot.sem, 16)
                elif ins.needs_inc:
                    bi.then_inc(sems[e], 1)
            for s in final_wait_slots.get(e, ()):
                if s.count > 0:
                    eng.wait_ge(s.sem, s.count)

        @block.sync
        def _(eng):
            run("sp", eng)

        @block.tensor
        def _(eng):
            run("pe", eng)

        @block.scalar
        def _(eng):
            run("act", eng)

        @block.vector
        def _(eng):
            run("dve", eng)

        @block.gpsimd
        def _(eng):
            run("pool", eng)


def alias(old, new):
    acc = {}
    for t in old:
        if t.w is not None:
            acc[("w", id(t.w))] = t.w
        for k, r in t.rs.items():
            acc[("r", id(r))] = r
    for t in new:
        t.w = None
        t.rs = dict(acc)


def vec_layout():
    off = {}
    n = 0

    def add(name, cols):
        nonlocal n
        off[name] = n
        n += cols

    for l in range(2):
        for nm in ("norm_ffn1", "norm_mix", "norm_ffn2"):
            add(f"{nm}{l}", 8)
        add(f"b_ada{l}", 72)
    add("final_norm", 8)
    add("hgrn_gnorm", 1)
    add("s5_d", 4)
    add("rwkv_mu", 14)
    for nm in ("rwkv_a0", "rwkv_k_k", "rwkv_k_a", "rwkv_r_k", "rwkv_lnx_g", "rwkv_lnx_b"):
        add(nm, 4)
    return off, n


class View:
    def __init__(self, arena, name):
        self.arena = arena
        self.off = 0
        self.toks = []
        self.name = name

    def alloc(self, shape, dt=F32):
        n = 1
        for d in shape[1:]:
            n *= d
        w = n if dt == F32 else (n + 1) // 2
        a = self.arena[0:shape[0], self.off:self.off + w]
        self.off += w
        assert self.off <= self.arena.shape[1], (self.name, self.off, self.arena.shape)
        if dt != F32:
            a = a.bitcast(dt)[:, 0:n]
        if len(shape) == 3:
            a = a.rearrange("p (a b) -> p a b", a=shape[1])
        elif len(shape) == 4:
            a = a.rearrange("p (a b c) -> p a b c", a=shape[1], b=shape[2])
        return a

    def tok(self, name=""):
        t = Tok(name)
        self.toks.append(t)
        return t


def build(cfg):
    SEQ = cfg["SEQ"]
    DSEQ = cfg.get("DSEQ", 32)
    TOK = min(512, SEQ)
    mix = cfg.get("mix", ("s5", "rwkv", "hgrn"))
    nc = bass.Bass("TRN2", target_bir_lowering=False)
    P = Prog()
    voff, NV = vec_layout()

    def din(name, shape, dt=F32):
        return nc.dram_tensor(name, list(shape), dt, kind="ExternalInput").ap()

    def dout(name, shape):
        return nc.dram_tensor(name, list(shape), F32, kind="ExternalOutput").ap()

    def dscr(name, shape, dt=BF16):
        return nc.dram_tensor(name, list(shape), dt, kind="Internal").ap()

    xT_d = din("xT", [2, D, SEQ])
    xsT_d = din("xsT", [2, D, DSEQ])
    cT_d = din("cT", [128, 8, 4])
    wada_d = din("w_ada", [2, D, 9 * D])
    vecs_d = din("vecs", [128, NV])
    ident_d = din("ident", [128, 128])
    cmask_d = din("cmask", [128, 6, 128])
    yT_d = dout("yT", [2, D, SEQ])
    ysT_d = dout("ysT", [2, D, DSEQ])
    wsrc = {}
    wscr = {}

    def wreg(key, name, shape):
        wsrc[key] = din("w_" + name, shape)
        wscr[key] = dscr("s_" + name, shape)

    for l in range(2):
        for fi in range(2):
            for nm in ("w1", "w3"):
                wreg((l, fi, nm), f"f{l}{fi}{nm}", [11, 128, 8, 256])
            wreg((l, fi, "w2"), f"f{l}{fi}w2", [2, 11, 128, 2, 512])
    if "hgrn" in mix:
        wreg("c_in", "c_in", [16, 128, 8, 256])
        wreg("c_out", "c_out", [2, 128, 8, 512])
        lbrow_d = din("lbrow", [128, 2, 1024])
        hg_in_d = din("hg_in", [2, 128, 8, 128])
        hg_out_d = dout("hg_out", [4, 128, 8, 128])

    if "s5" in mix or "rwkv" in mix:
        wreg("ab_in", "ab_in", [18, 128, 8, 128])
        wreg("ab_out", "ab_out", [2, 128, 8, 512])
    if "rwkv" in mix:
        rw_lo_d = din("rw_lo", [128, 512])
        rw_g2_d = din("rw_g2", [128, 512])
        rw_w0_d = din("rw_w0row", [128, 512])
        blk64_d = din("blk64", [128, 128])
        m5_d = din("m5", [64, 5, 64])
        rw_st_d = din("rw_st", [2, 128, 8, 64])
        rw_sh_d = din("rw_sh", [2, 128, 14])
        rw_sto_d = dout("rw_sto", [4, 128, 8, 64])
        rw_sho_d = dout("rw_sho", [4, 128, 14])
    if "s5" in mix:
        wreg("glu", "glu", [1, 128, 4, 512])
        s5p_d = din("s5p", [128, 3, 16])
        s5b_d = din("s5b", [128, 2, 16, 16])
        s5c_d = din("s5c", [128, 2, 16, 16])
        s5x_d = din("s5x", [2, 128, 2, 16])
        s5o_d = dout("s5o", [4, 128, 2, 16])
        rowmask_d = din("rowmask", [128, 4])

    es = ExitStack()
    with es:
        def sb(name, shape, dt=F32):
            return es.enter_context(nc.sbuf_tensor(name, list(shape), dt))

        xT = sb("xT_sb", [128, KT, TOK])
        hT = sb("hT_sb", [128, KT, TOK], BF16)
        rs1 = sb("rs1", [128, TOK])
        rstd = sb("rstd", [128, TOK])
        vecs = sb("vecs_sb", [128, NV])
        ident = sb("ident_sb", [128, 128])
        cmask = sb("cmask_sb", [128, 6, 128])
        onesb = sb("onesb", [128, 128], BF16)
        onesf = sb("onesf", [128, 128])
        epsc = sb("epsc", [128, 4])
        cs = sb("cs_sb", [128, 8, 4])
        modT = sb("modT", [128, 2, 72, 4])
        modA = sb("modA", [128, 2, 3, 8, 4])
        modG = sb("modG", [128, 2, 3, 8, 4])
        NW13 = 2
        w1b = [sb(f"w1b{i}", [128, 8, 256], BF16) for i in range(NW13)]
        w3b = [sb(f"w3b{i}", [128, 8, 256], BF16) for i in range(NW13)]
        NW2 = 3
        w2b = [sb(f"w2b{i}", [128, 2, 512], BF16) for i in range(NW2)]
        NWM = 2
        wmb = [sb(f"wmb{i}", [128, 8, 512], BF16) for i in range(NWM)]
        ps = [es.enter_context(nc.psum_tensor(f"ps{i}", [128, 512], F32)) for i in range(8)]
        NA = 15872
        arena = sb("arena", [128, NA])

        Tx = [Tok(f"x{k}") for k in range(KT)]
        Th = [Tok(f"h{k}") for k in range(KT)]
        Trs1, Trstd = Tok(), Tok()
        Tvecs, Tident, Tones, Teps, Tcs, Tcm = Tok(), Tok(), Tok(), Tok(), Tok(), Tok()
        TmodT, TmodA, TmodG = Tok(), Tok(), Tok()
        Tw13 = [Tok() for _ in range(NW13)]
        Tw2 = [Tok() for _ in range(NW2)]
        Twm = [Tok() for _ in range(NWM)]
        Tps = [Tok(f"ps{i}") for i in range(8)]
        Tscr = {k: Tok() for k in wscr}

        VI = View(arena, "init")
        NWA = 6
        wabuf = [VI.alloc([128, 512]) for _ in range(NWA)]
        Twab = [VI.tok() for _ in range(NWA)]
        modtm = VI.alloc([4, 512])
        Tmodtm = VI.tok()
        lbtmp = VI.alloc([128, 2, 1024])
        Tlbtmp = VI.tok()

        VF = View(arena, "ffn")
        sq = VF.alloc([128, KT, TOK], BF16)
        hid = VF.alloc([128, NFT, TOK], BF16)
        stmp = [VF.alloc([128, TOK]) for _ in range(2)]
        ntmp = [VF.alloc([128, TOK]) for _ in range(2)]
        yT = VF.alloc([128, KT, TOK])
        Tsqk = [VF.tok(f"sq{k}") for k in range(KT)]
        Thid = [VF.tok(f"hid{f}") for f in range(NFT)]
        Tstmp = [VF.tok(), VF.tok()]
        Tntmp = [VF.tok(), VF.tok()]
        Ty = [VF.tok() for _ in range(KT)]

        cur_view = [VI]

        def switch(view, keep=()):
            if cur_view[0] is not view:
                kk_ = set(id(t) for t in keep)
                alias(cur_view[0].toks, [t for t in view.toks if id(t) not in kk_])
                cur_view[0] = view

        S_const = [P.slot(f"const{i}") for i in range(8)]
        S_wab = [P.slot(f"wab{i}") for i in range(NWA)]
        S_w1 = [P.slot(f"w1_{i}") for i in range(NW13)]
        S_w3 = [P.slot(f"w3_{i}") for i in range(NW13)]
        S_w2 = [P.slot(f"w2_{i}") for i in range(NW2)]
        S_wm = [P.slot(f"wm_{i}") for i in range(NWM)]
        S_x = [P.slot("xin0"), P.slot("xin1")]
        S_y = [P.slot("yout0"), P.slot("yout1")]
        S_st = [P.slot(f"st{i}") for i in range(4)]
        S_so = [P.slot(f"so{i}") for i in range(4)]
        S_rm = P.slot("rm")
        S_rc = [P.slot(f"rc{i}") for i in range(5)]
        S_rst = [P.slot(f"rst{i}") for i in range(4)]
        S_rso = [P.slot(f"rso{i}") for i in range(8)]
        S_sx = [P.slot(f"sx{i}") for i in range(2)]
        S_s5o = [P.slot(f"s5o{i}") for i in range(4)]

        order = [(0, 0, "w1"), (0, 0, "w3"), (0, 0, "w2"), "ab_in", "glu", "ab_out", (0, 1, "w1"), (0, 1, "w3"), (0, 1, "w2"),
                 (1, 0, "w1"), (1, 0, "w3"), (1, 0, "w2"), "c_in", "c_out", (1, 1, "w1"), (1, 1, "w3"), (1, 1, "w2")]
        S_pre = []
        for key in order:
            if key not in wscr:
                continue
            src, dst = wsrc[key], wscr[key]
            sl_ = P.slot("pre%d" % len(S_pre))
            S_pre.append(sl_)
            if len(src.shape) == 5:
                for i in range(src.shape[0]):
                    P.op("pool", lambda e, s=src, d=dst, i=i: e.dma_start(out=d[i], in_=s[i]), writes=[Tscr[key]], slot=sl_)
            else:
                P.op("pool", lambda e, s=src, d=dst: e.dma_start(out=d[:], in_=s[:]), writes=[Tscr[key]], slot=sl_)

        P.op("sp", lambda e: e.dma_start(out=vecs[:], in_=vecs_d[:, :]), writes=[Tvecs], slot=S_const[0])
        P.op("sp", lambda e: e.dma_start(out=ident[:], in_=ident_d[:, :]), writes=[Tident], slot=S_const[1])
        P.op("sp", lambda e: e.dma_start(out=cs[:], in_=cT_d[:, :, :]), writes=[Tcs], slot=S_const[2])
        P.op("sp", lambda e: e.dma_start(out=cmask[:], in_=cmask_d[:, :, :]), writes=[Tcm], slot=S_const[3])
        P.op("dve", lambda e: e.memset(onesb[:], 1.0 / D), writes=[Tones])
        P.op("dve", lambda e: e.memset(onesf[:], 1.0 / 128), writes=[Tones])
        P.op("dve", lambda e: e.memset(epsc[:, 0:1], NORM_EPS), writes=[Teps])
        P.op("dve", lambda e: e.memset(epsc[:, 1:2], GN_EPS), writes=[Teps])
        P.op("dve", lambda e: e.memset(epsc[:, 2:3], 1.0), writes=[Teps])
        P.op("dve", lambda e: e.memset(epsc[:, 3:4], 0.0), writes=[Teps])

        P.op("act", lambda e: e.activation(out=cs[:], in_=cs[:], func=AF.Silu), reads=[Tcs], writes=[Tcs])
        wi = 0
        for l in range(2):
            for ch in range(18):
                for k in range(KT):
                    b = wi % NWA
                    wi += 1
                    P.op("sp", lambda e, l=l, ch=ch, k=k, b=b: e.dma_start(
                        out=wabuf[b][:], in_=wada_d[l, k * 128:(k + 1) * 128, ch * 512:(ch + 1) * 512]),
                        writes=[Twab[b]], slot=S_wab[b])
                    P.op("pe", lambda e, k=k, b=b: e.matmul(ps[0][0:4, :], lhsT=cs[:, k, :], rhs=wabuf[b][:],
                                                             start=(k == 0), stop=(k == KT - 1)),
                         reads=[Tcs, Twab[b]], writes=[Tps[0]])
                P.op("dve", lambda e: e.tensor_copy(out=modtm[:], in_=ps[0][0:4, :]), reads=[Tps[0]], writes=[Tmodtm])
                for j in range(4):
                    P.op("pe", lambda e, j=j: e.transpose(ps[1][:, j * 4:(j + 1) * 4], modtm[:, j * 128:(j + 1) * 128],
                                                           ident[0:4, 0:4]),
                         reads=[Tmodtm, Tident], writes=[Tps[1]])
                P.op("act", lambda e, l=l, ch=ch: e.copy(
                    out=modT[:, l, ch * 4:(ch + 1) * 4, :], in_=ps[1][:, 0:16].rearrange("p (j s) -> p j s", j=4)),
                    reads=[Tps[1]], writes=[TmodT])
        for l in range(2):
            o = voff[f"b_ada{l}"]
            P.op("dve", lambda e, l=l, o=o: e.tensor_tensor(
                out=modT[:, l, :, :], in0=modT[:, l, :, :],
                in1=vecs[:, o:o + 72].unsqueeze(2).to_broadcast([128, 72, 4]), op=ALU.add),
                reads=[TmodT, Tvecs], writes=[TmodT])
        for l in range(2):
            for n, nm in enumerate(("norm_ffn1", "norm_mix", "norm_ffn2")):
                o = voff[f"{nm}{l}"]
                P.op("dve", lambda e, l=l, n=n: e.tensor_scalar(
                    out=modA[:, l, n, :, :], in0=modT[:, l, (3 * n + 1) * 8:(3 * n + 2) * 8, :],
                    scalar1=1.0, scalar2=None, op0=ALU.add), reads=[TmodT], writes=[TmodA])
                P.op("dve", lambda e, l=l, n=n, o=o: e.tensor_tensor(
                    out=modA[:, l, n, :, :], in0=modA[:, l, n, :, :],
                    in1=vecs[:, o:o + 8].unsqueeze(2).to_broadcast([128, 8, 4]), op=ALU.mult),
                    reads=[TmodA, Tvecs], writes=[TmodA])
                cg = 1.0 if n == 1 else 0.5
                P.op("dve", lambda e, l=l, n=n, cg=cg: e.tensor_scalar(
                    out=modG[:, l, n, :, :], in0=modT[:, l, (3 * n + 2) * 8:(3 * n + 3) * 8, :],
                    scalar1=1.0, scalar2=cg, op0=ALU.add, op1=ALU.mult), reads=[TmodT], writes=[TmodG])

        def norm_mod(l, n, segs, T, final=False):
            switch(VF)
            for k in range(KT):
                if k % 2 == 0:
                    P.op("act", lambda e, k=k: e.activation(out=sq[:, k, :T], in_=xT[:, k, :T], func=AF.Square),
                         reads=[Tx[k]], writes=[Tsqk[k]])
                else:
                    P.op("dve", lambda e, k=k: e.tensor_tensor(out=sq[:, k, :T], in0=xT[:, k, :T], in1=xT[:, k, :T], op=ALU.mult),
                         reads=[Tx[k]], writes=[Tsqk[k]])
            for k in range(KT):
                P.op("pe", lambda e, k=k: e.matmul(ps[4][:, :T], lhsT=onesb[:], rhs=sq[:, k, :T],
                                                   start=(k == 0), stop=(k == KT - 1)),
                     reads=[Tsqk[k], Tones], writes=[Tps[4]])
            P.op("act", lambda e: e.activation(out=rs1[:, :T], in_=ps[4][:, :T], func=AF.Ln, bias=epsc[:, 0:1], scale=1.0),
                 reads=[Tps[4], Teps], writes=[Trs1])
            P.op("act", lambda e: e.activation(out=rstd[:, :T], in_=rs1[:, :T], func=AF.Exp, scale=-0.5),
                 reads=[Trs1], writes=[Trstd])
            for k in range(KT):
                for (sq_, c0, n_) in segs:
                    if final:
                        o = voff["final_norm"]
                        P.op("dve", lambda e, k=k, c0=c0, n_=n_, o=o: e.scalar_tensor_tensor(
                            out=yT[:, k, c0:c0 + n_], in0=xT[:, k, c0:c0 + n_], scalar=vecs[:, o + k:o + k + 1],
                            in1=rstd[:, c0:c0 + n_], op0=ALU.mult, op1=ALU.mult),
                            reads=[Tx[k], Trstd, Tvecs], writes=[Ty[k]])
                    else:
                        nb = k % 2
                        P.op("dve", lambda e, k=k, c0=c0, n_=n_, s=sq_, nb=nb: e.scalar_tensor_tensor(
                            out=ntmp[nb][:, c0:c0 + n_], in0=xT[:, k, c0:c0 + n_], scalar=modA[:, l, n, k, s:s + 1],
                            in1=rstd[:, c0:c0 + n_], op0=ALU.mult, op1=ALU.mult),
                            reads=[Tx[k], Trstd, TmodA], writes=[Tntmp[nb]])
                        P.op("act", lambda e, k=k, c0=c0, n_=n_, s=sq_, nb=nb: e.activation(
                            out=hT[:, k, c0:c0 + n_], in_=ntmp[nb][:, c0:c0 + n_], func=AF.Identity,
                            bias=modT[:, l, 3 * n * 8 + k, s:s + 1], scale=1.0),
                            reads=[Tntmp[nb], TmodT], writes=[Th[k]])

        cnt = {"w13": 0, "w2": 0, "wm": 0}

        def ffn(l, fi, segs, T):
            n = 0 if fi == 0 else 2
            norm_mod(l, n, segs, T)
            w1s, w3s, w2s = wscr[(l, fi, "w1")], wscr[(l, fi, "w3")], wscr[(l, fi, "w2")]
            t1, t3, t2 = Tscr[(l, fi, "w1")], Tscr[(l, fi, "w3")], Tscr[(l, fi, "w2")]
            for g in range(11):
                b = cnt["w13"] % NW13
                cnt["w13"] += 1
                P.op("sp", lambda e, g=g, b=b: e.dma_start(out=w1b[b][:], in_=w1s[g]), reads=[t1], writes=[Tw13[b]], slot=S_w1[b])
                P.op("sp", lambda e, g=g, b=b: e.dma_start(out=w3b[b][:], in_=w3s[g]), reads=[t3], writes=[Tw13[b]], slot=S_w3[b])
                for j in range(2):
                    ft = 2 * g + j
                    ia, ib = (2 * ft) % 4, (2 * ft + 1) % 4
                    for k in range(KT):
                        P.op("pe", lambda e, k=k, b=b, j=j, ia=ia: e.matmul(
                            ps[ia][:, :T], lhsT=w1b[b][:, k, j * 128:(j + 1) * 128], rhs=hT[:, k, :T],
                            start=(k == 0), stop=(k == KT - 1)), reads=[Tw13[b], Th[k]], writes=[Tps[ia]])
                    for k in range(KT):
                        P.op("pe", lambda e, k=k, b=b, j=j, ib=ib: e.matmul(
                            ps[ib][:, :T], lhsT=w3b[b][:, k, j * 128:(j + 1) * 128], rhs=hT[:, k, :T],
                            start=(k == 0), stop=(k == KT - 1)), reads=[Tw13[b], Th[k]], writes=[Tps[ib]])
                    sbi = ft % 2
                    P.op("act", lambda e, ia=ia, sbi=sbi: e.activation(out=stmp[sbi][:, :T], in_=ps[ia][:, :T], func=AF.Silu),
                         reads=[Tps[ia]], writes=[Tstmp[sbi]])
                    P.op("dve", lambda e, ib=ib, sbi=sbi, ft=ft: e.tensor_tensor(
                        out=hid[:, ft, :T], in0=stmp[sbi][:, :T], in1=ps[ib][:, :T], op=ALU.mult),
                        reads=[Tstmp[sbi], Tps[ib]], writes=[Thid[ft]])
            for hf in range(2):
                for c in range(11):
                    b = cnt["w2"] % NW2
                    cnt["w2"] += 1
                    P.op("sp", lambda e, hf=hf, c=c, b=b: e.dma_start(out=w2b[b][:], in_=w2s[hf, c]),
                         reads=[t2], writes=[Tw2[b]], slot=S_w2[b])
                    for j in range(2):
                        ft = 2 * c + j
                        for d4 in range(4):
                            P.op("pe", lambda e, b=b, j=j, d4=d4, ft=ft: e.matmul(
                                ps[4 + d4][:, :T], lhsT=w2b[b][:, j, d4 * 128:(d4 + 1) * 128], rhs=hid[:, ft, :T],
                                start=(ft == 0), stop=(ft == NFT - 1)), reads=[Tw2[b], Thid[ft]], writes=[Tps[4 + d4]])
                for d4 in range(4):
                    d = hf * 4 + d4
                    for (s, c0, n_) in segs:
                        P.op("dve", lambda e, d=d, d4=d4, s=s, c0=c0, n_=n_: e.scalar_tensor_tensor(
                            out=xT[:, d, c0:c0 + n_], in0=ps[4 + d4][:, c0:c0 + n_], scalar=modG[:, l, n, d, s:s + 1],
                            in1=xT[:, d, c0:c0 + n_], op0=ALU.mult, op1=ALU.add),
                            reads=[Tps[4 + d4], TmodG, Tx[d]], writes=[Tx[d]])

        def load_wm(key, idx, width=512):
            b = cnt["wm"] % NWM
            cnt["wm"] += 1
            src = wscr[key]
            P.op("sp", lambda e, b=b, idx=idx: e.dma_start(out=wmb[b][:, :, 0:width], in_=src[idx]),
                 reads=[Tscr[key]], writes=[Twm[b]], slot=S_wm[b])
            return b

        def out_proj(key, l, ycT, Tyc, segs, T):
            for hf in range(2):
                b = load_wm(key, hf)
                for d4 in range(4):
                    for k in range(KT):
                        P.op("pe", lambda e, b=b, d4=d4, k=k: e.matmul(
                            ps[4 + d4][:, :T], lhsT=wmb[b][:, k, d4 * 128:(d4 + 1) * 128], rhs=ycT[:, k, :T],
                            start=(k == 0), stop=(k == KT - 1)), reads=[Twm[b], Tyc[k]], writes=[Tps[4 + d4]])
                for d4 in range(4):
                    d = hf * 4 + d4
                    for (s, c0, n_) in segs:
                        P.op("dve", lambda e, d=d, d4=d4, s=s, c0=c0, n_=n_: e.scalar_tensor_tensor(
                            out=xT[:, d, c0:c0 + n_], in0=ps[4 + d4][:, c0:c0 + n_], scalar=modG[:, l, 1, d, s:s + 1],
                            in1=xT[:, d, c0:c0 + n_], op0=ALU.mult, op1=ALU.add),
                            reads=[Tps[4 + d4], TmodG, Tx[d]], writes=[Tx[d]])

        if "hgrn" in mix:
            omlrow = sb("omlrow", [128, 1024])
            Toml = Tok()
            Sg = [sb(f"Sg{i}", [128, 8, 128]) for i in range(2)]
            TS = [Tok(), Tok()]
            P.op("sp", lambda e: e.dma_start(out=lbtmp[:], in_=lbrow_d[:, :, :]), writes=[Tlbtmp], slot=S_const[4])
            P.op("dve", lambda e: e.tensor_tensor(out=lbtmp[:, 0, :], in0=lbtmp[:, 1, :], in1=lbtmp[:, 0, :], op=ALU.subtract),
                 reads=[Tlbtmp], writes=[Tlbtmp])
            P.op("act", lambda e: e.activation(out=omlrow[:], in_=lbtmp[:, 0, :], func=AF.Sigmoid, scale=-1.0),
                 reads=[Tlbtmp], writes=[Toml])
            VH = View(arena, "hgrn")
            NH = 2
            NHG = 8 // NH
            CW = NH * 128
            h_qT = VH.alloc([128, NH, TOK])
            h_sg = VH.alloc([128, NH, TOK])
            h_eb = VH.alloc([128, NH, TOK])
            h_ktT = VH.alloc([128, NH, TOK])
            NSM = max(1, TOK // 128)
            h_lf = VH.alloc([128, NSM, CW])
            h_k = VH.alloc([128, NSM, CW])
            h_v = VH.alloc([128, NSM, CW])
            h_att = [VH.alloc([128, NH, 128]) for _ in range(2)]
            h_e1 = VH.alloc([128, CW])
            h_e2 = VH.alloc([128, CW])
            h_osq = [VH.alloc([128, TOK]) for _ in range(2)]
            h_t1 = [VH.alloc([128, TOK]) for _ in range(2)]
            h_yc = VH.alloc([128, 8, TOK], BF16)
            TqT = [VH.tok() for _ in range(NH)]
            Tsg = [VH.tok() for _ in range(NH)]
            Teb = [VH.tok() for _ in range(NH)]
            TktT = [VH.tok() for _ in range(NH)]
            Tlf = [VH.tok() for _ in range(NSM)]
            Tk = [VH.tok() for _ in range(NSM)]
            Tv = [VH.tok() for _ in range(NSM)]
            Tatt = [VH.tok(), VH.tok()]
            Te1, Te2 = VH.tok(), VH.tok()
            Tosq = [VH.tok(), VH.tok()]
            Tt1 = [VH.tok(), VH.tok()]
            Tyc = [VH.tok() for _ in range(8)]

            def hgrn(segs, T, blocks, R, CB, mi):
                switch(VH)
                NS = T // R
                tri = cmask[:R, mi, :R]
                trirev = cmask[:R, mi + 1, :R]
                for HG in range(NHG):
                    for (grp, dst, Tdst) in ((HG, h_qT, TqT), (3 * NHG + HG, h_sg, Tsg)):
                        b = load_wm("c_in", grp, CW)
                        for hh in range(NH):
                            pb = hh % 2
                            for k in range(KT):
                                P.op("pe", lambda e, b=b, hh=hh, k=k, pb=pb: e.matmul(
                                    ps[pb][:, :T], lhsT=wmb[b][:, k, hh * 128:(hh + 1) * 128], rhs=hT[:, k, :T],
                                    start=(k == 0), stop=(k == KT - 1)), reads=[Twm[b], Th[k]], writes=[Tps[pb]])
                            P.op("act", lambda e, dst=dst, hh=hh, pb=pb: e.activation(
                                out=dst[:, hh, :T], in_=ps[pb][:, :T], func=AF.Silu),
                                reads=[Tps[pb]], writes=[Tdst[hh]])
                    b = load_wm("c_in", NHG + HG, CW)
                    for s in range(NS):
                        pb = 2 + s % 2
                        for k in range(KT):
                            P.op("pe", lambda e, b=b, s=s, k=k, pb=pb: e.matmul(
                                ps[pb][:R, :CW], lhsT=hT[:, k, s * R:(s + 1) * R], rhs=wmb[b][:, k, :CW],
                                start=(k == 0), stop=(k == KT - 1)), reads=[Twm[b], Th[k]], writes=[Tps[pb]])
                        P.op("act", lambda e, s=s, pb=pb: e.activation(
                            out=h_k[:R, s, :], in_=ps[pb][:R, :CW], func=AF.Sigmoid, scale=-1.0),
                            reads=[Tps[pb]], writes=[Tk[s]])
                        P.op("dve", lambda e, s=s, HG=HG: e.tensor_tensor(
                            out=h_k[:R, s, :], in0=h_k[:R, s, :], in1=omlrow[:R, HG * CW:(HG + 1) * CW], op=ALU.mult),
                            reads=[Tk[s], Toml], writes=[Tk[s]])
                        P.op("act", lambda e, s=s: e.activation(
                            out=h_lf[:R, s, :], in_=h_k[:R, s, :], func=AF.Ln, scale=-1.0, bias=epsc[:R, 2:3]),
                            reads=[Tk[s], Teps], writes=[Tlf[s]])
                    b = load_wm("c_in", 2 * NHG + HG, CW)
                    for s in range(NS):
                        pb = 2 + s % 2
                        for k in range(KT):
                            P.op("pe", lambda e, b=b, s=s, k=k, pb=pb: e.matmul(
                                ps[pb][:R, :CW], lhsT=hT[:, k, s * R:(s + 1) * R], rhs=wmb[b][:, k, :CW],
                                start=(k == 0), stop=(k == KT - 1)), reads=[Twm[b], Th[k]], writes=[Tps[pb]])
                        P.op("act", lambda e, s=s, pb=pb: e.copy(out=h_v[:R, s, :], in_=ps[pb][:R, :CW]),
                             reads=[Tps[pb]], writes=[Tv[s]])
                    for hh in range(NH):
                        pb = 4 + hh % 2
                        for s in range(NS):
                            P.op("pe", lambda e, hh=hh, s=s, pb=pb: e.matmul(
                                ps[pb][:, s * R:(s + 1) * R], lhsT=h_lf[:R, s, hh * 128:(hh + 1) * 128], rhs=tri,
                                start=True, stop=True), reads=[Tlf[s], Tcm], writes=[Tps[pb]])
                        P.op("act", lambda e, hh=hh, pb=pb: e.activation(out=h_eb[:, hh, :T], in_=ps[pb][:, :T], func=AF.Exp),
                             reads=[Tps[pb]], writes=[Teb[hh]])
                        P.op("dve", lambda e, hh=hh: e.tensor_tensor(
                            out=h_qT[:, hh, :T], in0=h_qT[:, hh, :T], in1=h_eb[:, hh, :T], op=ALU.mult),
                            reads=[TqT[hh], Teb[hh]], writes=[TqT[hh]])
                    for s in range(NS):
                        P.op("pe", lambda e, s=s: e.matmul(ps[6][:R, :CW], lhsT=tri, rhs=h_lf[:R, s, :], start=True, stop=True),
                             reads=[Tlf[s], Tcm], writes=[Tps[6]])
                        P.op("pe", lambda e, s=s: e.matmul(ps[7][:R, :CW], lhsT=trirev, rhs=h_lf[:R, s, :], start=True, stop=True),
                             reads=[Tlf[s], Tcm], writes=[Tps[7]])
                        P.op("act", lambda e: e.activation(out=h_e1[:R, :], in_=ps[6][:R, :CW], func=AF.Exp, scale=-1.0),
                             reads=[Tps[6]], writes=[Te1])
                        P.op("act", lambda e: e.activation(out=h_e2[:R, :], in_=ps[7][:R, :CW], func=AF.Exp),
                             reads=[Tps[7]], writes=[Te2])
                        P.op("dve", lambda e, s=s: e.tensor_tensor(out=h_lf[:R, s, :], in0=h_k[:R, s, :], in1=h_e1[:R, :], op=ALU.mult),
                             reads=[Tk[s], Te1], writes=[Tlf[s]])
                        P.op("dve", lambda e, s=s: e.tensor_tensor(out=h_k[:R, s, :], in0=h_k[:R, s, :], in1=h_e2[:R, :], op=ALU.mult),
                             reads=[Tk[s], Te2], writes=[Tk[s]])
                    for hh in range(NH):
                        pb = 4 + hh % 2
                        for s in range(NS):
                            P.op("pe", lambda e, hh=hh, s=s, pb=pb: e.transpose(
                                ps[pb][:, s * R:(s + 1) * R], h_lf[:R, s, hh * 128:(hh + 1) * 128], ident[:R, :R]),
                                reads=[Tlf[s], Tident], writes=[Tps[pb]])
                        P.op("act", lambda e, hh=hh, pb=pb: e.copy(out=h_ktT[:, hh, :T], in_=ps[pb][:, :T]),
                             reads=[Tps[pb]], writes=[TktT[hh]])
                    for s in range(NS):
                        pa = s % 2
                        ab = s % 2
                        for hh in range(NH):
                            P.op("pe", lambda e, hh=hh, s=s, pa=pa: e.matmul(
                                ps[pa][:R, hh * R:(hh + 1) * R], lhsT=h_ktT[:, hh, s * R:(s + 1) * R],
                                rhs=h_qT[:, hh, s * R:(s + 1) * R], start=True, stop=True),
                                reads=[TktT[hh], TqT[hh]], writes=[Tps[pa]])
                        P.op("dve", lambda e, pa=pa, ab=ab: e.tensor_tensor(
                            out=h_att[ab][:R, :, :R], in0=ps[pa][:R, 0:NH * R].rearrange("p (h t) -> p h t", h=NH),
                            in1=tri.unsqueeze(1).to_broadcast([R, NH, R]), op=ALU.mult),
                            reads=[Tps[pa], Tcm], writes=[Tatt[ab]])
                        po = 2 + s % 2
                        for hb in range(2):
                            sbuf_i, c0 = blocks[2 * s + hb]
                            Sb = Sg[sbuf_i]
                            for hh in range(NH):
                                h = HG * NH + hh
                                P.op("pe", lambda e, hh=hh, s=s, po=po, ab=ab, hb=hb: e.matmul(
                                    ps[po][:, hh * R + hb * CB:hh * R + (hb + 1) * CB], lhsT=h_v[:R, s, hh * 128:(hh + 1) * 128],
                                    rhs=h_att[ab][:R, hh, hb * CB:(hb + 1) * CB], start=True, stop=False),
                                    reads=[Tv[s], Tatt[ab]], writes=[Tps[po]])
                                P.op("pe", lambda e, hh=hh, h=h, po=po, hb=hb, c0=c0, Sb=Sb: e.matmul(
                                    ps[po][:, hh * R + hb * CB:hh * R + (hb + 1) * CB], lhsT=Sb[:, h, :],
                                    rhs=h_qT[:, hh, c0:c0 + CB], start=False, stop=True),
                                    reads=[TS[sbuf_i], TqT[hh]], writes=[Tps[po]])
                            pu = 6 + hb
                            for hh in range(NH):
                                P.op("pe", lambda e, hh=hh, s=s, hb=hb, pu=pu: e.matmul(
                                    ps[pu][:, hh * 128:(hh + 1) * 128],
                                    lhsT=h_k[hb * CB:(hb + 1) * CB, s, hh * 128:(hh + 1) * 128],
                                    rhs=h_v[hb * CB:(hb + 1) * CB, s, hh * 128:(hh + 1) * 128], start=True, stop=True),
                                    reads=[Tk[s], Tv[s]], writes=[Tps[pu]])
                            cl = c0 + CB - 1
                            P.op("dve", lambda e, HG=HG, cl=cl, Sb=Sb: e.tensor_tensor(
                                out=Sb[:, HG * NH:(HG + 1) * NH, :], in0=Sb[:, HG * NH:(HG + 1) * NH, :],
                                in1=h_eb[:, :, cl:cl + 1].to_broadcast([128, NH, 128]), op=ALU.mult),
                                reads=[TS[sbuf_i]] + Teb, writes=[TS[sbuf_i]])
                            P.op("dve", lambda e, HG=HG, pu=pu, Sb=Sb: e.tensor_tensor(
                                out=Sb[:, HG * NH:(HG + 1) * NH, :], in0=Sb[:, HG * NH:(HG + 1) * NH, :],
                                in1=ps[pu][:, :CW].rearrange("p (h v) -> p h v", h=NH), op=ALU.add),
                                reads=[TS[sbuf_i], Tps[pu]], writes=[TS[sbuf_i]])
                        P.op("act", lambda e, s=s, po=po: e.copy(
                            out=h_qT[:, :, s * R:(s + 1) * R], in_=ps[po][:, 0:NH * R].rearrange("p (h t) -> p h t", h=NH)),
                            reads=[Tps[po]], writes=TqT)
                    og = voff["hgrn_gnorm"]
                    for hh in range(NH):
                        h = HG * NH + hh
                        i2 = hh % 2
                        P.op("act", lambda e, hh=hh, i2=i2: e.activation(out=h_osq[i2][:, :T], in_=h_qT[:, hh, :T], func=AF.Square),
                             reads=[TqT[hh]], writes=[Tosq[i2]])
                        pb = 4 + hh % 2
                        P.op("pe", lambda e, i2=i2, pb=pb: e.matmul(ps[pb][:, :T], lhsT=onesf[:], rhs=h_osq[i2][:, :T], start=True, stop=True),
                             reads=[Tosq[i2], Tones], writes=[Tps[pb]])
                        P.op("act", lambda e, i2=i2, pb=pb: e.activation(out=h_osq[i2][:, :T], in_=ps[pb][:, :T], func=AF.Ln, bias=epsc[:, 0:1], scale=1.0),
                             reads=[Tps[pb], Teps], writes=[Tosq[i2]])
                        P.op("act", lambda e, i2=i2: e.activation(out=h_osq[i2][:, :T], in_=h_osq[i2][:, :T], func=AF.Exp, scale=-0.5),
                             reads=[Tosq[i2]], writes=[Tosq[i2]])
                        P.op("dve", lambda e, hh=hh, i2=i2: e.scalar_tensor_tensor(
                            out=h_t1[i2][:, :T], in0=h_qT[:, hh, :T], scalar=vecs[:, og:og + 1], in1=h_osq[i2][:, :T],
                            op0=ALU.mult, op1=ALU.mult), reads=[TqT[hh], Tosq[i2], Tvecs], writes=[Tt1[i2]])
                        P.op("dve", lambda e, hh=hh, h=h, i2=i2: e.tensor_tensor(
                            out=h_yc[:, h, :T], in0=h_t1[i2][:, :T], in1=h_sg[:, hh, :T], op=ALU.mult),
                            reads=[Tt1[i2], Tsg[hh]], writes=[Tyc[h]])
                out_proj("c_out", 1, h_yc, Tyc, segs, T)


        def TT(eng, out, a, b, op, r, w):
            return P.op(eng, lambda e: e.tensor_tensor(out=out, in0=a, in1=b, op=op), reads=r, writes=w)

        def TSC(eng, out, a, s1, s2, op0, op1, r, w):
            if s2 is None:
                return P.op(eng, lambda e: e.tensor_scalar(out=out, in0=a, scalar1=s1, scalar2=None, op0=op0), reads=r, writes=w)
            return P.op(eng, lambda e: e.tensor_scalar(out=out, in0=a, scalar1=s1, scalar2=s2, op0=op0, op1=op1), reads=r, writes=w)

        def ACTF(out, a, func, r, w, bias=None, scale=1.0):
            if bias is None:
                return P.op("act", lambda e: e.activation(out=out, in_=a, func=func, scale=scale), reads=r, writes=w)
            return P.op("act", lambda e: e.activation(out=out, in_=a, func=func, bias=bias, scale=scale), reads=r, writes=w)

        if "s5" in mix or "rwkv" in mix:
            VA = View(arena, "mix0")
            a_yc = VA.alloc([128, 8, TOK], BF16)
            Tayc = [VA.tok() for _ in range(8)]
            VB = View(arena, "rwkv")
            VB.alloc([128, 8, TOK], BF16)
            VB.toks.extend(Tayc)
        if "s5" in mix:
            FR = min(64, TOK)
            s5s = sb("s5s", [128, 24, 16])
            Ftab = sb("Ftab", [128, 2, 16, FR])
            Etab = sb("Etab", [128, 2, 16, FR])
            BbTz = sb("BbTz", [128, 16, 2, 128], BF16)
            Cblk = sb("Cblk", [128, 16, 2, 128], BF16)
            rmaskA = sb("rmaskA", [128, 16, FR])
            rmaskB = sb("rmaskB", [128, 16, DSEQ])
            rowmask = sb("rowmask_sb", [128, 4])
            Xs = [sb(f"Xs{i}", [128, 2, 16]) for i in range(2)]
            TXs = [Tok(), Tok()]
            Tc5 = Tok("s5const")
            LR, LI, LDT, DT, TH, MAG, CC, SS, T1, T2, ABR, ABI, ZR, ZI, DEN, PR, PI, T3, T4 = [s5s[:, i, :] for i in range(19)]
            Bsrc = VI.alloc([128, 2, 16, 16])
            Csrc = VI.alloc([128, 2, 16, 16])
            bbar = VI.alloc([128, 2, 16, 16])
            Bblk = VI.alloc([128, 2, 16, 2, 16]) if False else VI.alloc([128, 2, 512])
            ftmp = [VI.alloc([128, 16, FR]) for _ in range(4)]
            P.op("sp", lambda e: e.dma_start(out=s5s[:, 0:3, :], in_=s5p_d[:, :, :]), writes=[Tc5], slot=S_const[5])
            P.op("sp", lambda e: e.dma_start(out=Bsrc[:], in_=s5b_d[:, :, :, :]), writes=[Tc5], slot=S_const[6])
            P.op("sp", lambda e: e.dma_start(out=Csrc[:], in_=s5c_d[:, :, :, :]), writes=[Tc5], slot=S_const[7])
            P.op("sp", lambda e: e.dma_start(out=rowmask[:], in_=rowmask_d[:, :]), writes=[Tc5], slot=S_rm)
            C5 = [Tc5]
            ACTF(DT, LDT, AF.Exp, C5, C5)
            TT("dve", MAG, LR, DT, ALU.mult, C5, C5)
            TT("dve", TH, LI, DT, ALU.mult, C5, C5)
            ACTF(MAG, MAG, AF.Exp, C5, C5)
            P.op("dve", lambda e: e.memset(s5s[:, 23, :], float(np.pi / 2)), writes=C5)
            ACTF(SS, TH, AF.Sin, C5, C5, scale=1.0 / 16)
            TSC("dve", T1, TH, 1.0 / 16, float(np.pi / 2), ALU.mult, ALU.add, C5, C5)
            ACTF(CC, T1, AF.Sin, C5, C5)
            for _ in range(4):
                TT("dve", T1, CC, CC, ALU.mult, C5, C5)
                TT("dve", T2, SS, SS, ALU.mult, C5, C5)
                TT("dve", T3, CC, SS, ALU.mult, C5, C5)
                TSC("dve", SS, T3, 2.0, None, ALU.mult, None, C5, C5)
                TT("dve", CC, T1, T2, ALU.subtract, C5, C5)
            TT("dve", ABR, MAG, CC, ALU.mult, C5, C5)
            TT("dve", ABI, MAG, SS, ALU.mult, C5, C5)
            TT("dve", T1, LR, LR, ALU.mult, C5, C5)
            TT("dve", T2, LI, LI, ALU.mult, C5, C5)
            TT("dve", DEN, T1, T2, ALU.add, C5, C5)
            P.op("dve", lambda e: e.reciprocal(out=DEN, in_=DEN), reads=C5, writes=C5)
            TSC("dve", T3, ABR, -1.0, None, ALU.add, None, C5, C5)
            TT("dve", T1, T3, LR, ALU.mult, C5, C5)
            TT("dve", T2, ABI, LI, ALU.mult, C5, C5)
            TT("dve", T1, T1, T2, ALU.add, C5, C5)
            TT("dve", ZR, T1, DEN, ALU.mult, C5, C5)
            TT("dve", T1, ABI, LR, ALU.mult, C5, C5)
            TT("dve", T2, T3, LI, ALU.mult, C5, C5)
            TT("dve", T1, T1, T2, ALU.subtract, C5, C5)
            TT("dve", ZI, T1, DEN, ALU.mult, C5, C5)
            zrb = ZR.unsqueeze(2).to_broadcast([128, 16, 16])
            zib = ZI.unsqueeze(2).to_broadcast([128, 16, 16])
            b1, b2 = ftmp[0][:, :, 0:16], ftmp[1][:, :, 0:16]
            TT("dve", b1, Bsrc[:, 0], zrb, ALU.mult, C5, C5)
            TT("dve", b2, Bsrc[:, 1], zib, ALU.mult, C5, C5)
            TT("dve", bbar[:, 0], b1, b2, ALU.subtract, C5, C5)
            TT("dve", b1, Bsrc[:, 1], zrb, ALU.mult, C5, C5)
            TT("dve", b2, Bsrc[:, 0], zib, ALU.mult, C5, C5)
            TT("dve", bbar[:, 1], b1, b2, ALU.add, C5, C5)
            P.op("dve", lambda e: e.memset(Bblk[:], 0.0), writes=C5)
            for ri in range(2):
                bv = Bblk[:, ri, :].rearrange("p (j g h) -> p j g h", j=16, g=2)
                for gl in range(2):
                    P.op("dve", lambda e, ri=ri, gl=gl, bv=bv: e.tensor_copy(
                        out=bv[gl * 64:(gl + 1) * 64, :, gl, :], in_=bbar[gl * 64:(gl + 1) * 64, ri, :, :]), reads=C5, writes=C5)
            for ri in range(2):
                for J in range(4):
                    P.op("pe", lambda e, ri=ri, J=J: e.transpose(ps[0][:, 0:128], Bblk[:, ri, J * 128:(J + 1) * 128], ident[:, :]),
                         reads=[Tc5, Tident], writes=[Tps[0]])
                    for jj in range(4):
                        P.op("dve", lambda e, ri=ri, J=J, jj=jj: e.tensor_scalar(
                            out=BbTz[:, 4 * J + jj, ri, :], in0=ps[0][:, 0:128], scalar1=rowmask[:, jj:jj + 1], scalar2=None,
                            op0=ALU.mult), reads=[Tps[0], Tc5], writes=[Tc5])
            P.op("dve", lambda e: e.memset(Cblk[:], 0.0), writes=C5)
            Cb5 = Cblk[:].rearrange("p (J q) r c -> p J q r c", q=4)
            for ri in range(2):
                Cs4 = Csrc[:, ri].rearrange("p (J q) h -> p J q h", q=4)
                for jj in range(4):
                    for gl in range(2):
                        c0_ = 32 * jj + 16 * gl
                        P.op("dve", lambda e, ri=ri, jj=jj, gl=gl, c0_=c0_, Cs4=Cs4: e.tensor_scalar(
                            out=Cb5[gl * 64:(gl + 1) * 64, :, jj, ri, c0_:c0_ + 16], in0=Cs4[gl * 64:(gl + 1) * 64, :, jj, :],
                            scalar1=(1.0 if ri == 0 else -1.0), scalar2=None, op0=ALU.mult), reads=C5, writes=C5)
            P.op("dve", lambda e: e.memset(Ftab[:, 0, :, 0:1], 1.0), writes=C5)
            P.op("dve", lambda e: e.memset(Ftab[:, 1, :, 0:1], 0.0), writes=C5)
            P.op("dve", lambda e: e.tensor_copy(out=PR, in_=ABR), reads=C5, writes=C5)
            P.op("dve", lambda e: e.tensor_copy(out=PI, in_=ABI), reads=C5, writes=C5)
            m = 1
            while m < FR:
                prb = PR.unsqueeze(2).to_broadcast([128, 16, m])
                pib = PI.unsqueeze(2).to_broadcast([128, 16, m])
                f1, f2 = ftmp[0][:, :, 0:m], ftmp[1][:, :, 0:m]
                TT("dve", f1, Ftab[:, 0, :, 0:m], prb, ALU.mult, C5, C5)
                TT("dve", f2, Ftab[:, 1, :, 0:m], pib, ALU.mult, C5, C5)
                TT("dve", Ftab[:, 0, :, m:2 * m], f1, f2, ALU.subtract, C5, C5)
                TT("dve", f1, Ftab[:, 0, :, 0:m], pib, ALU.mult, C5, C5)
                TT("dve", f2, Ftab[:, 1, :, 0:m], prb, ALU.mult, C5, C5)
                TT("dve", Ftab[:, 1, :, m:2 * m], f1, f2, ALU.add, C5, C5)
                TT("dve", T1, PR, PR, ALU.mult, C5, C5)
                TT("dve", T2, PI, PI, ALU.mult, C5, C5)
                TT("dve", T3, PR, PI, ALU.mult, C5, C5)
                TT("dve", PR, T1, T2, ALU.subtract, C5, C5)
                TSC("dve", PI, T3, 2.0, None, ALU.mult, None, C5, C5)
                m *= 2
            TT("dve", ftmp[0][:], Ftab[:, 0], Ftab[:, 0], ALU.mult, C5, C5)
            TT("dve", ftmp[1][:], Ftab[:, 1], Ftab[:, 1], ALU.mult, C5, C5)
            TT("dve", ftmp[0][:], ftmp[0][:], ftmp[1][:], ALU.add, C5, C5)
            P.op("dve", lambda e: e.reciprocal(out=ftmp[0][:], in_=ftmp[0][:]), reads=C5, writes=C5)
            TT("dve", Etab[:, 0], Ftab[:, 0], ftmp[0][:], ALU.mult, C5, C5)
            TT("dve", ftmp[1][:], Ftab[:, 1], ftmp[0][:], ALU.mult, C5, C5)
            TSC("dve", Etab[:, 1], ftmp[1][:], -1.0, None, ALU.mult, None, C5, C5)
            P.op("dve", lambda e: e.memset(rmaskA[:], 1.0), writes=C5)
            P.op("dve", lambda e: e.memset(rmaskA[:, :, 0:1], 0.0), writes=C5)
            P.op("dve", lambda e: e.memset(rmaskB[:], 1.0), writes=C5)
            P.op("dve", lambda e: e.memset(rmaskB[:, :, 0:1], 0.0), writes=C5)

            a_uT = VA.alloc([128, 4, TOK])
            a_ub = VA.alloc([128, 4, TOK], BF16)
            a_t = [VA.alloc([128, 8 * FR]) for _ in range(4)]
            _w_off = VA.off
            a_w = [VA.alloc([128, 16 * FR]) for _ in range(2)]
            a_cs = [VA.alloc([128, 16 * FR]) for _ in range(2)]
            _cs_end = VA.off
            a_xb = [VA.alloc([128, 16 * FR], BF16) for _ in range(2)]
            a_lx = VA.alloc([128, 4, 16])
            a_ya = VA.alloc([128, 4, TOK])
            _off = VA.off
            VA.off = _w_off
            a_g1 = VA.alloc([128, 4, TOK])
            a_yb = VA.alloc([128, 4, TOK], BF16)
            assert VA.off <= _cs_end
            VA.off = _off
            Tu = [VA.tok() for _ in range(4)]
            Tt = [VA.tok() for _ in range(4)]
            Tw = [VA.tok(), VA.tok()]
            Tcs_ = [VA.tok(), VA.tok()]
            Txb = [VA.tok(), VA.tok()]
            Tlx = VA.tok()
            Tya = [VA.tok() for _ in range(4)]
            Tg1L = [Tw[0], Tw[1]]
            Tyb = [Tcs_[0]]
            GC = float(2.0 * np.sqrt(2.0 / np.pi))

            def s5_proj(T):
                for ft in range(4):
                    b = load_wm("ab_in", ft, 128)
                    pb = ft % 2
                    for k in range(KT):
                        P.op("pe", lambda e, b=b, k=k, pb=pb: e.matmul(
                            ps[pb][:, :T], lhsT=wmb[b][:, k, 0:128], rhs=hT[:, k, :T],
                            start=(k == 0), stop=(k == KT - 1)), reads=[Twm[b], Th[k]], writes=[Tps[pb]])
                    P.op("act", lambda e, ft=ft, pb=pb: e.copy(out=a_uT[:, ft, :T], in_=ps[pb][:, :T]), reads=[Tps[pb]], writes=[Tu[ft]])
                    P.op("dve", lambda e, ft=ft, pb=pb: e.tensor_copy(out=a_ub[:, ft, :T], in_=ps[pb][:, :T]), reads=[Tps[pb]], writes=[Tu[ft]])

            def s5_frame(xi, c0, n, fi):
                X = Xs[xi]
                TX = TXs[xi]
                rm = (rmaskA if n == FR else rmaskB)[:].rearrange("p j t -> p (j t)")
                v3 = lambda buf, nj: buf[:, 0:nj * n].rearrange("p (j t) -> p j t", j=nj)
                for hf in range(2):
                    pz = (ps[0], ps[1]) if hf == 0 else (ps[2], ps[3])
                    tz = (Tps[0], Tps[1]) if hf == 0 else (Tps[2], Tps[3])
                    for ri in range(2):
                        for jj in range(8):
                            j = hf * 8 + jj
                            P.op("pe", lambda e, ri=ri, jj=jj, j=j, pz=pz: e.matmul(
                                pz[ri][:, jj * n:(jj + 1) * n], lhsT=BbTz[:, j, ri, :], rhs=a_ub[:, j // 4, c0:c0 + n],
                                start=True, stop=True), reads=[Tc5, Tu[j // 4]], writes=[tz[ri]])
                    zr = pz[0][:, 0:8 * n].rearrange("p (j t) -> p j t", j=8)
                    zi = pz[1][:, 0:8 * n].rearrange("p (j t) -> p j t", j=8)
                    Er = Etab[:, 0, hf * 8:(hf + 1) * 8, 0:n]
                    Ei = Etab[:, 1, hf * 8:(hf + 1) * 8, 0:n]
                    t = [v3(a_t[i], 8) for i in range(4)]
                    TT("dve", t[0], zr, Er, ALU.mult, [tz[0], Tc5], [Tt[0]])
                    TT("dve", t[1], zi, Ei, ALU.mult, [tz[1], Tc5], [Tt[1]])
                    TT("dve", t[2], zr, Ei, ALU.mult, [tz[0], Tc5], [Tt[2]])
                    TT("dve", t[3], zi, Er, ALU.mult, [tz[1], Tc5], [Tt[3]])
                    TT("pool", v3(a_w[0], 16)[:, hf * 8:(hf + 1) * 8, :], t[0], t[1], ALU.subtract, [Tt[0], Tt[1]], [Tw[0]])
                    TT("pool", v3(a_w[1], 16)[:, hf * 8:(hf + 1) * 8, :], t[2], t[3], ALU.add, [Tt[2], Tt[3]], [Tw[1]])
                lx = a_lx
                TT("pool", lx[:, 0], ABR, X[:, 0], ALU.mult, [Tc5, TX], [Tlx])
                TT("pool", lx[:, 1], ABI, X[:, 1], ALU.mult, [Tc5, TX], [Tlx])
                TT("pool", lx[:, 2], ABR, X[:, 1], ALU.mult, [Tc5, TX], [Tlx])
                TT("pool", lx[:, 3], ABI, X[:, 0], ALU.mult, [Tc5, TX], [Tlx])
                TT("pool", lx[:, 0], lx[:, 0], lx[:, 1], ALU.subtract, [Tlx], [Tlx])
                TT("pool", lx[:, 2], lx[:, 2], lx[:, 3], ALU.add, [Tlx], [Tlx])
                w0 = v3(a_w[0], 16)
                w1 = v3(a_w[1], 16)
                TT("pool", w0[:, :, 0:1], w0[:, :, 0:1], lx[:, 0].unsqueeze(2), ALU.add, [Tw[0], Tlx], [Tw[0]])
                TT("pool", w1[:, :, 0:1], w1[:, :, 0:1], lx[:, 2].unsqueeze(2), ALU.add, [Tw[1], Tlx], [Tw[1]])
                for ri, eng in ((0, "dve"), (1, "dve")):
                    P.op(eng, lambda e, ri=ri: e.tensor_tensor_scan(
                        out=a_cs[ri][:, 0:16 * n], data0=rm[:, 0:16 * n] if n == FR else rm, data1=a_w[ri][:, 0:16 * n], initial=0.0,
                        op0=ALU.mult, op1=ALU.add), reads=[Tw[ri], Tc5], writes=[Tcs_[ri]])
                for hf in range(2):
                    t = [v3(a_t[i], 8) for i in range(4)]
                    cr = v3(a_cs[0], 16)[:, hf * 8:(hf + 1) * 8, :]
                    ci = v3(a_cs[1], 16)[:, hf * 8:(hf + 1) * 8, :]
                    Fr = Ftab[:, 0, hf * 8:(hf + 1) * 8, 0:n]
                    Fi = Ftab[:, 1, hf * 8:(hf + 1) * 8, 0:n]
                    TT("dve", t[0], cr, Fr, ALU.mult, [Tcs_[0], Tc5], [Tt[0]])
                    TT("dve", t[1], ci, Fi, ALU.mult, [Tcs_[1], Tc5], [Tt[1]])
                    TT("dve", t[2], cr, Fi, ALU.mult, [Tcs_[0], Tc5], [Tt[2]])
                    TT("dve", t[3], ci, Fr, ALU.mult, [Tcs_[1], Tc5], [Tt[3]])
                    TT("pool", v3(a_xb[0], 16)[:, hf * 8:(hf + 1) * 8, :], t[0], t[1], ALU.subtract, [Tt[0], Tt[1]], [Txb[0]])
                    TT("pool", v3(a_xb[1], 16)[:, hf * 8:(hf + 1) * 8, :], t[2], t[3], ALU.add, [Tt[2], Tt[3]], [Txb[1]])
                    TT("pool", X[:, 0, hf * 8:(hf + 1) * 8].unsqueeze(2), t[0][:, :, n - 1:n], t[1][:, :, n - 1:n], ALU.subtract, [Tt[0], Tt[1]], [TX])
                    TT("pool", X[:, 1, hf * 8:(hf + 1) * 8].unsqueeze(2), t[2][:, :, n - 1:n], t[3][:, :, n - 1:n], ALU.add, [Tt[2], Tt[3]], [TX])
                py = ps[4 + fi % 2]
                tpy = Tps[4 + fi % 2]
                xr = v3(a_xb[0], 16)
                xim = v3(a_xb[1], 16)
                for c in range(4):
                    for jj in range(4):
                        for ri in range(2):
                            xx = xr if ri == 0 else xim
                            P.op("pe", lambda e, c=c, jj=jj, ri=ri, xx=xx, py=py: e.matmul(
                                py[:, c * n:(c + 1) * n], lhsT=Cblk[:, 4 * c + jj, ri, :], rhs=xx[:, 4 * c + jj, :],
                                start=(jj == 0 and ri == 0), stop=(jj == 3 and ri == 1)), reads=[Tc5, Txb[ri]], writes=[tpy])
                og = voff["s5_d"]
                for c in range(4):
                    P.op("dve", lambda e, c=c, py=py, og=og: e.scalar_tensor_tensor(
                        out=a_ya[:, c, c0:c0 + n], in0=a_uT[:, c, c0:c0 + n], scalar=vecs[:, og + c:og + c + 1],
                        in1=py[:, c * n:(c + 1) * n], op0=ALU.mult, op1=ALU.add), reads=[Tu[c], tpy, Tvecs], writes=[Tya[c]])

            def s5_post(T):
                TT("dve", a_g1[:, :, :T], a_ya[:, :, :T], a_ya[:, :, :T], ALU.mult, Tya, [Tw[0], Tw[1], Tcs_[0], Tcs_[1]])
                TSC("dve", a_g1[:, :, :T], a_g1[:, :, :T], 0.044715, 1.0, ALU.mult, ALU.add, Tg1L, Tg1L)
                TT("dve", a_g1[:, :, :T], a_g1[:, :, :T], a_ya[:, :, :T], ALU.mult, Tya + Tg1L, Tg1L)
                ACTF(a_g1[:, :, :T], a_g1[:, :, :T], AF.Sigmoid, Tg1L, Tg1L, scale=GC)
                TT("dve", a_ya[:, :, :T], a_ya[:, :, :T], a_g1[:, :, :T], ALU.mult, Tya + Tg1L, Tya)
                P.op("pool", lambda e: e.tensor_copy(out=a_yb[:, :, :T], in_=a_ya[:, :, :T]), reads=Tya, writes=Tyb)
                b = cnt["wm"] % NWM
                cnt["wm"] += 1
                gsrc = wscr["glu"]
                P.op("sp", lambda e, b=b: e.dma_start(out=wmb[b][:, 0:4, :], in_=gsrc[0]), reads=[Tscr["glu"]], writes=[Twm[b]], slot=S_wm[b])
                for c2 in range(4):
                    pb = c2 % 2
                    for c in range(4):
                        P.op("pe", lambda e, b=b, c=c, c2=c2, pb=pb: e.matmul(
                            ps[pb][:, :T], lhsT=wmb[b][:, c, c2 * 128:(c2 + 1) * 128], rhs=a_yb[:, c, :T],
                            start=(c == 0), stop=(c == 3)), reads=[Twm[b]] + Tyb, writes=[Tps[pb]])
                    ACTF(a_g1[:, c2, :T], ps[pb][:, :T], AF.Sigmoid, [Tps[pb]], Tg1L)
                    TT("dve", a_yc[:, c2, :T], a_ya[:, c2, :T], a_g1[:, c2, :T], ALU.mult, [Tya[c2]] + Tg1L, [Tayc[c2]])


        if "rwkv" in mix:
            LAM = float(np.exp(-0.5))
            wlo = sb("wlo", [128, 512])
            wg2 = sb("wg2", [128, 512])
            w0row = sb("w0row", [128, 512])
            blk64 = sb("blk64_sb", [128, 128])
            m5 = sb("m5_sb", [64, 5, 64])
            STz = [sb(f"STz{i}", [128, 8, 64]) for i in range(2)]
            shs = [sb(f"shs{i}", [128, 14]) for i in range(2)]
            omm = sb("omm", [128, 14])
            omka = sb("omka", [128, 4])
            TST = [Tok(), Tok()]
            Tsh = [Tok(), Tok()]
            Trc = Tok("rwconst")
            P.op("sp", lambda e: e.dma_start(out=wlo[:], in_=rw_lo_d[:, :]), writes=[Trc], slot=S_rc[0])
            P.op("sp", lambda e: e.dma_start(out=wg2[:], in_=rw_g2_d[:, :]), writes=[Trc], slot=S_rc[1])
            P.op("sp", lambda e: e.dma_start(out=w0row[:], in_=rw_w0_d[:, :]), writes=[Trc], slot=S_rc[2])
            P.op("sp", lambda e: e.dma_start(out=blk64[:], in_=blk64_d[:, :]), writes=[Trc], slot=S_rc[3])
            P.op("sp", lambda e: e.dma_start(out=m5[:], in_=m5_d[:, :, :]), writes=[Trc], slot=S_rc[4])
            omu = voff["rwkv_mu"]
            TSC("dve", omm[:], vecs[:, omu:omu + 14], -1.0, 1.0, ALU.mult, ALU.add, [Tvecs], [Trc])
            oka = voff["rwkv_k_a"]
            TSC("dve", omka[:], vecs[:, oka:oka + 4], -1.0, 1.0, ALU.mult, ALU.add, [Tvecs], [Trc])

            TS_ = min(256, TOK)
            NSM2 = max(1, TS_ // 128)
            b_wa = VB.alloc([128, TOK])
            b_gl = VB.alloc([128, TOK])
            b_r = VB.alloc([128, TS_])
            b_k = VB.alloc([128, TS_])
            b_a = VB.alloc([128, TS_])
            b_sig = VB.alloc([128, NSM2, 128])
            b_e = [VB.alloc([128, TS_]) for _ in range(2)]
            b_kk = VB.alloc([128, TS_])
            b_kp = VB.alloc([128, TS_])
            b_kb = VB.alloc([128, TS_])
            b_tmp = VB.alloc([128, TS_])
            Twa, Tgl, Tr_, Tk_, Ta_, Tsig = [VB.tok() for _ in range(6)]
            Te_ = [VB.tok(), VB.tok()]
            Tkk, Tkp, Tkb, Ttmp = [VB.tok() for _ in range(4)]
            LV = []
            for ci in range(2):
                d = {}
                for nm in ("v", "eW", "Bh", "Kh", "Bc", "Kc", "bon", "y"):
                    d[nm] = VB.alloc([128, TS_])
                    d["T" + nm] = VB.tok()
                d["AR"] = VB.alloc([128, 2, TS_])
                d["TAR"] = VB.tok()
                d["tm"] = VB.alloc([64, 384])
                d["Ttm"] = VB.tok()
                d["A5"] = [VB.alloc([64, 5, 64]) for _ in range(2)]
                d["TA5"] = [VB.tok(), VB.tok()]
                d["NP"] = [VB.alloc([64, 4, 64]) for _ in range(2)]
                d["TNP"] = [VB.tok(), VB.tok()]
                d["X"] = [VB.alloc([64, 2, 64]) for _ in range(2)]
                d["TX"] = [VB.tok(), VB.tok()]
                d["pX"], d["pY"], d["pT"] = 2 + 2 * ci, 3 + 2 * ci, (0 if ci == 0 else 6)
                LV.append(d)

            def shift_evac(pb, dst, Tdst, ti, lsegs):
                for (sq_, c0, n_) in lsegs:
                    sl = sq_ % 2
                    P.op("act", lambda e, c0=c0, n_=n_: e.activation(
                        out=dst[:, c0:c0 + n_], in_=ps[pb][:, c0:c0 + n_], func=AF.Identity, bias=epsc[:, 3:4], scale=omm[:, ti:ti + 1]),
                        reads=[Tps[pb], Trc, Teps], writes=[Tdst])
                    P.op("dve", lambda e, c0=c0, n_=n_: e.scalar_tensor_tensor(
                        out=dst[:, c0 + 1:c0 + n_], in0=ps[pb][:, c0:c0 + n_ - 1], scalar=vecs[:, omu + ti:omu + ti + 1],
                        in1=dst[:, c0 + 1:c0 + n_], op0=ALU.mult, op1=ALU.add), reads=[Tps[pb], Tvecs, Tdst], writes=[Tdst])
                    P.op("dve", lambda e, c0=c0, sl=sl: e.scalar_tensor_tensor(
                        out=dst[:, c0:c0 + 1], in0=shs[sl][:, ti:ti + 1], scalar=vecs[:, omu + ti:omu + ti + 1],
                        in1=dst[:, c0:c0 + 1], op0=ALU.mult, op1=ALU.add), reads=[Tsh[sl], Tvecs, Tdst], writes=[Tdst])
                    P.op("act", lambda e, c0=c0, n_=n_, sl=sl: e.copy(out=shs[sl][:, ti:ti + 1], in_=ps[pb][:, c0 + n_ - 1:c0 + n_]),
                         reads=[Tps[pb]], writes=[Tsh[sl]])

            def proj128(grp, pb, cb, W):
                b = load_wm("ab_in", grp, 128)
                for k in range(KT):
                    P.op("pe", lambda e, b=b, k=k: e.matmul(ps[pb][:, :W], lhsT=wmb[b][:, k, 0:128], rhs=hT[:, k, cb:cb + W],
                                                            start=(k == 0), stop=(k == KT - 1)),
                         reads=[Twm[b], Th[k]], writes=[Tps[pb]])

            def rw_prep(hp, ci, cb, W, lsegs, R, tri, tristr, trirev):
                L = LV[ci]
                NS = W // R
                hc = slice(hp * 128, (hp + 1) * 128)
                for (grp, dst, Td, ti, pb) in ((4 + hp, b_r, Tr_, hp, 0), (8 + hp, b_k, Tk_, 4 + hp, 7), (12 + hp, L["v"], L["Tv"], 8 + hp, 0)):
                    proj128(grp, pb, cb, W)
                    shift_evac(pb, dst, Td, ti, lsegs)
                pq = 7
                P.op("pe", lambda e: e.matmul(ps[pq][:, :W], lhsT=wlo[64:128, hc], rhs=b_wa[64:128, cb:cb + W], start=True, stop=True),
                     reads=[Trc, Twa], writes=[Tps[pq]])
                oa0 = voff["rwkv_a0"]
                ACTF(b_a[:, :W], ps[pq][:, :W], AF.Sigmoid, [Tps[pq], Tvecs], [Ta_], bias=vecs[:, oa0 + hp:oa0 + hp + 1])
                for s_ in range(NS):
                    P.op("pe", lambda e, s_=s_: e.matmul(ps[1 - 1][:R, 0:128] if False else ps[7][:R, 256:384], lhsT=b_wa[0:64, cb + s_ * R:cb + (s_ + 1) * R], rhs=wlo[0:64, hc],
                                                         start=True, stop=True), reads=[Trc, Twa], writes=[Tps[7]])
                    TT("dve", b_sig[:R, s_, :], ps[7][:R, 256:384], w0row[:R, hc], ALU.add, [Tps[7], Trc], [Tsig])
                    ACTF(b_sig[:R, s_, :], b_sig[:R, s_, :], AF.Sigmoid, [Tsig], [Tsig])
                pc, pcx, psf = L["pX"], L["pY"], 7
                for (pb, msk) in ((pc, tri), (pcx, tristr), (psf, trirev)):
                    for s_ in range(NS):
                        P.op("pe", lambda e, s_=s_, pb=pb, msk=msk: e.matmul(
                            ps[pb][:, s_ * R:(s_ + 1) * R], lhsT=b_sig[:R, s_, :], rhs=msk, start=True, stop=True),
                            reads=[Tsig, Tcm], writes=[Tps[pb]])
                ACTF(L["eW"][:, :W], ps[pc][:, :W], AF.Exp, [Tps[pc]], [L["TeW"]], scale=-LAM)
                ACTF(b_e[0][:, :W], ps[pcx][:, :W], AF.Exp, [Tps[pcx]], [Te_[0]], scale=-LAM)
                ACTF(b_e[1][:, :W], ps[pc][:, :W], AF.Exp, [Tps[pc]], [Te_[1]], scale=LAM)
                okk = voff["rwkv_k_k"]
                TSC("dve", b_kk[:, :W], b_k[:, :W], vecs[:, okk + hp:okk + hp + 1], None, ALU.mult, None, [Tk_, Tvecs], [Tkk])
                TT("pool", b_tmp[:, :W], b_kk[:, :W], b_kk[:, :W], ALU.mult, [Tkk], [Ttmp])
                P.op("pe", lambda e: e.matmul(ps[pcx][:, :W], lhsT=blk64[:], rhs=b_tmp[:, :W], start=True, stop=True),
                     reads=[Trc, Ttmp], writes=[Tps[pcx]])
                ACTF(b_tmp[:, :W], ps[pcx][:, :W], AF.Ln, [Tps[pcx]], [Ttmp])
                ACTF(b_tmp[:, :W], b_tmp[:, :W], AF.Exp, [Ttmp], [Ttmp], scale=-0.5)
                TT("dve", b_kk[:, :W], b_kk[:, :W], b_tmp[:, :W], ALU.mult, [Tkk, Ttmp], [Tkk])
                P.op("dve", lambda e: e.scalar_tensor_tensor(out=L["AR"][:, 0, :W], in0=b_kk[:, :W], scalar=-1.0, in1=b_e[0][:, :W],
                                                             op0=ALU.mult, op1=ALU.mult), reads=[Tkk, Te_[0]], writes=[L["TAR"]])
                TT("dve", L["AR"][:, 1, :W], b_r[:, :W], L["eW"][:, :W], ALU.mult, [Tr_, L["TeW"]], [L["TAR"]])
                TSC("dve", b_kp[:, :W], b_a[:, :W], vecs[:, oka + hp:oka + hp + 1], omka[:, hp:hp + 1], ALU.mult, ALU.add, [Ta_, Tvecs, Trc], [Tkp])
                TT("dve", b_kp[:, :W], b_kp[:, :W], b_k[:, :W], ALU.mult, [Tkp, Tk_], [Tkp])
                TT("pool", b_kb[:, :W], b_kk[:, :W], b_a[:, :W], ALU.mult, [Tkk, Ta_], [Tkb])
                TT("dve", L["Bh"][:, :W], b_kb[:, :W], b_e[1][:, :W], ALU.mult, [Tkb, Te_[1]], [L["TBh"]])
                TT("pool", L["Kh"][:, :W], b_kp[:, :W], b_e[1][:, :W], ALU.mult, [Tkp, Te_[1]], [L["TKh"]])
                ACTF(b_e[0][:, :W], ps[psf][:, :W], AF.Exp, [Tps[psf]], [Te_[0]], scale=-LAM)
                TT("dve", L["Bc"][:, :W], b_kb[:, :W], b_e[0][:, :W], ALU.mult, [Tkb, Te_[0]], [L["TBc"]])
                TT("pool", L["Kc"][:, :W], b_kp[:, :W], b_e[0][:, :W], ALU.mult, [Tkp, Te_[0]], [L["TKc"]])
                ork = voff["rwkv_r_k"]
                P.op("dve", lambda e: e.scalar_tensor_tensor(out=b_tmp[:, :W], in0=b_r[:, :W], scalar=vecs[:, ork + hp:ork + hp + 1],
                                                             in1=b_kp[:, :W], op0=ALU.mult, op1=ALU.mult), reads=[Tr_, Tvecs, Tkp, Ttmp], writes=[Ttmp])
                P.op("pe", lambda e: e.matmul(ps[pcx][:, :W], lhsT=blk64[:], rhs=b_tmp[:, :W], start=True, stop=True),
                     reads=[Trc, Ttmp], writes=[Tps[pcx]])
                TT("dve", L["bon"][:, :W], L["v"][:, :W], ps[pcx][:, :W], ALU.mult, [L["Tv"], Tps[pcx]], [L["Tbon"]])
                P.op("pe", lambda e: e.matmul(ps[1][:, ci * TS_:ci * TS_ + W], lhsT=wg2[:, hc], rhs=b_gl[:, cb:cb + W], start=True, stop=True),
                     reads=[Trc, Tgl], writes=[Tps[1]])

            def rw_chain(hp, ci, lchunks):
                L = LV[ci]
                pX, pY, pT = L["pX"], L["pY"], L["pT"]
                AR, tm, A5, NPb, Xb = L["AR"], L["tm"], L["A5"], L["NP"], L["X"]
                for (sl, c0, n) in lchunks:
                    cc = slice(c0, c0 + n)
                    ST = STz[sl]
                    for q, (src, Ts_) in enumerate(((L["v"], L["Tv"]), (L["Bc"], L["TBc"]), (L["Kc"], L["TKc"]))):
                        P.op("pe", lambda e, q=q, src=src, cc=cc, n=n: e.transpose(ps[pT][:n, q * 128:(q + 1) * 128], src[:, cc], ident[:, :]),
                             reads=[Ts_, Tident], writes=[Tps[pT]])
                    P.op("act", lambda e, n=n: e.copy(out=tm[:n, :], in_=ps[pT][:n, 0:384]), reads=[Tps[pT]], writes=[L["Ttm"]])
                    for hl in range(2):
                        pr = slice(64 * hl, 64 * hl + 64)
                        pa = ps[pX if hl == 0 else pY]
                        tpa = Tps[pX if hl == 0 else pY]
                        P.op("pe", lambda e, pr=pr, pa=pa, cc=cc, n=n: e.matmul(
                            pa[:n, 0:2 * n], lhsT=L["Bh"][pr, cc], rhs=AR[pr, :, cc], start=True, stop=True),
                            reads=[L["TBh"], L["TAR"]], writes=[tpa])
                        P.op("pe", lambda e, pr=pr, pa=pa, cc=cc, n=n: e.matmul(
                            pa[:n, 2 * n:4 * n], lhsT=L["Kh"][pr, cc], rhs=AR[pr, :, cc], start=True, stop=True),
                            reads=[L["TKh"], L["TAR"]], writes=[tpa])
                        P.op("pe", lambda e, pr=pr, pa=pa, cc=cc, n=n: e.matmul(
                            pa[:n, 4 * n:5 * n], lhsT=AR[pr, 0, cc], rhs=L["Bh"][pr, cc], start=True, stop=True),
                            reads=[L["TBh"], L["TAR"]], writes=[tpa])
                        P.op("dve", lambda e, hl=hl, pa=pa, n=n: e.tensor_tensor(
                            out=A5[hl][:n, :, :n], in0=pa[:n, 0:5 * n].rearrange("p (q t) -> p q t", q=5),
                            in1=m5[:n, :, :n], op=ALU.mult), reads=[tpa, Trc], writes=[L["TA5"][hl]])
                    yield
                    for hl in range(2):
                        h = 2 * hp + hl
                        P.op("pe", lambda e, hl=hl, h=h, cc=cc, ST=ST, n=n: e.matmul(
                            ps[pX][:n, 384 + hl * 64:384 + (hl + 1) * 64], lhsT=AR[:, 0, cc], rhs=ST[:, h, :], start=True, stop=False),
                            reads=[L["TAR"], TST[sl]], writes=[Tps[pX]])
                        P.op("pe", lambda e, hl=hl, n=n: e.matmul(
                            ps[pX][:n, 384 + hl * 64:384 + (hl + 1) * 64], lhsT=A5[hl][:n, 2, :n], rhs=tm[:n, hl * 64:(hl + 1) * 64], start=False, stop=True),
                            reads=[L["TA5"][hl], L["Ttm"]], writes=[Tps[pX]])
                    P.op("act", lambda e, n=n: e.copy(out=Xb[0][:n, :, :], in_=ps[pX][:n, 384:512].rearrange("p (h v) -> p h v", h=2)),
                         reads=[Tps[pX]], writes=[L["TX"][0]])
                    yield
                    xi = 0
                    for lev in range(6):
                        for hl in range(2):
                            if lev == 0:
                                Nn, NTn, rd = A5[hl][:n, 4, :n], A5[hl][:n, 0, :n], [L["TA5"][hl]]
                            else:
                                pv = (lev - 1) % 2
                                Nn, NTn, rd = NPb[pv][:n, 2 * hl, :n], NPb[pv][:n, 2 * hl + 1, :n], [L["TNP"][pv]]
                            P.op("pe", lambda e, hl=hl, NTn=NTn, xi=xi, n=n: e.matmul(
                                ps[pX][:n, 384 + hl * 64:384 + (hl + 1) * 64], lhsT=NTn, rhs=Xb[xi][:n, hl, :], start=True, stop=True),
                                reads=rd + [L["TX"][xi]], writes=[Tps[pX]])
                            if lev < 5:
                                P.op("pe", lambda e, hl=hl, Nn=Nn, NTn=NTn, n=n: e.matmul(
                                    ps[pY][:n, (2 * hl) * n:(2 * hl + 1) * n], lhsT=NTn, rhs=Nn, start=True, stop=True), reads=rd, writes=[Tps[pY]])
                                P.op("pe", lambda e, hl=hl, Nn=Nn, NTn=NTn, n=n: e.matmul(
                                    ps[pY][:n, (2 * hl + 1) * n:(2 * hl + 2) * n], lhsT=Nn, rhs=NTn, start=True, stop=True), reads=rd, writes=[Tps[pY]])
                        P.op("dve", lambda e, xi=xi, n=n: e.tensor_tensor(
                            out=Xb[1 - xi][:n, :, :], in0=Xb[xi][:n, :, :], in1=ps[pX][:n, 384:512].rearrange("p (h v) -> p h v", h=2),
                            op=ALU.add), reads=[L["TX"][xi], Tps[pX]], writes=[L["TX"][1 - xi]])
                        if lev < 5:
                            P.op("act", lambda e, lev=lev, n=n: e.copy(out=NPb[lev % 2][:n, :, :n], in_=ps[pY][:n, 0:4 * n].rearrange("p (q t) -> p q t", q=4)),
                                 reads=[Tps[pY]], writes=[L["TNP"][lev % 2]])
                        xi = 1 - xi
                        yield
                    U = Xb[xi]
                    TU = L["TX"][xi]
                    for hl in range(2):
                        h = 2 * hp + hl
                        pr = slice(64 * hl, 64 * hl + 64)
                        P.op("pe", lambda e, pr=pr, h=h, cc=cc, ST=ST, n=n: e.matmul(
                            ps[pX][pr, 0:n], lhsT=ST[:, h, :], rhs=AR[:, 1, cc], start=True, stop=False),
                            reads=[L["TAR"], TST[sl]], writes=[Tps[pX]])
                        P.op("pe", lambda e, pr=pr, hl=hl, U=U, n=n: e.matmul(
                            ps[pX][pr, 0:n], lhsT=U[:n, hl, :], rhs=A5[hl][:n, 1, :n], start=False, stop=False),
                            reads=[L["TA5"][hl], TU], writes=[Tps[pX]])
                        P.op("pe", lambda e, pr=pr, hl=hl, n=n: e.matmul(
                            ps[pX][pr, 0:n], lhsT=tm[:n, hl * 64:(hl + 1) * 64], rhs=A5[hl][:n, 3, :n], start=False, stop=True),
                            reads=[L["TA5"][hl], L["Ttm"]], writes=[Tps[pX]])
                    P.op("act", lambda e, cc=cc, n=n: e.copy(out=L["y"][:, cc], in_=ps[pX][:, 0:n]), reads=[Tps[pX]], writes=[L["Ty"]])
                    for hl in range(2):
                        h = 2 * hp + hl
                        pr = slice(64 * hl, 64 * hl + 64)
                        P.op("pe", lambda e, pr=pr, hl=hl, U=U, n=n: e.matmul(
                            ps[pY][pr, 256:320], lhsT=tm[:n, 128 + hl * 64:128 + (hl + 1) * 64], rhs=U[:n, hl, :], start=True, stop=False),
                            reads=[L["Ttm"], TU], writes=[Tps[pY]])
                        P.op("pe", lambda e, pr=pr, hl=hl, n=n: e.matmul(
                            ps[pY][pr, 256:320], lhsT=tm[:n, 256 + hl * 64:256 + (hl + 1) * 64], rhs=tm[:n, hl * 64:(hl + 1) * 64], start=False, stop=True),
                            reads=[L["Ttm"]], writes=[Tps[pY]])
                        cl = c0 + n - 1
                        P.op("dve", lambda e, pr=pr, h=h, cl=cl, ST=ST: e.scalar_tensor_tensor(
                            out=ST[pr, h, :], in0=ST[pr, h, :], scalar=L["eW"][pr, cl:cl + 1], in1=ps[pY][pr, 256:320],
                            op0=ALU.mult, op1=ALU.add), reads=[TST[sl], L["TeW"], Tps[pY]], writes=[TST[sl]])
                    yield

            def rw_final(hp, ci, cb, W):
                L = LV[ci]
                by = L["y"]
                P.op("pe", lambda e: e.matmul(ps[6][:, :W], lhsT=blk64[:], rhs=by[:, :W], start=True, stop=True),
                     reads=[Trc, L["Ty"]], writes=[Tps[6]])
                P.op("dve", lambda e: e.scalar_tensor_tensor(out=by[:, :W], in0=ps[6][:, :W], scalar=-1.0 / 64, in1=by[:, :W],
                                                             op0=ALU.mult, op1=ALU.add), reads=[Tps[6], L["Ty"]], writes=[L["Ty"]])
                TT("pool", b_tmp[:, :W], by[:, :W], by[:, :W], ALU.mult, [L["Ty"], Ttmp], [Ttmp])
                P.op("pe", lambda e: e.matmul(ps[7][:, :W], lhsT=blk64[:], rhs=b_tmp[:, :W], start=True, stop=True),
                     reads=[Trc, Ttmp], writes=[Tps[7]])
                ACTF(b_tmp[:, :W], ps[7][:, :W], AF.Ln, [Tps[7], Teps], [Ttmp], bias=epsc[:, 1:2], scale=1.0 / 64)
                ACTF(b_tmp[:, :W], b_tmp[:, :W], AF.Exp, [Ttmp], [Ttmp], scale=-0.5)
                olg, olb = voff["rwkv_lnx_g"], voff["rwkv_lnx_b"]
                P.op("dve", lambda e: e.scalar_tensor_tensor(out=by[:, :W], in0=by[:, :W], scalar=vecs[:, olg + hp:olg + hp + 1],
                                                             in1=b_tmp[:, :W], op0=ALU.mult, op1=ALU.mult), reads=[L["Ty"], Tvecs, Ttmp], writes=[L["Ty"]])
                P.op("dve", lambda e: e.scalar_tensor_tensor(out=by[:, :W], in0=by[:, :W], scalar=vecs[:, olb + hp:olb + hp + 1],
                                                             in1=L["bon"][:, :W], op0=ALU.add, op1=ALU.add), reads=[L["Ty"], Tvecs, L["Tbon"]], writes=[L["Ty"]])
                TT("dve", a_yc[:, 4 + hp, cb:cb + W], by[:, :W], ps[1][:, ci * TS_:ci * TS_ + W], ALU.mult, [L["Ty"], Tps[1]], [Tayc[4 + hp]])

            def rwkv(segs, T, chunks, R, mi):
                switch(VB, keep=Tayc)
                tri = cmask[:R, mi, :R]
                trirev = cmask[:R, mi + 1, :R]
                tristr = cmask[:R, 4 + mi // 2, :R]
                proj128(16, 0, 0, T)
                shift_evac(0, b_wa, Twa, 12, segs)
                proj128(17, 1, 0, T)
                shift_evac(1, b_gl, Tgl, 13, segs)
                ACTF(b_wa[0:64, :T], b_wa[0:64, :T], AF.Tanh, [Twa], [Twa])
                ACTF(b_gl[:, :T], b_gl[:, :T], AF.Sigmoid, [Tgl], [Tgl])
                W = min(TS_, T)
                for cb in range(0, T, W):
                    lsegs = [(sq_, max(c0, cb) - cb, min(c0 + n_, cb + W) - max(c0, cb)) for (sq_, c0, n_) in segs
                             if c0 < cb + W and c0 + n_ > cb]
                    lchunks = [(sl, c0 - cb, n) for (sl, c0, n) in chunks if cb <= c0 < cb + W]
                    for pair in ((0, 1), (2, 3)):
                        for ci, hp in enumerate(pair):
                            rw_prep(hp, ci, cb, W, lsegs, R, tri, tristr, trirev)
                        gens = [rw_chain(hp, ci, lchunks) for ci, hp in enumerate(pair)]
                        while gens:
                            for g in list(gens):
                                try:
                                    next(g)
                                except StopIteration:
                                    gens.remove(g)
                        for ci, hp in enumerate(pair):
                            rw_final(hp, ci, cb, W)


        dbg_done = []
        S_dbg = P.slot("dbg")
        if cfg.get("dbg"):
            dbg_d = nc.dram_tensor("dbg", [128, 8, TOK], BF16, kind="ExternalOutput").ap()

        def mixer0(segs, T, frames, kind):
            switch(VA)
            if "s5" in mix:
                s5_proj(T)
                for fi, (xi, c0, n) in enumerate(frames):
                    s5_frame(xi, c0, n, fi)
                s5_post(T)
            else:
                P.op("dve", lambda e: e.memset(a_yc[:, 0:4, :T], 0.0), writes=Tayc[0:4])
            if "rwkv" in mix:
                rwkv(segs, T, frames, min(128, T), 0 if kind == "p" else 2)
                if cfg.get("dbg") and not dbg_done:
                    dbg_done.append(1)
                    P.op("pool", lambda e: e.dma_start(out=dbg_d[:, :, :], in_=a_yc[:, :, :]), reads=Tayc, slot=S_dbg)
            else:
                P.op("dve", lambda e: e.memset(a_yc[:, 4:8, :T], 0.0), writes=Tayc[4:8])
            out_proj("ab_out", 0, a_yc, Tayc, segs, T)

        tiles = []
        for s in range(2):
            for t0 in range(0, SEQ, TOK):
                tiles.append(("p", [(s, 0, TOK)], TOK, t0))
        tiles.append(("s", [(2, 0, DSEQ), (3, DSEQ, DSEQ)], 2 * DSEQ, 0))

        for (kind, segs, T, t0) in tiles:
            if kind == "p":
                s = segs[0][0]
                P.op("sp", lambda e, s=s, t0=t0: e.dma_start(
                    out=xT[:, :, :TOK], in_=xT_d[s, :, t0:t0 + TOK].rearrange("(k p) t -> p k t", p=128)),
                    writes=Tx, slot=S_x[0])
                R, CB, mi = min(128, TOK), min(128, TOK) // 2, 0
                blocks = [(s, c) for c in range(0, TOK, CB)]
                if t0 == 0 and "hgrn" in mix:
                    P.op("dve", lambda e, s=s: e.memset(Sg[s][:], 0.0), writes=[TS[s]])
                if t0 == 0 and "s5" in mix:
                    P.op("dve", lambda e, s=s: e.memset(Xs[s][:], 0.0), writes=[TXs[s]])
                if t0 == 0 and "rwkv" in mix:
                    P.op("dve", lambda e, s=s: e.memset(STz[s][:], 0.0), writes=[TST[s]])
                    P.op("dve", lambda e, s=s: e.memset(shs[s][:], 0.0), writes=[Tsh[s]])
                frames = [(s, c, min(64, TOK)) for c in range(0, TOK, min(64, TOK))]
            else:
                for si, (s, c0, n_) in enumerate(segs):
                    P.op("sp", lambda e, s=s, c0=c0, n_=n_: e.dma_start(
                        out=xT[:, :, c0:c0 + n_], in_=xsT_d[s - 2, :, :].rearrange("(k p) t -> p k t", p=128)),
                        writes=Tx, slot=S_x[si])
                R, CB, mi = 2 * DSEQ, DSEQ, 2
                blocks = [(0, 0), (1, DSEQ)]
                if "hgrn" in mix:
                    for i in range(2):
                        P.op("sp", lambda e, i=i: e.dma_start(out=Sg[i][:], in_=hg_in_d[i]), writes=[TS[i]], slot=S_st[i])
                if "s5" in mix:
                    for i in range(2):
                        P.op("sp", lambda e, i=i: e.dma_start(out=Xs[i][:], in_=s5x_d[i]), writes=[TXs[i]], slot=S_sx[i])
                if "rwkv" in mix:
                    for i in range(2):
                        P.op("sp", lambda e, i=i: e.dma_start(out=STz[i][:], in_=rw_st_d[i]), writes=[TST[i]], slot=S_rst[i])
                        P.op("sp", lambda e, i=i: e.dma_start(out=shs[i][:], in_=rw_sh_d[i]), writes=[Tsh[i]], slot=S_rst[2 + i])
                frames = [(0, 0, DSEQ), (1, DSEQ, DSEQ)]
            for l in range(2):
                ffn(l, 0, segs, T)
                if l == 0 and ("s5" in mix or "rwkv" in mix):
                    norm_mod(l, 1, segs, T)
                    mixer0(segs, T, frames, kind)
                if l == 1 and "hgrn" in mix:
                    norm_mod(l, 1, segs, T)
                    hgrn(segs, T, blocks, R, CB, mi)
                ffn(l, 1, segs, T)
            norm_mod(0, 0, segs, T, final=True)
            if kind == "p":
                s = segs[0][0]
                P.op("pool", lambda e, s=s, t0=t0: e.dma_start(
                    out=yT_d[s, :, t0:t0 + TOK].rearrange("(k p) t -> p k t", p=128), in_=yT[:, :, :TOK]),
                    reads=Ty, slot=S_y[0])
                if t0 + TOK >= SEQ and "hgrn" in mix:
                    P.op("pool", lambda e, s=s: e.dma_start(out=hg_out_d[s], in_=Sg[s][:]), reads=[TS[s]], slot=S_so[s])
                if t0 + TOK >= SEQ and "s5" in mix:
                    P.op("pool", lambda e, s=s: e.dma_start(out=s5o_d[s], in_=Xs[s][:]), reads=[TXs[s]], slot=S_s5o[s])
                if t0 + TOK >= SEQ and "rwkv" in mix:
                    P.op("pool", lambda e, s=s: e.dma_start(out=rw_sto_d[s], in_=STz[s][:]), reads=[TST[s]], slot=S_rso[s])
                    P.op("pool", lambda e, s=s: e.dma_start(out=rw_sho_d[s], in_=shs[s][:]), reads=[Tsh[s]], slot=S_rso[4 + s])
            else:
                for si, (s, c0, n_) in enumerate(segs):
                    P.op("pool", lambda e, s=s, c0=c0, n_=n_: e.dma_start(
                        out=ysT_d[s - 2, :, :].rearrange("(k p) t -> p k t", p=128), in_=yT[:, :, c0:c0 + n_]),
                        reads=Ty, slot=S_y[si])
                if "hgrn" in mix:
                    for i in range(2):
                        P.op("pool", lambda e, i=i: e.dma_start(out=hg_out_d[2 + i], in_=Sg[i][:]), reads=[TS[i]], slot=S_so[2 + i])
                if "s5" in mix:
                    for i in range(2):
                        P.op("pool", lambda e, i=i: e.dma_start(out=s5o_d[2 + i], in_=Xs[i][:]), reads=[TXs[i]], slot=S_s5o[2 + i])
                if "rwkv" in mix:
                    for i in range(2):
                        P.op("pool", lambda e, i=i: e.dma_start(out=rw_sto_d[2 + i], in_=STz[i][:]), reads=[TST[i]], slot=S_rso[2 + i])
                        P.op("pool", lambda e, i=i: e.dma_start(out=rw_sho_d[2 + i], in_=shs[i][:]), reads=[Tsh[i]], slot=S_rso[6 + i])

        sems = {e: es.enter_context(nc.semaphore("sem_" + e)) for e in ("pe", "act", "dve", "pool")}
        for s in P.slots:
            s.sem = es.enter_context(nc.semaphore("dsem_" + s.name))
        with nc.Block() as block:
            P.emit(block, sems, final_wait_slots={"pool": S_y + S_pre + S_so + S_s5o + S_rso + [S_dbg], "sp": []})
    return nc


def fm(v):
    v = np.asarray(v, np.float32)
    return np.ascontiguousarray(v.reshape(-1, 128).T)


def tile_kc(w, ncol):
    w = np.asarray(w, np.float32)
    K, N = w.shape
    return np.ascontiguousarray(w.reshape(K // 128, 128, N // ncol, ncol).transpose(2, 1, 0, 3))


def const_masks():
    cm = np.zeros((128, 6, 128), np.float32)
    for (mi, n, cb) in ((0, 128, 64), (2, 64, 32)):
        i = np.arange(n)
        same = (i[:, None] // cb) == (i[None, :] // cb)
        cm[:n, mi, :n] = (same & (i[:, None] <= i[None, :])).astype(np.float32)
        cm[:n, mi + 1, :n] = (same & (i[:, None] > i[None, :])).astype(np.float32)
        cm[:n, 4 + mi // 2, :n] = (same & (i[:, None] < i[None, :])).astype(np.float32)
    return cm


def prep_shared(inp, mix):
    sh = {}
    sh["w_ada"] = np.ascontiguousarray(inp["w_ada"], np.float32)
    voff, NV = vec_layout()
    vecs = np.zeros((128, NV), np.float32)
    for l in range(2):
        for nm in ("norm_ffn1", "norm_mix", "norm_ffn2"):
            vecs[:, voff[f"{nm}{l}"]:voff[f"{nm}{l}"] + 8] = fm(inp[nm][l])
        vecs[:, voff[f"b_ada{l}"]:voff[f"b_ada{l}"] + 72] = fm(inp["b_ada"][l])
    vecs[:, voff["final_norm"]:voff["final_norm"] + 8] = fm(inp["final_norm"])
    vecs[:, voff["hgrn_gnorm"]:voff["hgrn_gnorm"] + 1] = fm(inp["hgrn_gnorm"][0])
    sh["vecs"] = vecs
    sh["ident"] = np.eye(128, dtype=np.float32)
    sh["cmask"] = const_masks()
    ffw = ((inp["ffn1_w1"], inp["ffn1_w3"], inp["ffn1_w2"]), (inp["ffn2_w1"], inp["ffn2_w3"], inp["ffn2_w2"]))
    for l in range(2):
        for fi in range(2):
            sh[f"w_f{l}{fi}w1"] = tile_kc(ffw[fi][0][l], 256)
            sh[f"w_f{l}{fi}w3"] = tile_kc(ffw[fi][1][l], 256)
            w2 = np.asarray(ffw[fi][2][l], np.float32)
            sh[f"w_f{l}{fi}w2"] = np.ascontiguousarray(w2.reshape(11, 2, 128, 2, 512).transpose(3, 0, 2, 1, 4))
    vecs[:, voff["s5_d"]:voff["s5_d"] + 4] = fm(np.asarray(inp["s5_d"][0]).reshape(-1))
    vecs[:, voff["rwkv_mu"]:voff["rwkv_mu"] + 14] = fm(inp["rwkv_mu"][0])
    for nm in ("rwkv_a0", "rwkv_k_k", "rwkv_k_a", "rwkv_r_k", "rwkv_lnx_g", "rwkv_lnx_b"):
        vecs[:, voff[nm]:voff[nm] + 4] = fm(np.asarray(inp[nm][0]).reshape(-1))
    if "rwkv" in mix:
        sh["rw_lo"] = np.ascontiguousarray(np.concatenate([inp["rwkv_w_w2"][0], inp["rwkv_w_a2"][0]], 0), np.float32)
        sh["rw_g2"] = np.ascontiguousarray(inp["rwkv_w_g2"][0], np.float32)
        sh["rw_w0row"] = np.ascontiguousarray(np.broadcast_to(np.asarray(inp["rwkv_w0"][0], np.float32)[None], (128, 512)))
        bl = np.zeros((128, 128), np.float32)
        bl[0:64, 0:64] = 1.0
        bl[64:128, 64:128] = 1.0
        sh["blk64"] = bl
        i = np.arange(64)
        lt = (i[:, None] < i[None, :]).astype(np.float32)
        le = (i[:, None] <= i[None, :]).astype(np.float32)
        gt = (i[:, None] > i[None, :]).astype(np.float32)
        sh["m5"] = np.ascontiguousarray(np.stack([lt, le, lt, le, gt], 1))
    if "s5" in mix or "rwkv" in mix:
        sh["w_ab_in"] = tile_kc(inp["ab_w_in"][0], 128)
        sh["w_ab_out"] = tile_kc(inp["ab_w_out"][0], 512)
    if "s5" in mix:
        sh["w_glu"] = tile_kc(inp["s5_w_glu"][0], 512)
        gp = lambda a: np.asarray(a, np.float32).reshape(16, 2, 64, -1).transpose(1, 2, 0, 3).reshape(128, 16, -1)
        ldt = np.broadcast_to(np.asarray(inp["s5_log_dt"][0], np.float32)[:, None], (32, 64))
        sh["s5p"] = np.ascontiguousarray(np.stack([gp(inp["s5_lam_re"][0])[:, :, 0], gp(inp["s5_lam_im"][0])[:, :, 0], gp(ldt)[:, :, 0]], 1))
        sh["s5b"] = np.ascontiguousarray(np.stack([gp(inp["s5_b_re"][0]), gp(inp["s5_b_im"][0])], 1))
        cT = lambda a: np.asarray(a, np.float32).transpose(0, 2, 1)
        sh["s5c"] = np.ascontiguousarray(np.stack([gp(cT(inp["s5_c_re"][0])), gp(cT(inp["s5_c_im"][0]))], 1))
        rmk = np.zeros((128, 4), np.float32)
        for jj in range(4):
            rmk[32 * jj:32 * jj + 32, jj] = 1.0
        sh["rowmask"] = rmk
    if "hgrn" in mix:
        sh["w_c_in"] = tile_kc(inp["c_w_in"][0], 256)
        sh["w_c_out"] = tile_kc(inp["c_w_out"][0], 512)
        sh["lbrow"] = np.ascontiguousarray(np.broadcast_to(
            np.asarray(inp["hgrn_lower_bounds"], np.float32)[None], (128, 2, 1024)))
    return sh


def prep_core(inp, c, mix):
    m = {}
    sl = slice(2 * c, 2 * c + 2)
    m["xT"] = np.ascontiguousarray(np.asarray(inp["x_prompt"][sl], np.float32).transpose(0, 2, 1))
    m["xsT"] = np.ascontiguousarray(np.asarray(inp["x_sample"][sl], np.float32).transpose(0, 2, 1))
    cc = np.concatenate([inp["c_prompt"][sl], inp["c_sample"][sl]], 0).astype(np.float32)
    m["cT"] = np.ascontiguousarray(cc.reshape(4, 8, 128).transpose(2, 1, 0))
    if "rwkv" in mix:
        st = np.asarray(inp["state_rwkv"][0, sl], np.float32)
        stz = np.zeros((2, 128, 8, 64), np.float32)
        for h in range(8):
            hl = h % 2
            stz[:, 64 * hl:64 * hl + 64, h, :] = st[:, h].transpose(0, 2, 1)
        m["rw_st"] = stz
        m["rw_sh"] = np.ascontiguousarray(np.stack([fm(inp["state_rwkv_shift"][0, b]) for b in range(2 * c, 2 * c + 2)], 0))
    if "s5" in mix:
        gp = lambda a: np.asarray(a, np.float32).reshape(16, 2, 64).transpose(1, 2, 0).reshape(128, 16)
        m["s5x"] = np.ascontiguousarray(np.stack([np.stack([gp(inp["state_s5_re"][0, b]), gp(inp["state_s5_im"][0, b])], 1)
                                                  for b in range(2 * c, 2 * c + 2)], 0))
    if "hgrn" in mix:
        m["hg_in"] = np.ascontiguousarray(np.asarray(inp["state_hgrn"][0, sl], np.float32).transpose(0, 2, 1, 3))
    return m


ALL_MIX = ("s5", "rwkv", "hgrn")


def run(inp, cfg, runner=None):
    mix = cfg.get("mix", ALL_MIX)
    nc = build(cfg)
    sh = prep_shared(inp, mix)
    in_maps = []
    for c in range(NCORES):
        m = dict(sh)
        m.update(prep_core(inp, c, mix))
        in_maps.append(m)
    if runner is not None:
        return runner(nc, in_maps)
    res = run_bass_kernel_spmd(nc, in_maps, core_ids=list(range(NCORES)))
    return res.results


def assemble(results, cfg):
    mix = cfg.get("mix", ALL_MIX)
    f32 = lambda a: np.ascontiguousarray(a, np.float32)
    y_p = f32(np.concatenate([np.asarray(r["yT"]).transpose(0, 2, 1) for r in results], 0))
    y_s = f32(np.concatenate([np.asarray(r["ysT"]).transpose(0, 2, 1) for r in results], 0))
    nb = 2 * len(results)
    z = lambda *s: np.zeros(s, np.float32)
    s5_re_p, s5_im_p, s5_re_s, s5_im_s = z(1, nb, 32, 64), z(1, nb, 32, 64), z(1, nb, 32, 64), z(1, nb, 32, 64)
    rw_p, rw_s = z(1, nb, 8, 64, 64), z(1, nb, 8, 64, 64)
    sh_p, sh_s = z(1, nb, 1792), z(1, nb, 1792)
    hg_p, hg_s = z(1, nb, 8, 128, 128), z(1, nb, 8, 128, 128)
    if "s5" in mix:
        so = np.stack([np.asarray(r["s5o"]) for r in results], 0)
        so = so.reshape(len(results), 4, 2, 64, 2, 16).transpose(0, 1, 4, 5, 2, 3).reshape(len(results), 4, 2, 32, 64)
        s5_re_p, s5_im_p = f32(so[:, 0:2, 0].reshape(1, nb, 32, 64)), f32(so[:, 0:2, 1].reshape(1, nb, 32, 64))
        s5_re_s, s5_im_s = f32(so[:, 2:4, 0].reshape(1, nb, 32, 64)), f32(so[:, 2:4, 1].reshape(1, nb, 32, 64))
    if "rwkv" in mix:
        sto = np.stack([np.asarray(r["rw_sto"]) for r in results], 0)
        rw = np.zeros((len(results), 4, 8, 64, 64), np.float32)
        for h in range(8):
            hl = h % 2
            rw[:, :, h] = sto[:, :, 64 * hl:64 * hl + 64, h, :].transpose(0, 1, 3, 2)
        rw_p, rw_s = f32(rw[:, 0:2].reshape(1, nb, 8, 64, 64)), f32(rw[:, 2:4].reshape(1, nb, 8, 64, 64))
        sho = np.stack([np.asarray(r["rw_sho"]) for r in results], 0)
        sho = sho.transpose(0, 1, 3, 2).reshape(len(results), 4, 1792)
        sh_p, sh_s = f32(sho[:, 0:2].reshape(1, nb, 1792)), f32(sho[:, 2:4].reshape(1, nb, 1792))
    if "hgrn" in mix:
        hg = np.stack([np.asarray(r["hg_out"]) for r in results], 0)
        hg = hg.transpose(0, 1, 3, 2, 4)
        hg_p = f32(hg[:, 0:2].reshape(1, nb, 8, 128, 128))
        hg_s = f32(hg[:, 2:4].reshape(1, nb, 8, 128, 128))
    return (y_p, y_s, s5_re_p, s5_im_p, rw_p, sh_p, hg_p, s5_re_s, s5_im_s, rw_s, sh_s, hg_s)


def kernel(**inputs):
    cfg = {"SEQ": inputs["x_prompt"].shape[1], "DSEQ": inputs["x_sample"].shape[1]}
    results = run(inputs, cfg)
    return assemble(results, cfg)
```
